# Optimizing a Trainium2 kernel written in Bass

```python
import math
import jax, jax.numpy as jnp
from jax import lax
import numpy as np

D_MODEL = 1024
BATCH = 8
SEQ = 4096
DEPTH = 2

CTX_LEN = 256
GRID_W = 64
F32 = jnp.float32
NORM_EPS = 1e-6

N_BRANCH = 4
BRANCH_W = D_MODEL // N_BRANCH

A_HD = 64
A_HEADS = BRANCH_W // A_HD
A_W = A_HEADS * A_HD
A_DECAY_R = 32
A_AAA_R = 32
A_GATE_R = 64
A_GN_EPS = 64e-5
A_COLS = (A_W, A_W, A_W, A_DECAY_R, A_DECAY_R, A_AAA_R, A_AAA_R, A_GATE_R)
A_IN = sum(A_COLS)

B_DK = 64
B_DV = 64
B_HEADS = BRANCH_W // B_DV
B_QK = B_HEADS * B_DK
B_VW = B_HEADS * B_DV
B_CONV = 7
B_CHUNK = 64
B_COLS = (B_QK, B_QK, B_VW, B_HEADS, B_HEADS, B_HEADS, B_HEADS, B_VW)
B_IN = sum(B_COLS)

C_DV = 64
C_HEADS = BRANCH_W // C_DV
C_DK = C_DV // 2
C_QK = C_HEADS * C_DK
C_VW = C_HEADS * C_DV
C_GATE_R = 16
C_GATE_NORM = 16.0
C_CHUNK = 64
C_COLS = (C_QK, C_QK, C_VW, C_GATE_R, C_GATE_R, C_VW)
C_IN = sum(C_COLS)

D_HD = 64
D_HEADS = BRANCH_W // D_HD
D_KV_HEADS = 2
WINDOW = 128
BLOCK = 128
ROPE_BASE = 10000.0
D_COLS = (D_HEADS * D_HD, D_KV_HEADS * D_HD, D_KV_HEADS * D_HD)
D_IN = sum(D_COLS)

MIXER_COLS = (A_IN, B_IN, C_IN, D_IN, N_BRANCH * D_MODEL)
IN_WIDTH = sum(MIXER_COLS)
FFN_HIDDEN = ((8 * D_MODEL + 3 * 256 - 1) // (3 * 256)) * 256

kernel_name = 'hybrid_flow_block'


def _split(t, sizes):
    cuts, acc = [], 0
    for s in sizes[:-1]:
        acc += s
        cuts.append(acc)
    return jnp.split(t, cuts, axis=-1)


def _rms(x, g, eps=NORM_EPS):
    xf = x.astype(F32)
    y = xf * lax.rsqrt(jnp.mean(xf * xf, axis=-1, keepdims=True) + eps)
    return y * g.astype(F32)


def _l2norm(x):
    return x * lax.rsqrt(jnp.sum(x * x, axis=-1, keepdims=True) + 1e-6)


def _centred_shift(x):
    xp = jnp.pad(x, ((0, 0), (1, 1), (0, 0)))
    return 0.5 * (xp[:, :-2] + xp[:, 2:])


def _centred_dwconv(x, w):
    k = w.shape[0]
    return lax.conv_general_dilated(x, w[:, None, :].astype(x.dtype), window_strides=(1,),
                                    padding=[(k // 2, k // 2)], dimension_numbers=('NWC', 'WIO', 'NWC'),
                                    feature_group_count=x.shape[-1])


def _to_chunks(t, c):
    b, n = t.shape[:2]
    return jnp.swapaxes(t.reshape(b, n // c, c, *t.shape[2:]), 2, 3)


def _from_chunks(t):
    t = jnp.swapaxes(t, 2, 3)
    return t.reshape(t.shape[0], -1, *t.shape[3:])


def _dir_fn(d):
    return (lambda t: jnp.flip(t, axis=1)) if d == 1 else (lambda t: t)


def _rwkv7_inputs(p, mu, w0, w2, a0, a2, g2, k_k, k_a):
    b, t, _ = p.shape
    xm = p + (_centred_shift(p) - p) * mu
    r, k, v, xwf, xwb, xaf, xab, xg = _split(xm, A_COLS)
    hd = lambda z: z.reshape(b, t, A_HEADS, A_HD)
    gate = jax.nn.sigmoid(xg) @ g2
    kk = _l2norm(hd(k * k_k))
    dirs = []
    for d, (xw, xa) in enumerate(((xwf, xaf), (xwb, xab))):
        w_raw = w0[d] + jnp.tanh(xw) @ w2[d]
        decay = jnp.exp(-jnp.exp(-jax.nn.softplus(-w_raw) - 0.5))
        a = jax.nn.sigmoid(a0[d] + xa @ a2[d])
        kd = k * (1.0 + (a - 1.0) * k_a)
        dirs.append((hd(decay), hd(kd), hd(a)))
    return hd(r), hd(v), kk, gate, dirs


def _rwkv7_scan(inputs, d, s0, reverse):
    r, v, kk, _, dirs = inputs
    decay, kd, a = dirs[d]

    def step(s, inp):
        r_t, w_t, k_t, v_t, kk_t, a_t = inp
        sa = jnp.einsum('bhvk,bhk->bhv', s, kk_t)
        s = (s * w_t[:, :, None, :] - sa[..., None] * (kk_t * a_t)[:, :, None, :]
             + v_t[..., None] * k_t[:, :, None, :])
        return s, jnp.einsum('bhvk,bhk->bhv', s, r_t)

    xs = tuple(jnp.moveaxis(z, 1, 0) for z in (r, decay, kd, v, kk, a))
    s, y = lax.scan(step, s0, xs, reverse=reverse)
    return s, jnp.moveaxis(y, 0, 1)


def _rwkv7_output(y, inputs, r_k, ln_g, ln_b):
    r, v, _, gate, dirs = inputs
    b, t = y.shape[:2]
    mean = jnp.mean(y, axis=-1, keepdims=True)
    var = jnp.mean(jnp.square(y - mean), axis=-1, keepdims=True)
    yn = ((y - mean) * lax.rsqrt(var + A_GN_EPS)).reshape(b, t, A_W) * ln_g + ln_b
    bonus = sum(jnp.sum(r * kd * r_k, axis=-1, keepdims=True) * v for (_, kd, _) in dirs)
    return (yn + bonus.reshape(b, t, A_W)) * gate


def _rwkv7_mixer(pa, ca, mu, w0, w2, a0, a2, g2, k_k, k_a, r_k, ln_g, ln_b, with_ctx):
    lat = _rwkv7_inputs(pa, mu, w0, w2, a0, a2, g2, k_k, k_a)
    ctx = _rwkv7_inputs(ca, mu, w0, w2, a0, a2, g2, k_k, k_a)
    y_lat, y_ctx = 0.0, 0.0
    for d in range(2):
        s0 = jnp.zeros((pa.shape[0], A_HEADS, A_HD, A_HD), F32)
        s_c, yc = _rwkv7_scan(ctx, d, s0, d == 1)
        _, yl = _rwkv7_scan(lat, d, s_c, d == 1)
        y_lat, y_ctx = y_lat + yl, y_ctx + yc
    out = _rwkv7_output(y_lat, lat, r_k, ln_g, ln_b)
    out_c = _rwkv7_output(y_ctx, ctx, r_k, ln_g, ln_b) if with_ctx else None
    return out, out_c


def _gdn_inputs(p, conv_w, a_log, dt_bias):
    b, t, _ = p.shape
    q, k, v, bf, bb, af, ab, g = _split(p, B_COLS)
    qkv = jax.nn.silu(_centred_dwconv(jnp.concatenate([q, k, v], axis=-1), conv_w))
    q, k, v = _split(qkv, (B_QK, B_QK, B_VW))
    q = _l2norm(q.reshape(b, t, B_HEADS, B_DK)) * (B_DK ** -0.5)
    k = _l2norm(k.reshape(b, t, B_HEADS, B_DK))
    v = v.reshape(b, t, B_HEADS, B_DV)
    dirs = []
    for d, (bx, ax) in enumerate(((bf, af), (bb, ab))):
        beta = jax.nn.sigmoid(bx)
        log_decay = -jnp.exp(a_log[d]) * jax.nn.softplus(ax + dt_bias[d])
        dirs.append((beta, log_decay))
    return q, k, v, g, dirs


def _gdn_chunked(q, k, v, beta, g, s0):
    c = B_CHUNK
    q, k, v = _to_chunks(q, c), _to_chunks(k, c), _to_chunks(v, c)
    beta, g = _to_chunks(beta, c), _to_chunks(g, c)
    gc = jnp.cumsum(g, axis=-1)
    causal = jnp.tril(jnp.ones((c, c), bool))
    strict = jnp.tril(jnp.ones((c, c), bool), -1)
    decay = jnp.exp(jnp.where(causal, gc[..., :, None] - gc[..., None, :], -jnp.inf))
    lmat = jnp.where(strict, beta[..., :, None] * jnp.einsum('bnhid,bnhjd->bnhij', k, k) * decay, 0.0)
    m = lmat + jnp.eye(c, dtype=F32)
    u = lax.linalg.triangular_solve(m, v * beta[..., None], left_side=True, lower=True, unit_diagonal=True)
    w = lax.linalg.triangular_solve(m, k * (beta * jnp.exp(gc))[..., None], left_side=True, lower=True,
                                    unit_diagonal=True)
    attn = jnp.einsum('bnhid,bnhjd->bnhij', q, k) * decay
    q_in = q * jnp.exp(gc)[..., None]
    k_st = k * jnp.exp(gc[..., -1:] - gc)[..., None]
    dec_last = jnp.exp(gc[..., -1])

    def step(s, inp):
        qi, ki, ui, wi, ai, di = inp
        v_new = ui - jnp.einsum('bhcd,bhde->bhce', wi, s)
        o = jnp.einsum('bhcd,bhde->bhce', qi, s) + jnp.einsum('bhij,bhje->bhie', ai, v_new)
        s = s * di[..., None, None] + jnp.einsum('bhcd,bhce->bhde', ki, v_new)
        return s, o

    xs = tuple(jnp.moveaxis(z, 1, 0) for z in (q_in, k_st, u, w, attn, dec_last))
    s, o = lax.scan(step, s0, xs)
    return s, _from_chunks(jnp.moveaxis(o, 0, 1))


def _gdn_mixer(pb, cb, conv_w, a_log, dt_bias, norm_g, with_ctx):
    lat = _gdn_inputs(pb, conv_w, a_log, dt_bias)
    ctx = _gdn_inputs(cb, conv_w, a_log, dt_bias)
    y_lat, y_ctx = 0.0, 0.0
    for d in range(2):
        f = _dir_fn(d)
        s0 = jnp.zeros((pb.shape[0], B_HEADS, B_DK, B_DV), F32)
        s_c, oc = _gdn_chunked(f(ctx[0]), f(ctx[1]), f(ctx[2]), f(ctx[4][d][0]), f(ctx[4][d][1]), s0)
        _, ol = _gdn_chunked(f(lat[0]), f(lat[1]), f(lat[2]), f(lat[4][d][0]), f(lat[4][d][1]), s_c)
        y_lat, y_ctx = y_lat + f(ol), y_ctx + f(oc)

    def finish(y, gate):
        b, t = y.shape[:2]
        return (_rms(y, norm_g) * jax.nn.silu(gate.reshape(b, t, B_HEADS, B_DV))).reshape(b, t, B_VW)

    return finish(y_lat, lat[3]), (finish(y_ctx, ctx[3]) if with_ctx else None)


def _gla_inputs(p, gw2, gb):
    b, t, _ = p.shape
    q, k, v, gf, gbk, g = _split(p, C_COLS)
    q = q.reshape(b, t, C_HEADS, C_DK) * (C_DK ** -0.5)
    k = k.reshape(b, t, C_HEADS, C_DK)
    v = v.reshape(b, t, C_HEADS, C_DV)
    logg = [(jax.nn.log_sigmoid(xg @ gw2[d] + gb[d]) / C_GATE_NORM).reshape(b, t, C_HEADS, C_DK)
            for d, xg in enumerate((gf, gbk))]
    return q, k, v, g, logg


def _gla_chunked(q, k, v, lg, s0):
    c = C_CHUNK
    q, k, v, lg = (_to_chunks(z, c) for z in (q, k, v, lg))
    bcum = jnp.cumsum(lg, axis=3)
    ref = bcum[:, :, :, c // 2:c // 2 + 1]
    causal = jnp.tril(jnp.ones((c, c), bool))
    a = jnp.einsum('bnhid,bnhjd->bnhij', q * jnp.exp(bcum - ref), k * jnp.exp(ref - bcum))
    o_intra = jnp.einsum('bnhij,bnhje->bnhie', jnp.where(causal, a, 0.0), v)
    b_last = bcum[:, :, :, -1:]
    q_in = q * jnp.exp(bcum)
    k_st = k * jnp.exp(b_last - bcum)
    dec_last = jnp.exp(b_last[:, :, :, 0])

    def step(s, inp):
        qi, ki, vi, di = inp
        o = jnp.einsum('bhcd,bhde->bhce', qi, s)
        s = s * di[..., None] + jnp.einsum('bhcd,bhce->bhde', ki, vi)
        return s, o

    xs = tuple(jnp.moveaxis(z, 1, 0) for z in (q_in, k_st, v, dec_last))
    s, o_inter = lax.scan(step, s0, xs)
    return s, _from_chunks(o_intra + jnp.moveaxis(o_inter, 0, 1))


def _gla_mixer(pc, cc, gw2, gb, norm_g, with_ctx):
    lat = _gla_inputs(pc, gw2, gb)
    ctx = _gla_inputs(cc, gw2, gb)
    y_lat, y_ctx = 0.0, 0.0
    for d in range(2):
        f = _dir_fn(d)
        s0 = jnp.zeros((pc.shape[0], C_HEADS, C_DK, C_DV), F32)
        s_c, oc = _gla_chunked(f(ctx[0]), f(ctx[1]), f(ctx[2]), f(ctx[4][d]), s0)
        _, ol = _gla_chunked(f(lat[0]), f(lat[1]), f(lat[2]), f(lat[4][d]), s_c)
        y_lat, y_ctx = y_lat + f(ol), y_ctx + f(oc)

    def finish(y, gate):
        b, t = y.shape[:2]
        return (_rms(y, norm_g) * jax.nn.silu(gate.reshape(b, t, C_HEADS, C_DV))).reshape(b, t, C_VW)

    return finish(y_lat, lat[3]), (finish(y_ctx, ctx[3]) if with_ctx else None)


def _axial_rope(x, rows, cols):
    half = x.shape[-1] // 2
    quarter = half // 2
    inv = ROPE_BASE ** (-jnp.arange(quarter, dtype=F32) / quarter)

    def rot(xa, pos):
        ang = pos[:, None] * inv[None, :]
        cos, sin = jnp.cos(ang)[None, :, None, :], jnp.sin(ang)[None, :, None, :]
        x1, x2 = xa[..., :quarter], xa[..., quarter:]
        return jnp.concatenate([x1 * cos - x2 * sin, x1 * sin + x2 * cos], axis=-1)

    return jnp.concatenate([rot(x[..., :half], rows), rot(x[..., half:], cols)], axis=-1)


def _window_attention(q, k, v, kc, vc, sink):
    b, s, h, hd = q.shape
    nb, grp, lc = s // BLOCK, h // D_KV_HEADS, kc.shape[1]
    scale = hd ** -0.5
    qb = q.reshape(b, nb, BLOCK, D_KV_HEADS, grp, hd)

    def band(t):
        tp = jnp.pad(t, ((0, 0), (BLOCK, BLOCK), (0, 0), (0, 0)))
        return jnp.concatenate([tp[:, o:o + s].reshape(b, nb, BLOCK, D_KV_HEADS, hd)
                                for o in (0, BLOCK, 2 * BLOCK)], axis=2)

    kb, vb = band(k), band(v)
    qpos = jnp.arange(s).reshape(nb, BLOCK)
    kpos = (jnp.arange(nb) * BLOCK - BLOCK)[:, None] + jnp.arange(3 * BLOCK)[None, :]
    valid = ((jnp.abs(qpos[:, :, None] - kpos[:, None, :]) <= WINDOW)
             & (kpos >= 0)[:, None, :] & (kpos < s)[:, None, :])
    s_loc = jnp.einsum('bnqhgd,bnkhd->bnhgqk', qb, kb) * scale
    s_loc = jnp.where(valid[None, :, None, None], s_loc, -jnp.inf)
    s_ctx = jnp.einsum('bnqhgd,bkhd->bnhgqk', qb, kc) * scale
    sink_col = jnp.broadcast_to(sink.reshape(D_KV_HEADS, grp)[:, :, None, None], s_ctx.shape[:-1] + (1,))
    prob = jax.nn.softmax(jnp.concatenate([s_ctx, s_loc, sink_col], axis=-1), axis=-1)
    o = (jnp.einsum('bnhgqk,bkhd->bnqhgd', prob[..., :lc], vc)
         + jnp.einsum('bnhgqk,bnkhd->bnqhgd', prob[..., lc:lc + 3 * BLOCK], vb))
    return o.reshape(b, s, h * hd)


def _context_attention(qc, kc, vc, sink):
    b, lc, h, hd = qc.shape
    grp = h // D_KV_HEADS
    qg = qc.reshape(b, lc, D_KV_HEADS, grp, hd)
    sc = jnp.einsum('bqhgd,bkhd->bhgqk', qg, kc) * (hd ** -0.5)
    sink_col = jnp.broadcast_to(sink.reshape(D_KV_HEADS, grp)[:, :, None, None], sc.shape[:-1] + (1,))
    prob = jax.nn.softmax(jnp.concatenate([sc, sink_col], axis=-1), axis=-1)
    return jnp.einsum('bhgqk,bkhd->bqhgd', prob[..., :lc], vc).reshape(b, lc, h * hd)


def _attn_mixer(pd, cd, sink, rows, cols, with_ctx):
    def heads(p):
        b, t, _ = p.shape
        q, k, v = _split(p, D_COLS)
        return (q.reshape(b, t, D_HEADS, D_HD), k.reshape(b, t, D_KV_HEADS, D_HD),
                v.reshape(b, t, D_KV_HEADS, D_HD))

    q, k, v = heads(pd)
    qc, kc, vc = heads(cd)
    q, k = _axial_rope(q, rows, cols), _axial_rope(k, rows, cols)
    sink = sink.astype(F32)
    y = _window_attention(q, k, v, kc, vc, sink)
    yc = _context_attention(qc, kc, vc, sink) if with_ctx else None
    return y, yc


def _merge(ys, gate_pre, gate_b, w_branch, w_out):
    acc = 0.0
    for i, y in enumerate(ys):
        g = jax.nn.sigmoid(gate_pre[..., i * D_MODEL:(i + 1) * D_MODEL] + gate_b[i])
        acc = acc + g * (y @ w_branch[i])
    return acc @ w_out


def _mixer_block(h, hc, w_in, gate_b, w_branch, w_out,
                 rwkv_mu, rwkv_w0, rwkv_w2, rwkv_a0, rwkv_a2, rwkv_g2, rwkv_kk, rwkv_ka, rwkv_rk,
                 rwkv_ln_g, rwkv_ln_b, gdn_conv, gdn_a_log, gdn_dt_bias, gdn_norm_g,
                 gla_gw2, gla_gb, gla_norm_g, attn_sink, rows, cols, with_ctx):
    p = (h @ w_in).astype(F32)
    pc = (hc @ w_in).astype(F32)
    pa, pb, pg, pd, gl = _split(p, MIXER_COLS)
    ca, cb, cg, cd, gcx = _split(pc, MIXER_COLS)
    ya, yac = _rwkv7_mixer(pa, ca, rwkv_mu, rwkv_w0, rwkv_w2, rwkv_a0, rwkv_a2, rwkv_g2, rwkv_kk, rwkv_ka,
                           rwkv_rk, rwkv_ln_g, rwkv_ln_b, with_ctx)
    yb, ybc = _gdn_mixer(pb, cb, gdn_conv, gdn_a_log, gdn_dt_bias, gdn_norm_g, with_ctx)
    yg, ygc = _gla_mixer(pg, cg, gla_gw2, gla_gb, gla_norm_g, with_ctx)
    yd, ydc = _attn_mixer(pd, cd, attn_sink, rows, cols, with_ctx)
    out = _merge((ya, yb, yg, yd), gl, gate_b, w_branch, w_out)
    out_c = _merge((yac, ybc, ygc, ydc), gcx, gate_b, w_branch, w_out) if with_ctx else None
    return out, out_c


def _swiglu(h, w1, w2):
    gt, up = jnp.split(h @ w1, 2, axis=-1)
    return (jax.nn.silu(gt) * up) @ w2


def setup_inputs(seed: int = 0) -> dict:
    key = jax.random.key(seed)
    ks = iter(jax.random.split(key, 48))
    nrm = lambda shape, s: jax.random.normal(next(ks), shape, F32) * s
    L = DEPTH
    dt = jnp.exp(jax.random.uniform(next(ks), (L, 2, B_HEADS), F32, math.log(1e-3), math.log(1e-1)))
    return {
        'x': nrm((BATCH, SEQ, D_MODEL), 1.0),
        'c': nrm((BATCH, D_MODEL), 1.0),
        'ctx': nrm((BATCH, CTX_LEN, D_MODEL), 1.0),
        'c_ctx': nrm((D_MODEL,), 1.0),
        'ada_w': nrm((L, D_MODEL, 6 * D_MODEL), 0.5 * D_MODEL ** -0.5),
        'ada_b': nrm((L, 6 * D_MODEL), 0.01),
        'norm_g': 1.0 + nrm((L, 4, D_MODEL), 0.02),
        'w_in': nrm((L, D_MODEL, IN_WIDTH), D_MODEL ** -0.5),
        'gate_b': nrm((L, N_BRANCH, D_MODEL), 0.01),
        'w_branch': nrm((L, N_BRANCH, BRANCH_W, D_MODEL), BRANCH_W ** -0.5),
        'w_out': nrm((L, D_MODEL, D_MODEL), D_MODEL ** -0.5),
        'rwkv_mu': jax.random.uniform(next(ks), (L, A_IN), F32),
        'rwkv_w0': nrm((L, 2, A_W), 0.5),
        'rwkv_w2': nrm((L, 2, A_DECAY_R, A_W), A_DECAY_R ** -0.5),
        'rwkv_a0': nrm((L, 2, A_W), 0.5),
        'rwkv_a2': nrm((L, 2, A_AAA_R, A_W), A_AAA_R ** -0.5),
        'rwkv_g2': nrm((L, A_GATE_R, A_W), A_GATE_R ** -0.5),
        'rwkv_kk': 0.85 + nrm((L, A_W), 0.05),
        'rwkv_ka': 1.0 + nrm((L, A_W), 0.05),
        'rwkv_rk': nrm((L, A_HEADS, A_HD), 0.1),
        'rwkv_ln_g': 1.0 + nrm((L, A_W), 0.02),
        'rwkv_ln_b': nrm((L, A_W), 0.01),
        'gdn_conv': nrm((L, B_CONV, 2 * B_QK + B_VW), B_CONV ** -0.5),
        'gdn_a_log': jnp.log(jax.random.uniform(next(ks), (L, 2, B_HEADS), F32, 1.0, 16.0)),
        'gdn_dt_bias': dt + jnp.log(-jnp.expm1(-dt)),
        'gdn_norm_g': 1.0 + nrm((L, B_DV), 0.02),
        'gla_gw2': nrm((L, 2, C_GATE_R, C_QK), C_GATE_R ** -0.5),
        'gla_gb': nrm((L, 2, C_QK), 0.1),
        'gla_norm_g': 1.0 + nrm((L, C_DV), 0.02),
        'attn_sink': nrm((L, D_HEADS), 0.5),
        'ffn_w1': nrm((L, D_MODEL, 2 * FFN_HIDDEN), D_MODEL ** -0.5),
        'ffn_w2': nrm((L, FFN_HIDDEN, D_MODEL), FFN_HIDDEN ** -0.5),
    }


def reference(x, c, ctx, c_ctx, ada_w, ada_b, norm_g, w_in, gate_b, w_branch, w_out,
              rwkv_mu, rwkv_w0, rwkv_w2, rwkv_a0, rwkv_a2, rwkv_g2, rwkv_kk, rwkv_ka, rwkv_rk,
              rwkv_ln_g, rwkv_ln_b, gdn_conv, gdn_a_log, gdn_dt_bias, gdn_norm_g,
              gla_gw2, gla_gb, gla_norm_g, attn_sink, ffn_w1, ffn_w2):
    n_rows = x.shape[1] // GRID_W
    rows = jnp.repeat(jnp.arange(n_rows, dtype=F32), GRID_W)
    cols = jnp.tile(jnp.arange(GRID_W, dtype=F32), n_rows)
    xc = ctx
    for l in range(DEPTH):
        last = l == DEPTH - 1
        mod = (jax.nn.silu(c) @ ada_w[l] + ada_b[l])[:, None, :]
        mod_c = (jax.nn.silu(c_ctx) @ ada_w[l] + ada_b[l])[None, None, :]
        sh1, sc1, g1, sh2, sc2, g2 = jnp.split(mod, 6, axis=-1)
        csh1, csc1, cg1, csh2, csc2, cg2 = jnp.split(mod_c, 6, axis=-1)
        ng = norm_g[l]
        h = (_rms(x, ng[0]) * (1.0 + sc1) + sh1).astype(x.dtype)
        hc = (_rms(xc, ng[0]) * (1.0 + csc1) + csh1).astype(xc.dtype)
        y, yc = _mixer_block(h, hc, w_in[l], gate_b[l], w_branch[l], w_out[l],
                             rwkv_mu[l], rwkv_w0[l], rwkv_w2[l], rwkv_a0[l], rwkv_a2[l], rwkv_g2[l],
                             rwkv_kk[l], rwkv_ka[l], rwkv_rk[l], rwkv_ln_g[l], rwkv_ln_b[l],
                             gdn_conv[l], gdn_a_log[l], gdn_dt_bias[l], gdn_norm_g[l],
                             gla_gw2[l], gla_gb[l], gla_norm_g[l], attn_sink[l], rows, cols, not last)
        x = x + (g1 * _rms(y, ng[1])).astype(x.dtype)
        h = (_rms(x, ng[2]) * (1.0 + sc2) + sh2).astype(x.dtype)
        x = x + (g2 * _rms(_swiglu(h, ffn_w1[l], ffn_w2[l]), ng[3])).astype(x.dtype)
        if not last:
            xc = xc + (cg1 * _rms(yc, ng[1])).astype(xc.dtype)
            hc = (_rms(xc, ng[2]) * (1.0 + csc2) + csh2).astype(xc.dtype)
            xc = xc + (cg2 * _rms(_swiglu(hc, ffn_w1[l], ffn_w2[l]), ng[3])).astype(xc.dtype)
    return x
```

```python
import os
from contextlib import ExitStack
import numpy as np
import concourse.bass as bass
import concourse.mybir as mybir
from concourse.bass_utils import run_bass_kernel_spmd

F32 = mybir.dt.float32
BF16 = mybir.dt.bfloat16
ALU = mybir.AluOpType
AF = mybir.ActivationFunctionType
AX = mybir.AxisListType

SEM_CAP = 30000
NDMA_SLOTS = 6
L = 2
D = 1024
T = 4352
NT = T // 128
NG = T // 256
FH = 2816


class _Rec:
    def __init__(self):
        self.call = None

    def __getattr__(self, name):
        def f(*a, **kw):
            self.call = (name, a, kw)
            return self
        return f


class Prog:
    ENGS = ("tensor", "vector", "scalar", "gpsimd", "sync")

    def __init__(self, nc):
        self.nc = nc
        self.ops = {e: [] for e in self.ENGS}
        self.cnt = {e: 0 for e in self.ENGS}
        self.seen = {e: {} for e in self.ENGS}
        self.res = {}
        self.pe_rt = {}
        self.sems = {}
        self.dma_n = {e: 0 for e in self.ENGS}
        self._stack = []
        self.n_inst = 0

    def sem(self, key):
        if key not in self.sems:
            cm = self.nc.semaphore("s_%s" % "_".join(str(k) for k in key))
            self.sems[key] = cm.__enter__()
            self._stack.append(cm)
        return self.sems[key]

    def _key(self, r):
        if isinstance(r, (tuple, str)):
            return r
        t = getattr(r, "tensor", r)
        return getattr(t, "name", None) or id(t)

    def _deps(self, reads, writes):
        need = {}
        for r in reads:
            w, _ = self.res.get(self._key(r), ({}, {}))
            for s, v in w.items():
                need[s] = max(need.get(s, 0), v)
        for r in writes:
            w, rd = self.res.get(self._key(r), ({}, {}))
            for d in (w, rd):
                for s, v in d.items():
                    need[s] = max(need.get(s, 0), v)
        return need

    def _commit(self, reads, writes, tok):
        s, v = tok
        for r in reads:
            w, rd = self.res.setdefault(self._key(r), ({}, {}))
            rd[s] = max(rd.get(s, 0), v)
        for r in writes:
            self.res[self._key(r)] = ({s: v}, {})

    def _waits(self, eng, need, skip_pe=False):
        out = []
        seen = self.seen[eng]
        for s, v in need.items():
            if skip_pe and s[0] == "tensor":
                continue
            if seen.get(s, 0) >= v:
                continue
            seen[s] = v
            out.append((s, v))
        return out

    def op(self, eng, fn, reads=(), writes=(), acc=False, rowtile=None):
        need = self._deps(reads, writes)
        if eng == "tensor":
            wk = [self._key(w) for w in writes]
            if rowtile is not None and wk and all(self.pe_rt.get(k) == rowtile for k in wk):
                acc = True
            for k in wk:
                self.pe_rt[k] = rowtile
        waits = self._waits(eng, need, skip_pe=(eng == "tensor" and acc))
        i = self.cnt[eng]
        self.cnt[eng] += 1
        key = (eng, i // SEM_CAP)
        self.sem(key)
        tok = (key, i % SEM_CAP + 1)
        rec = _Rec()
        fn(rec)
        self.ops[eng].append((waits, rec.call, (key, 1)))
        self._commit(reads, writes, tok)
        self.n_inst += 1
        return tok

    def dma(self, eng, out, in_, reads=None, writes=None, **kw):
        reads = [in_] if reads is None else reads
        writes = [out] if writes is None else writes
        need = self._deps(reads, writes)
        j = self.dma_n[eng]
        self.dma_n[eng] += 1
        key = ("dma", eng, j % NDMA_SLOTS)
        self.sem(key)
        val = 16 * (j // NDMA_SLOTS + 1)
        if val > 16:
            need[key] = max(need.get(key, 0), val - 16)
        waits = self._waits(eng, need)
        self.ops[eng].append((waits, ("dma_start", (), dict(out=out, in_=in_, **kw)), (key, 16)))
        self._commit(reads, writes, (key, val))
        self.n_inst += 1

    def _all_tokens(self):
        need = {}
        for e in self.ENGS:
            n = self.dma_n[e]
            for slot in range(min(n, NDMA_SLOTS)):
                need[("dma", e, slot)] = 16 * (((n - 1 - slot) // NDMA_SLOTS) + 1)
            if self.cnt[e]:
                i = self.cnt[e] - 1
                need[(e, i // SEM_CAP)] = i % SEM_CAP + 1
        return need

    def barrier(self):
        need = self._all_tokens()
        for e in self.ENGS:
            waits = self._waits(e, dict(need))
            if waits:
                self.ops[e].append((waits, None, None))
        self.res = {}

    def emit(self):
        nc = self.nc
        with nc.Block() as block:
            def mk(ename):
                lst = self.ops[ename]

                def body(e):
                    for waits, fn, inc in lst:
                        for s, v in waits:
                            e.wait_ge(self.sems[s], v)
                        if fn is not None:
                            getattr(e, fn[0])(*fn[1], **fn[2]).then_inc(self.sems[inc[0]], inc[1])
                return body
            for ename in self.ENGS:
                if self.ops[ename]:
                    getattr(block, ename)(mk(ename))
        self.ops = {e: [] for e in self.ENGS}

    def close(self):
        for cm in reversed(self._stack):
            cm.__exit__(None, None, None)


class Pool:
    def __init__(self, nc):
        self.nc = nc
        self.es = ExitStack()

    def __enter__(self):
        self.es.__enter__()
        return self

    def __exit__(self, *a):
        return self.es.__exit__(*a)

    _n = [0]

    def sb(self, name, shape, dt):
        Pool._n[0] += 1
        return self.es.enter_context(self.nc.sbuf_tensor("%s_%d" % (name, Pool._n[0]), shape, dt))

    def ps(self, name, shape, dt):
        Pool._n[0] += 1
        return self.es.enter_context(self.nc.psum_tensor("%s_%d" % (name, Pool._n[0]), shape, dt))


class Ctx:
    pass


class AV:
    def __init__(self, ap, name):
        self.ap, self.name = ap, name

    def __getitem__(self, key):
        return self.ap[key]


class H:
    def __init__(self, t, pb):
        self.t, self.pb = t, pb
        self.name = "%s@%d" % (t.name, pb)

    def __getitem__(self, key):
        if not isinstance(key, tuple):
            key = (key,)
        k0 = key[0]
        a = 0 if k0.start is None else k0.start
        b = 64 if k0.stop is None else k0.stop
        return self.t[(slice(self.pb + a, self.pb + b),) + tuple(key[1:])]


def V(P, fn, reads, writes):
    return P.op("vector", fn, reads, writes)


def S(P, fn, reads, writes):
    return P.op("scalar", fn, reads, writes)


def MM(P, out, lhsT, rhs, start=True, stop=True, reads=None, writes=None):
    rd = [lhsT, rhs] if reads is None else reads
    wr = [out] if writes is None else writes
    return P.op("tensor", lambda e: e.matmul(out, lhsT=lhsT, rhs=rhs, start=start, stop=stop),
                rd, wr, acc=(not start) and lhsT.shape[0] == 128, rowtile=(lhsT.base_partition(), lhsT.shape[0]))


def TR(P, out, in_, ident, reads=None, writes=None):
    rd = [in_, ident] if reads is None else reads
    wr = [out] if writes is None else writes
    return P.op("tensor", lambda e: e.transpose(out=out, in_=in_, identity=ident), rd, wr)


def phase0_mod(g, l):
    nc, P = g.nc, g.P
    with Pool(nc) as pool:
        w0 = pool.sb("adaw0", [128, 8, 512], F32)
        w1 = pool.sb("adaw1", [128, 8, 512], F32)
        modrow = pool.sb("modrow", [2, 6144], F32)
        adab = pool.sb("adab", [2, 6144], F32)
        ps = pool.ps("modps", [2, 512], F32)
        wb = [w0, w1]
        P.dma("sync", adab[:], g.ada_b[l:l + 1, :].broadcast_to([2, 6144]))
        for j in range(12):
            w = wb[j % 2]
            P.dma("sync" if j % 2 == 0 else "gpsimd", w[:],
                  g.ada_w[l, :, j * 512:(j + 1) * 512].rearrange("(kc p) n -> p kc n", p=128))
            for kc in range(8):
                MM(P, ps[:], g.sc[:, kc, :], w[:, kc, :], start=(kc == 0), stop=(kc == 7))
            V(P, lambda e, j=j: e.tensor_tensor(out=modrow[:, j * 512:(j + 1) * 512], in0=ps[:],
                                                in1=adab[:, j * 512:(j + 1) * 512], op=ALU.add),
              [ps, adab], [modrow])
        P.dma("sync", g.modD[l], modrow[:])
        with Pool(nc) as pool:
            tp = pool.ps("modtp", [128, 48, 2], F32)
            for col in range(48):
                TR(P, tp[:, col, :], modrow[:, col * 128:(col + 1) * 128], g.ident[0:2, 0:2])
            V(P, lambda e: e.tensor_copy(out=g.modT[:], in_=tp[:]), [tp], [g.modT])
        mT = g.modT
        ng = g.ngT
        for j in range(2):
            V(P, lambda e, j=j: e.scalar_tensor_tensor(out=g.gain1[:, :, j], in0=mT[:, 8:16, j], scalar=1.0,
                                                       in1=ng[:, l, 0, :], op0=ALU.add, op1=ALU.mult),
              [mT, ng], [g.gain1])
            V(P, lambda e, j=j: e.scalar_tensor_tensor(out=g.gain2[:, :, j], in0=mT[:, 32:40, j], scalar=1.0,
                                                       in1=ng[:, l, 2, :], op0=ALU.add, op1=ALU.mult),
              [mT, ng], [g.gain2])
    P.barrier()


def load_gbc(g, l, slot, gi):
    nc, P = g.nc, g.P
    with Pool(nc) as pool:
        tmpbc = pool.sb("tmpbc", [128, 1024], F32)
        P.dma("sync", tmpbc[:], g.norm_g[l, gi:gi + 1, :].broadcast_to([128, 1024]))
        for j in range(2):
            P.dma("sync", g.Gbc[:, j, :], g.modD[l][j:j + 1, slot * 1024:(slot + 1) * 1024].broadcast_to([128, 1024]),
                  reads=[g.modD[l]], writes=[g.Gbc])
            V(P, lambda e, j=j: e.tensor_tensor(out=g.Gbc[:, j, :], in0=g.Gbc[:, j, :], in1=tmpbc[:], op=ALU.mult),
              [g.Gbc, tmpbc], [g.Gbc])
        P.barrier()


def norm_transpose(g, xt, j, gain, shift, dst, col0, tag):
    nc, P = g.nc, g.P
    ss, rstd, xn = g.ss, g.rstd, g.xn
    sq = xn
    V(P, lambda e: e.memset(ss[:], 0.0), [], [ss])
    S(P, lambda e: e.activation(out=sq[:], in_=xt[:], func=AF.Square, accum_out=ss[:]), [xt, ss], [sq, ss])
    S(P, lambda e: e.activation(out=rstd[:], in_=ss[:], func=AF.Sqrt, bias=1e-6, scale=1.0 / 1024), [ss], [rstd])
    V(P, lambda e: e.reciprocal(out=rstd[:], in_=rstd[:]), [rstd], [rstd])
    V(P, lambda e: e.tensor_scalar(out=xn[:], in0=xt[:], scalar1=rstd[:, 0:1], scalar2=None, op0=ALU.mult),
      [xt, rstd], [xn])
    for half in range(2):
        tp = g.tp[half]
        for q in range(4):
            kc = half * 4 + q
            TR(P, tp[:, q, :], xn[:, kc * 128:(kc + 1) * 128], g.ident[:])
        for q in range(4):
            kc = half * 4 + q
            if q % 2 == 0:
                V(P, lambda e, kc=kc, q=q, tp=tp: e.tensor_scalar(
                    out=dst[:, kc, col0:col0 + 128], in0=tp[:, q, :], scalar1=gain[:, kc, j:j + 1],
                    scalar2=shift[:, kc, j:j + 1], op0=ALU.mult, op1=ALU.add), [tp, gain, shift], [dst])
            else:
                S(P, lambda e, kc=kc, q=q, tp=tp: e.activation(
                    out=dst[:, kc, col0:col0 + 128], in_=tp[:, q, :], func=AF.Identity,
                    bias=shift[:, kc, j:j + 1], scale=gain[:, kc, j:j + 1]), [tp, gain, shift], [dst])


def phase1_h(g, l, xsrc):
    nc, P = g.nc, g.P
    with Pool(nc) as pool:
        g.tp = [pool.ps("tpa", [128, 4, 128], F32), pool.ps("tpb", [128, 4, 128], F32)]
        x0 = pool.sb("p1x0", [128, 1024], F32)
        x1 = pool.sb("p1x1", [128, 1024], F32)
        xb = [x0, x1]
        for i in range(NT):
            xt = xb[i % 2]
            P.dma("sync", xt[:], xsrc[i * 128:(i + 1) * 128, :], reads=[("x", l)], writes=[xt])
            j = 1 if i < 2 else 0
            norm_transpose(g, xt, j, g.gain1, g.modT[:, 0:8, :], g.hT, i * 128, "p1")
    P.barrier()


def mixer_D(g, l, with_ctx):
    nc, P = g.nc, g.P
    wm = g.wmix
    with Pool(nc) as pool:
        Q = pool.sb("dq", [128, 2, T], BF16)
        K = pool.sb("dk", [128, T], BF16)
        Vt = pool.sb("dv", [128, NT, 2, 72], BF16)
        cosb = pool.sb("dcos", [128, 256], F32)
        sinb = pool.sb("dsin", [128, 256], F32)
        t1 = pool.sb("dt1", [128, 256], F32)
        t2 = pool.sb("dt2", [128, 256], F32)
        E0 = pool.sb("dE", [128, 2, 128], BF16)
        E1 = pool.sb("dE1", [128, 2, 128], BF16)
        ytm = pool.sb("dy", [128, 256], F32)
        yTs = pool.sb("dyT", [128, 2, 128], BF16)
        den = pool.sb("dden", [128, 4], F32)
        es = pool.sb("des", [128, 4], F32)
        pa = pool.ps("dpa", [128, 256], F32)
        pb = pool.ps("dpb", [128, 256], F32)
        psA = pool.ps("dps", [128, 2, 128], F32)
        psB = pool.ps("dps2", [128, 2, 128], F32)
        poA = pool.ps("dpoA", [128, 2, 128], F32)
        poB = pool.ps("dpoB", [128, 2, 128], F32)
        pt = pool.ps("d_pt", [128, 4, 128], F32)
        pos = [poA, poB]
        P.dma("gpsimd", wm[:, :, 0:896], g.w_D[l].rearrange("(kc p) n -> p kc n", p=128))
        P.dma("sync", es[:], g.attn_sink[l:l + 1, :].broadcast_to([128, 4]))
        S(P, lambda e: e.activation(out=es[:], in_=es[:], func=AF.Exp), [es], [es])
        V(P, lambda e: e.memset(Vt[:], 1.0), [], [Vt])
        for grp in range(NG):
            ts = slice(grp * 256, (grp + 1) * 256)
            P.dma("sync", cosb[:], g.ropec[:, ts])
            P.dma("gpsimd", sinb[:], g.ropes[:, ts])
            for which in range(3):
                for kc in range(8):
                    MM(P, pa[:], wm[:, kc, which * 128:(which + 1) * 128], g.hT[:, kc, ts], start=(kc == 0), stop=(kc == 7),
                       reads=[wm, g.hT])
                for kc in range(8):
                    MM(P, pb[:], wm[:, kc, 384 + which * 128:384 + (which + 1) * 128], g.hT[:, kc, ts], start=(kc == 0),
                       stop=(kc == 7), reads=[wm, g.hT])
                dst = Q[:, which, ts] if which < 2 else K[:, ts]
                V(P, lambda e: e.tensor_tensor(out=t1[:], in0=pa[:], in1=cosb[:], op=ALU.mult), [pa, cosb], [t1])
                V(P, lambda e: e.tensor_tensor(out=t2[:], in0=pb[:], in1=sinb[:], op=ALU.mult), [pb, sinb], [t2])
                V(P, lambda e, dst=dst: e.tensor_tensor(out=dst, in0=t1[:], in1=t2[:], op=ALU.add), [t1, t2],
                  [Q if which < 2 else K])
            for tt in range(2):
                ti = grp * 2 + tt
                for kc in range(8):
                    MM(P, pa[:, 0:128], g.hT[:, kc, ti * 128:(ti + 1) * 128], wm[:, kc, 768:896], start=(kc == 0),
                       stop=(kc == 7), reads=[wm, g.hT], writes=[pa])
                S(P, lambda e, ti=ti: e.activation(out=Vt[:, ti, :, 0:64],
                                                   in_=pa[:, 0:128].rearrange("p (a b) -> p a b", a=2), func=AF.Copy),
                  [pa], [Vt])
        Eb = [E0, E1]
        ne = 0
        for qi in range(NT):
            if qi < 2:
                if not with_ctx:
                    continue
                keys = [(0, None), (1, None)]
            else:
                n = qi - 2
                keys = [(0, None), (1, None)]
                if n > 0:
                    keys.append((qi - 1, g.maskP))
                keys.append((qi, None))
                if n < 31:
                    keys.append((qi + 1, g.maskN))
            qs = slice(qi * 128, (qi + 1) * 128)
            for j in range(2):
                jb = j * 64
                for idx, (kt, msk) in enumerate(keys):
                    ps = psA if (ne % 2 == 0) else psB
                    E = Eb[ne % 2]
                    ne += 1
                    MM(P, ps[:], K[jb:jb + 64, kt * 128:(kt + 1) * 128], Q[jb:jb + 64, :, qs], reads=[K, Q])
                    S(P, lambda e, ps=ps, E=E: e.activation(out=E[:], in_=ps[:], func=AF.Exp, scale=0.125), [ps], [E])
                    if msk is not None:
                        V(P, lambda e, E=E, msk=msk: e.tensor_tensor(out=E[:], in0=E[:],
                                                                     in1=msk[:].unsqueeze(1).broadcast_to([128, 2, 128]),
                                                                     op=ALU.mult), [E, msk], [E])
                    for gq in range(2):
                        hh = j * 2 + gq
                        MM(P, pos[gq][:, j, 0:65], E[:, gq, :], Vt[:, kt, j, 0:65], start=(idx == 0), stop=(idx == len(keys) - 1),
                           reads=[E, Vt], writes=[pos[gq]])
            pk = [poA, poB]
            y4 = ytm[:].rearrange("p (j q d) -> p j q d", j=2, q=2)
            for gq in range(2):
                V(P, lambda e, gq=gq: e.tensor_tensor(out=den[:, gq:4:2], in0=pos[gq][:, :, 64], in1=es[:, gq:4:2], op=ALU.add),
                  pk + [es], [den])
            V(P, lambda e: e.reciprocal(out=den[:], in_=den[:]), [den], [den])
            for gq in range(2):
                V(P, lambda e, gq=gq: e.tensor_tensor(out=y4[:, :, gq, :], in0=pos[gq][:, :, 0:64],
                                                      in1=den[:, gq:4:2].unsqueeze(2).broadcast_to([128, 2, 64]), op=ALU.mult),
                  pk + [den], [ytm] + pk)
            for c2 in range(2):
                TR(P, pt[:, c2, :], ytm[:, c2 * 128:(c2 + 1) * 128], g.ident[:])
            S(P, lambda e: e.activation(out=yTs[:], in_=pt[:, 0:2, :], func=AF.Copy), [pt], [yTs])
            P.dma("sync", g.yT[3][:, :, qs], yTs[:], reads=[yTs], writes=[("yT", 3)])
    P.barrier()


def neumann_inverse(g, pool_t, Y, X, ident_bc_ap):
    P = g.P
    PPT, Z, psI1, psI2 = pool_t["PPT"], pool_t["Z"], pool_t["psI1"], pool_t["psI2"]
    pP = psI1[:, 0:256].rearrange("p (h t) -> p h t", h=4)
    pPT = psI1[:, 256:512].rearrange("p (h t) -> p h t", h=4)
    pPZ = psI2[:, 0:256].rearrange("p (h t) -> p h t", h=4)
    V(P, lambda e: e.tensor_tensor(out=Z[:], in0=ident_bc_ap, in1=Y[:], op=ALU.subtract), [Y, g.ident], [Z])
    curP = lambda h: Y[:, h, :]
    curPT = lambda h: X[:, h, :]
    rd = [Y, X]
    for k in range(1, 6):
        T2 = PPT[k % 2]
        if k < 5:
            for h in range(4):
                MM(P, pP[:, h, :], curPT(h), curP(h), reads=rd, writes=[psI1])
        for h in range(4):
            MM(P, pPT[:, h, :], curP(h), curPT(h), reads=rd, writes=[psI1])
        if k < 5:
            S(P, lambda e: e.activation(out=T2[:], in_=psI1[:, 0:512].rearrange("p (h t) -> p h t", h=8), func=AF.Copy),
              [psI1], [T2])
        else:
            S(P, lambda e: e.activation(out=T2[:, 4:8, :], in_=pPT, func=AF.Copy), [psI1], [T2])
        yield
        for h in range(4):
            MM(P, pPZ[:, h, :], T2[:, 4 + h, :], Z[:, h, :], reads=[T2, Z], writes=[psI2])
        V(P, lambda e: e.tensor_tensor(out=Z[:], in0=Z[:], in1=pPZ, op=ALU.add), [Z, psI2], [Z])
        if k < 5:
            yield
        curP = lambda h, T2=T2: T2[:, h, :]
        curPT = lambda h, T2=T2: T2[:, 4 + h, :]
        rd = [T2]


def mixer_B_pre(g, l):
    nc, P = g.nc, g.P
    wm = g.wmix
    P.dma("gpsimd", wm[:, :, 0:1040], g.w_in[l, :, 960:2000].rearrange("(kc p) n -> p kc n", p=128))
    with Pool(nc) as pool:
        convw = pool.sb("b_convw", [128, 6, 7], F32)
        xpad_2 = [pool.sb("b_xpad0", [128, 262], F32), pool.sb("b_xpad1", [128, 262], F32)]
        acc_2 = [pool.sb("b_acc0", [128, 256], F32), pool.sb("b_acc1", [128, 256], F32)]
        sT_2 = [pool.sb("b_sT0", [128, 256], F32), pool.sb("b_sT1", [128, 256], F32)]
        sq_2 = [pool.sb("b_sq0", [128, 256], F32), pool.sb("b_sq1", [128, 256], F32)]
        rst_2 = [pool.sb("b_rst0", [128, 256], F32), pool.sb("b_rst1", [128, 256], F32)]
        qn_2 = [pool.sb("b_qn0", [128, 256], F32), pool.sb("b_qn1", [128, 256], F32)]
        qnb_2 = [pool.sb("b_qnb0", [128, 256], BF16), pool.sb("b_qnb1", [128, 256], BF16)]
        stage_2 = [pool.sb("b_stage0", [128, 2, 512], F32), pool.sb("b_stage1", [128, 2, 512], F32)]
        baall = pool.sb("b_baall", [128, NT, 16], F32)
        dtb8 = pool.sb("b_dtb8", [128, 8], F32)
        aex8 = pool.sb("b_aex8", [128, 8], F32)
        pp_2 = [pool.ps("b_pp0", [128, 512], F32), pool.ps("b_pp1", [128, 512], F32)]
        pss_2 = [pool.ps("b_pss0", [128, 512], F32), pool.ps("b_pss1", [128, 512], F32)]
        ptr_2 = [pool.ps("b_ptr0", [128, 512], F32), pool.ps("b_ptr1", [128, 512], F32)]
        P.dma("sync", convw[:], g.convT[:, l])
        for grp in range(NG):
            t0 = grp * 256
            s0, s1 = (0, 256) if grp == 0 else (256, T)
            lo, hi = max(t0 - 3, s0), min(t0 + 259, s1)
            o0, o1 = lo - (t0 - 3), hi - (t0 - 3)
            stage = stage_2[grp % 2]
            for c in range(6):
                q = c % 2
                xpad, acc, sT, sq, rst, qn, qnb = xpad_2[q], acc_2[q], sT_2[q], sq_2[q], rst_2[q], qn_2[q], qnb_2[q]
                pp, pss, ptr = pp_2[q], pss_2[q], ptr_2[q]
                for kc in range(8):
                    MM(P, pp[:, o0:o1], wm[:, kc, c * 128:(c + 1) * 128], g.hT[:, kc, lo:hi], start=(kc == 0), stop=(kc == 7),
                       reads=[wm, g.hT], writes=[pp])
                V(P, lambda e: e.memset(xpad[:], 0.0), [], [xpad])
                S(P, lambda e: e.activation(out=xpad[:, o0:o1], in_=pp[:, o0:o1], func=AF.Copy), [pp], [xpad])
                V(P, lambda e: e.tensor_scalar(out=acc[:], in0=xpad[:, 0:256], scalar1=convw[:, c, 0:1], scalar2=None,
                                               op0=ALU.mult), [xpad, convw], [acc])
                for k in range(1, 7):
                    V(P, lambda e: e.scalar_tensor_tensor(out=acc[:], in0=xpad[:, k:k + 256], scalar=convw[:, c, k:k + 1],
                                                          in1=acc[:], op0=ALU.mult, op1=ALU.add), [xpad, convw, acc], [acc])
                S(P, lambda e: e.activation(out=sT[:], in_=acc[:], func=AF.Silu), [acc], [sT])
                src = sT
                if c < 4:
                    V(P, lambda e: e.tensor_tensor(out=sq[:], in0=sT[:], in1=sT[:], op=ALU.mult), [sT], [sq])
                    MM(P, pss[:, 0:256], g.blk[:], sq[:], reads=[g.blk, sq], writes=[pss])
                    S(P, lambda e: e.activation(out=rst[:], in_=pss[:, 0:256], func=AF.Sqrt, bias=1e-6), [pss], [rst])
                    V(P, lambda e: e.reciprocal(out=rst[:], in_=rst[:]), [rst], [rst])
                    V(P, lambda e: e.scalar_tensor_tensor(out=qn[:], in0=sT[:], scalar=(0.125 if c < 2 else 1.0), in1=rst[:],
                                                          op0=ALU.mult, op1=ALU.mult), [sT, rst], [qn])
                    S(P, lambda e: e.activation(out=qnb[:], in_=qn[:], func=AF.Copy), [qn], [qnb])
                    P.dma("sync", g.bqk[c, :, t0:t0 + 256], qnb[:], reads=[qnb], writes=["bqk"])
                    src = qn
                if c >= 2:
                    for tt in range(2):
                        TR(P, ptr[:, tt * 128:(tt + 1) * 128], src[:, tt * 128:(tt + 1) * 128], g.ident[:], reads=[src, g.ident],
                           writes=[ptr])
                    S(P, lambda e: e.activation(out=stage[:, :, (c - 2) * 128:(c - 1) * 128],
                                                in_=ptr[:, 0:256].rearrange("p (a b) -> p a b", a=2), func=AF.Copy),
                      [ptr], [stage])
            pss = pss_2[0]
            for tt in range(2):
                r0 = t0 + tt * 128
                P.dma("sync", g.bkv[r0:r0 + 128, :], stage[:, tt, :], reads=[stage], writes=["bkv"])
                for kc in range(8):
                    MM(P, pss[:, 256:272], g.hT[:, kc, r0:r0 + 128], wm[:, kc, 768:784], start=(kc == 0), stop=(kc == 7),
                       reads=[wm, g.hT], writes=[pss])
                S(P, lambda e: e.activation(out=baall[:, grp * 2 + tt, :], in_=pss[:, 256:272], func=AF.Copy), [pss], [baall])
        P.dma("sync", dtb8[:], g.gdn_dt_bias[l:l + 1].rearrange("o d h -> o (d h)").broadcast_to([128, 8]))
        P.dma("sync", aex8[:], g.gdn_a_log[l:l + 1].rearrange("o d h -> o (d h)").broadcast_to([128, 8]))
        S(P, lambda e: e.activation(out=aex8[:], in_=aex8[:], func=AF.Exp), [aex8], [aex8])
        bc8 = lambda t: t[:].unsqueeze(1).broadcast_to([128, NT, 8])
        V(P, lambda e: e.tensor_tensor(out=baall[:, :, 8:16], in0=baall[:, :, 8:16], in1=bc8(dtb8), op=ALU.add), [baall, dtb8], [baall])
        S(P, lambda e: e.activation(out=baall[:, :, 8:16], in_=baall[:, :, 8:16], func=AF.Exp), [baall], [baall])
        S(P, lambda e: e.activation(out=baall[:, :, 8:16], in_=baall[:, :, 8:16], func=AF.Ln, bias=1.0), [baall], [baall])
        V(P, lambda e: e.tensor_tensor(out=baall[:, :, 8:16], in0=baall[:, :, 8:16], in1=bc8(aex8), op=ALU.mult), [baall, aex8], [baall])
        S(P, lambda e: e.activation(out=baall[:, :, 0:8], in_=baall[:, :, 0:8], func=AF.Sigmoid), [baall], [baall])
        P.dma("sync", g.bba.rearrange("(t p) c -> p t c", p=128), baall[:], reads=[baall], writes=["bba"])
    P.barrier()
    with Pool(nc) as pool:
        pgs = [pool.ps("b_gp0", [128, 512], F32), pool.ps("b_gp1", [128, 512], F32)]
        sgs = [pool.sb("b_gs0", [128, 256], F32), pool.sb("b_gs1", [128, 256], F32)]
        for ti in range(NT):
            pg, sgt = pgs[ti % 2], sgs[ti % 2]
            for kc in range(8):
                MM(P, pg[:, 0:256], g.hT[:, kc, ti * 128:(ti + 1) * 128], wm[:, kc, 784:1040], start=(kc == 0), stop=(kc == 7),
                   reads=[wm, g.hT], writes=[pg])
            S(P, lambda e: e.activation(out=sgt[:], in_=pg[:, 0:256], func=AF.Silu), [pg], [sgt])
            P.dma("sync", g.bgate[ti * 128:(ti + 1) * 128, :], sgt[:], reads=[sgt], writes=["bgate"])
    P.barrier()


def mixer_B_chains(g, l, pool, banks, psV_shared, PB):
    nc, P = g.nc, g.P
    if True:
        psGC_f, psK2_f, psI1_f, psI2_f, psS2_f, psVA_f = banks
        r4 = lambda ap: ap.rearrange("p (h t) -> p h t", h=4)
        bc4 = lambda ap: ap.unsqueeze(2).broadcast_to([64, 4, 64])
        def chain(d):
            pb = PB

            def sb(name, shape, dt):
                if shape[0] == 64:
                    return H(pool.sb(name, [128] + list(shape[1:]), dt), pb)
                return pool.sb(name, shape, dt)
            tri = H(g.triD, pb)
            identh = g.identD[pb:pb + 64, :]
            psVf = psV_shared
            psV = H(psVf, pb)
            psGC, psK2, psI1, psI2 = H(psGC_f, pb), H(psK2_f, pb), H(psI1_f, pb), H(psI2_f, pb)
            psS2, psVA = H(psS2_f, pb), H(psVA_f, pb)
            pGC, pBR = r4(psGC[:, 0:256]), r4(psGC[:, 256:512])
            pKK, pQK = r4(psK2[:, 0:256]), r4(psK2[:, 256:512])
            pKS, pQS = r4(psS2[:, 0:256]), r4(psS2[:, 256:512])
            pVN, pAV = r4(psVA[:, 0:256]), r4(psVA[:, 256:512])
            pSn = psVf[:, 16:144].rearrange("p (c t) -> p c t", c=2)
            identbc = identh.unsqueeze(1).broadcast_to([64, 4, 64])
            qk = sb("b_qk", [128, 4, 64], BF16)
            kv = sb("b_kv", [64, 512], F32)
            ba = sb("b_ba", [64, 16], F32)
            gcum = sb("b_gcum", [64, 4], F32)
            Eg = sb("b_Eg", [64, 4], F32)
            Es = sb("b_Es", [64, 4], F32)
            be = sb("b_be", [64, 4], F32)
            decbc = sb("b_dec", [128, 4], F32)
            Gbc = sb("b_Gbc", [64, 4, 64], F32)
            Bbc = sb("b_Bbc", [64, 4, 64], F32)
            nd = sb("b_nd", [64, 4, 64], F32)
            ndt = sb("b_ndt", [64, 4, 64], F32)
            DT = sb("b_DT", [64, 4, 64], F32)
            Dn = sb("b_Dn", [64, 4, 64], F32)
            Y = sb("b_Y", [64, 4, 64], F32)
            X = sb("b_X", [64, 4, 64], F32)
            AT = sb("b_AT", [64, 4, 64], BF16)
            tmp = sb("b_tmp", [64, 4, 64], F32)
            tmp2 = sb("b_tmp2", [64, 4, 64], F32)
            scr = dict(PPT=[sb("b_PPT0", [64, 8, 64], F32), sb("b_PPT1", [64, 8, 64], F32)],
                       Z=sb("b_Z", [64, 4, 64], F32))
            scr["psI1"], scr["psI2"] = psI1, psI2
            Vb = sb("b_Vb", [64, 4, 64], F32)
            R = sb("b_R", [64, 4, 64], F32)
            vn = sb("b_vn", [64, 4, 64], BF16)
            kst = sb("b_kst", [64, 4, 64], BF16)
            Sst = sb("b_S", [128, 2, 64], F32)
            Sbf = sb("b_Sbf", [128, 2, 64], BF16)
            osb = sb("b_osb", [64, 256], F32)
            incl = tri[:, 2 * d, :]
            after = tri[:, 2 * d + 1, :]
            before = tri[:, 2 * (1 - d) + 1, :]
            inclbc = incl.unsqueeze(1).broadcast_to([64, 4, 64])
            afterbc = after.unsqueeze(1).broadcast_to([64, 4, 64])
            beforebc = before.unsqueeze(1).broadcast_to([64, 4, 64])
            V(P, lambda e: e.memset(Sst[:], 0.0), [], [Sst])
            V(P, lambda e: e.memset(Sbf[:], 0.0), [], [Sbf])
            for n in chunk_order(d):
                ts = slice(n * 64, (n + 1) * 64)
                P.dma("sync", qk[:], g.bqk[:, :, ts].rearrange("w p t -> p w t"), reads=[], writes=[qk])
                P.dma("gpsimd", kv[:], g.bkv[ts, :], reads=[], writes=[kv])
                P.dma("sync", ba[:], g.bba[ts, :], reads=[], writes=[ba])
                qT = lambda h: qk[(h % 2) * 64:(h % 2) * 64 + 64, h // 2, :]
                kT = lambda h: qk[(h % 2) * 64:(h % 2) * 64 + 64, 2 + h // 2, :]
                vtm = kv[:, 256:512].rearrange("p (h t) -> p h t", h=4)
                ktm = kv[:, 0:256].rearrange("p (h t) -> p h t", h=4)
                beta = AV(ba[:, 4 * d:4 * d + 4], ba.name)
                gneg = AV(ba[:, 8 + 4 * d:12 + 4 * d], ba.name)
                MM(P, psV[0:64, 0:4], incl, gneg[:], reads=[g.tri, gneg], writes=[psV])
                MM(P, psV[0:64, 4:8], after, gneg[:], reads=[g.tri, gneg], writes=[psV])
                MM(P, psVf[:, 8:12], g.ones128[pb:pb + 64, :], gneg[:], reads=[g.ones128, gneg], writes=[psV, psVf])
                S(P, lambda e: e.activation(out=gcum[:], in_=psV[0:64, 0:4], func=AF.Copy), [psV], [gcum])
                S(P, lambda e: e.activation(out=Eg[:], in_=psV[0:64, 0:4], func=AF.Exp, scale=-1.0), [psV], [Eg])
                S(P, lambda e: e.activation(out=Es[:], in_=psV[0:64, 4:8], func=AF.Exp, scale=-1.0), [psV], [Es])
                S(P, lambda e: e.activation(out=decbc[:], in_=psVf[:, 8:12], func=AF.Exp, scale=-1.0), [psV, psVf], [decbc])
                V(P, lambda e: e.tensor_tensor(out=be[:], in0=beta[:], in1=Eg[:], op=ALU.mult), [beta, Eg], [be])
                V(P, lambda e: e.tensor_copy(out=Gbc[:], in_=bc4(gneg[:])), [gneg], [Gbc])
                V(P, lambda e: e.tensor_copy(out=Bbc[:], in_=bc4(beta[:])), [beta], [Bbc])
                yield
                for h in range(4):
                    MM(P, pGC[:, h, :], Gbc[:, h, :], incl, reads=[Gbc, g.tri], writes=[psGC])
                V(P, lambda e: e.tensor_tensor(out=nd[:], in0=pGC, in1=bc4(gcum[:]), op=ALU.subtract), [psGC, gcum], [nd])
                S(P, lambda e: e.activation(out=ndt[:], in_=nd[:], func=AF.Relu), [nd], [ndt])
                S(P, lambda e: e.activation(out=DT[:], in_=ndt[:], func=AF.Exp, scale=-1.0), [ndt], [DT])
                S(P, lambda e: e.activation(out=ndt[:], in_=nd[:], func=AF.Relu, scale=-1.0), [nd, DT], [ndt])
                S(P, lambda e: e.activation(out=Dn[:], in_=ndt[:], func=AF.Exp, scale=-1.0), [ndt], [Dn])
                yield
                for h in range(4):
                    MM(P, pBR[:, h, :], Bbc[:, h, :], identh, reads=[Bbc, g.ident], writes=[psGC])
                for h in (0, 2, 1, 3):
                    MM(P, pKK[:, h, :], kT(h), kT(h), reads=[qk], writes=[psK2])
                for h in (0, 2, 1, 3):
                    MM(P, pQK[:, h, :], kT(h), qT(h), reads=[qk], writes=[psK2])
                V(P, lambda e: e.tensor_tensor(out=tmp[:], in0=pKK, in1=DT[:], op=ALU.mult), [psK2, DT], [tmp])
                V(P, lambda e: e.tensor_tensor(out=tmp[:], in0=tmp[:], in1=beforebc, op=ALU.mult), [tmp, g.tri], [tmp])
                V(P, lambda e: e.tensor_tensor(out=Y[:], in0=tmp[:], in1=pBR, op=ALU.mult), [tmp, psGC], [Y])
                V(P, lambda e: e.tensor_tensor(out=tmp2[:], in0=pKK, in1=Dn[:], op=ALU.mult), [psK2, Dn], [tmp2])
                V(P, lambda e: e.tensor_tensor(out=tmp2[:], in0=tmp2[:], in1=afterbc, op=ALU.mult), [tmp2, g.tri], [tmp2])
                V(P, lambda e: e.tensor_tensor(out=X[:], in0=tmp2[:], in1=bc4(beta[:]), op=ALU.mult), [tmp2, beta], [X])
                V(P, lambda e: e.tensor_tensor(out=tmp[:], in0=pQK, in1=DT[:], op=ALU.mult), [psK2, DT], [tmp])
                V(P, lambda e: e.tensor_tensor(out=AT[:], in0=tmp[:], in1=inclbc, op=ALU.mult), [tmp, g.tri], [AT])
                yield
                yield from neumann_inverse(g, scr, Y, X, identbc)
                Z = scr["Z"]
                yield
                for h in (0, 2, 1, 3):
                    hb, c = (h % 2) * 64, h // 2
                    MM(P, pKS[:, h, :], kT(h), Sbf[hb:hb + 64, c, :], reads=[qk, Sbf], writes=[psS2])
                for h in (0, 2, 1, 3):
                    hb, c = (h % 2) * 64, h // 2
                    MM(P, pQS[:, h, :], qT(h), Sbf[hb:hb + 64, c, :], reads=[qk, Sbf], writes=[psS2])
                V(P, lambda e: e.tensor_tensor(out=Vb[:], in0=vtm, in1=bc4(beta[:]), op=ALU.mult), [kv, beta], [Vb])
                V(P, lambda e: e.tensor_tensor(out=tmp2[:], in0=pKS, in1=bc4(be[:]), op=ALU.mult), [psS2, be], [tmp2])
                V(P, lambda e: e.tensor_tensor(out=R[:], in0=Vb[:], in1=tmp2[:], op=ALU.subtract), [Vb, tmp2], [R])
                V(P, lambda e: e.tensor_tensor(out=tmp[:], in0=pQS, in1=bc4(Eg[:]), op=ALU.mult), [psS2, Eg], [tmp])
                yield
                for h in range(4):
                    MM(P, pVN[:, h, :], Z[:, h, :], R[:, h, :], reads=[Z, R], writes=[psVA])
                S(P, lambda e: e.activation(out=vn[:], in_=pVN, func=AF.Copy), [psVA], [vn])
                yield
                for h in range(4):
                    MM(P, pAV[:, h, :], AT[:, h, :], vn[:, h, :], reads=[AT, vn], writes=[psVA])
                V(P, lambda e: e.tensor_tensor(out=osb[:].rearrange("p (h t) -> p h t", h=4), in0=tmp[:], in1=pAV, op=ALU.add),
                  [tmp, psVA], [osb])
                V(P, lambda e: e.tensor_tensor(out=kst[:], in0=ktm, in1=bc4(Es[:]), op=ALU.mult), [kv, Es], [kst])
                yield
                for h in (0, 2, 1, 3):
                    hb, c = (h % 2) * 64, h // 2
                    MM(P, pSn[hb:hb + 64, c, :], kst[:, h, :], vn[:, h, :], reads=[kst, vn], writes=[psV, psVf])
                for h in range(4):
                    hb, c = (h % 2) * 64, h // 2
                    V(P, lambda e: e.scalar_tensor_tensor(out=Sst[hb:hb + 64, c, :], in0=Sst[hb:hb + 64, c, :],
                                                          scalar=decbc[hb:hb + 64, h:h + 1], in1=pSn[hb:hb + 64, c, :],
                                                          op0=ALU.mult, op1=ALU.add), [Sst, decbc, psV, psVf], [Sst])
                S(P, lambda e: e.activation(out=Sbf[:], in_=Sst[:], func=AF.Copy), [Sst], [Sbf])
                P.dma("sync", g.ofwB[d][ts, :], osb[:], reads=[osb], writes=[("ofwB", d)])

        return [chain(0), chain(1)]


def mixer_B_finish(g, l):
    nc, P = g.nc, g.P
    with Pool(nc) as pool:
        ngbc = pool.sb("b_ng", [64, 256], F32)
        psI2_2 = [pool.ps("b_fpG_a", [64, 512], F32), pool.ps("b_fpG_b", [64, 512], F32)]
        pt_2 = [pool.ps("b_pt_a", [128, 4, 128], F32), pool.ps("b_pt_b", [128, 4, 128], F32)]
        P.dma("sync", ngbc[:], g.ngB[l:l + 1, :].broadcast_to([64, 256]))
        osb_2 = [pool.sb("b_fo_a", [64, 256], F32), pool.sb("b_fo_b", [64, 256], F32)]
        ofl_2 = [pool.sb("b_fl_a", [64, 256], F32), pool.sb("b_fl_b", [64, 256], F32)]
        ysq_2 = [pool.sb("b_fysq_a", [64, 256], F32), pool.sb("b_fysq_b", [64, 256], F32)]
        ss4_2 = [pool.sb("b_fss4_a", [64, 4], F32), pool.sb("b_fss4_b", [64, 4], F32)]
        sg_2 = [pool.sb("b_fsg_a", [64, 256], F32), pool.sb("b_fsg_b", [64, 256], F32)]
        yTs_2 = [pool.sb("b_fyTs_a", [128, 2, 64], BF16), pool.sb("b_fyTs_b", [128, 2, 64], BF16)]
        for n in range(68):
            osb, ofl, ysq, ss4, sg, yTs, psI2, pt = osb_2[n % 2], ofl_2[n % 2], ysq_2[n % 2], ss4_2[n % 2], sg_2[n % 2], yTs_2[n % 2], psI2_2[n % 2], pt_2[n % 2]
            pG = psI2[:, 256:512]
            ts = slice(n * 64, (n + 1) * 64)
            P.dma("sync", osb[:], g.ofwB[0][ts, :], reads=[], writes=[osb])
            P.dma("gpsimd", ofl[:], g.ofwB[1][ts, :], reads=[], writes=[ofl])
            P.dma("sync", sg[:], g.bgate[ts, :], reads=[], writes=[sg])
            V(P, lambda e: e.tensor_tensor(out=osb[:], in0=osb[:], in1=ofl[:], op=ALU.add), [osb, ofl], [osb])
            rms_gate_finish(g, osb, ysq, ss4, ngbc, None, None, sg, pt, yTs, 1, ts, 1e-6)
    P.barrier()


def mixer_A_pre(g, l):
    nc, P = g.nc, g.P
    wm = g.wmix
    CW = float(np.exp(-0.5))
    P.dma("gpsimd", wm[:, :, 0:960], g.w_in[l, :, 0:960].rearrange("(kc p) n -> p kc n", p=128))
    with Pool(nc) as pool:
        mu = pool.sb("a_mu", [128, 8], F32)
        omu = pool.sb("a_omu", [128, 8], F32)
        hmu = pool.sb("a_hmu", [128, 8], F32)
        kkw = pool.sb("a_kkw", [128, 2], F32)
        xpad_2 = [pool.sb("a_xpad0", [128, 258], F32), pool.sb("a_xpad1", [128, 258], F32)]
        acc_2 = [pool.sb("a_acc0", [128, 256], F32), pool.sb("a_acc1", [128, 256], F32)]
        t1_2 = [pool.sb("a_t10", [128, 256], F32), pool.sb("a_t11", [128, 256], F32)]
        sq_2 = [pool.sb("a_sq0", [128, 256], F32), pool.sb("a_sq1", [128, 256], F32)]
        rst_2 = [pool.sb("a_rst0", [128, 256], F32), pool.sb("a_rst1", [128, 256], F32)]
        kkn_2 = [pool.sb("a_kkn0", [128, 256], F32), pool.sb("a_kkn1", [128, 256], F32)]
        stage_2 = [pool.sb("a_stage0", [128, 2, 1024], F32), pool.sb("a_stage1", [128, 2, 1024], F32)]
        pp_2 = [pool.ps("a_pp0", [128, 512], F32), pool.ps("a_pp1", [128, 512], F32)]
        pss_2 = [pool.ps("a_pss0", [128, 512], F32), pool.ps("a_pss1", [128, 512], F32)]
        ptr_2 = [pool.ps("a_ptr0", [128, 512], F32), pool.ps("a_ptr1", [128, 512], F32)]
        P.dma("sync", mu[:], g.muT[:, l])
        P.dma("sync", kkw[:], g.kkwT[:, l])
        V(P, lambda e: e.tensor_scalar(out=omu[:], in0=mu[:], scalar1=-1.0, scalar2=1.0, op0=ALU.mult, op1=ALU.add), [mu], [omu])
        V(P, lambda e: e.tensor_scalar(out=hmu[:], in0=mu[:], scalar1=0.5, scalar2=None, op0=ALU.mult), [mu], [hmu])

        def to_tm(src, col0, ptr, stage):
            for tt in range(2):
                TR(P, ptr[:, tt * 128:(tt + 1) * 128], src[:, tt * 128:(tt + 1) * 128], g.ident[:], reads=[src, g.ident], writes=[ptr])
            S(P, lambda e: e.activation(out=stage[:, :, col0:col0 + 128], in_=ptr[:, 0:256].rearrange("p (a b) -> p a b", a=2),
                                        func=AF.Copy), [ptr], [stage])

        for grp in range(NG):
            t0 = grp * 256
            s0, s1 = (0, 256) if grp == 0 else (256, T)
            lo, hi = max(t0 - 1, s0), min(t0 + 257, s1)
            o0, o1 = lo - (t0 - 1), hi - (t0 - 1)
            stage = stage_2[grp % 2]
            for c in range(8):
                q = c % 2
                xpad, acc, t1, sq, rst, kkn = xpad_2[q], acc_2[q], t1_2[q], sq_2[q], rst_2[q], kkn_2[q]
                pp, pss, ptr = pp_2[q], pss_2[q], ptr_2[q]
                npart = 128 if c < 7 else 64
                ncol = 128 if c < 7 else 64
                for kc in range(8):
                    MM(P, pp[0:npart, o0:o1], wm[:, kc, c * 128:c * 128 + ncol], g.hT[:, kc, lo:hi], start=(kc == 0), stop=(kc == 7),
                       reads=[wm, g.hT], writes=[pp])
                V(P, lambda e: e.memset(xpad[:], 0.0), [], [xpad])
                S(P, lambda e: e.activation(out=xpad[0:npart, o0:o1], in_=pp[0:npart, o0:o1], func=AF.Copy), [pp], [xpad])
                V(P, lambda e: e.tensor_scalar(out=acc[:], in0=xpad[:, 1:257], scalar1=omu[:, c:c + 1], scalar2=None, op0=ALU.mult),
                  [xpad, omu], [acc])
                V(P, lambda e: e.tensor_tensor(out=t1[:], in0=xpad[:, 0:256], in1=xpad[:, 2:258], op=ALU.add), [xpad], [t1])
                V(P, lambda e: e.scalar_tensor_tensor(out=acc[:], in0=t1[:], scalar=hmu[:, c:c + 1], in1=acc[:], op0=ALU.mult,
                                                      op1=ALU.add), [t1, hmu, acc], [acc])
                if c < 2:
                    P.dma("sync", g.aF[c, :, t0:t0 + 256], acc[:], reads=[acc], writes=["aF"])
                    to_tm(acc, c * 128, ptr, stage)
                elif c < 4:
                    P.dma("sync", g.aF[c, :, t0:t0 + 256], acc[:], reads=[acc], writes=["aF"])
                    to_tm(acc, c * 128, ptr, stage)
                    V(P, lambda e: e.tensor_scalar(out=kkn[:], in0=acc[:], scalar1=kkw[:, c - 2:c - 1], scalar2=None, op0=ALU.mult),
                      [acc, kkw], [kkn])
                    V(P, lambda e: e.tensor_tensor(out=sq[:], in0=kkn[:], in1=kkn[:], op=ALU.mult), [kkn], [sq])
                    MM(P, pss[:, 0:256], g.blk[:], sq[:], reads=[g.blk, sq], writes=[pss])
                    S(P, lambda e: e.activation(out=rst[:], in_=pss[:, 0:256], func=AF.Sqrt, bias=1e-6), [pss], [rst])
                    V(P, lambda e: e.reciprocal(out=rst[:], in_=rst[:]), [rst], [rst])
                    V(P, lambda e: e.tensor_tensor(out=kkn[:], in0=kkn[:], in1=rst[:], op=ALU.mult), [kkn, rst], [kkn])
                    P.dma("sync", g.aF[c + 2, :, t0:t0 + 256], kkn[:], reads=[kkn], writes=["aF"])
                    to_tm(kkn, 768 + (c - 2) * 128, ptr, stage)
                elif c < 6:
                    to_tm(acc, c * 128, ptr, stage)
                elif c == 6:
                    S(P, lambda e: e.activation(out=acc[0:64, :], in_=acc[0:64, :], func=AF.Tanh), [acc], [acc])
                    P.dma("sync", g.asm[:, t0:t0 + 256], acc[:], reads=[acc], writes=["asm"])
                else:
                    S(P, lambda e: e.activation(out=acc[0:64, :], in_=acc[0:64, :], func=AF.Sigmoid), [acc], [acc])
                    P.dma("sync", g.asg[:, t0:t0 + 256], acc[0:64, :], reads=[acc], writes=["asg"])
            for tt in range(2):
                r0 = t0 + tt * 128
                P.dma("sync", g.aTM[r0:r0 + 128, :], stage[:, tt, :], reads=[stage], writes=["aTM"])
    P.barrier()


def mixer_A_chains(g, l, pool, banks, psAF_shared, PB):
    nc, P = g.nc, g.P
    CW = float(np.exp(-0.5))
    if True:
        psW_f, psM1_f, psM2_f, psM3_f, psI1_f, psI2_f = banks
        sb = pool.sb
        w2 = sb("a_w2", [32, 2, 256], F32)
        a2 = sb("a_a2", [32, 2, 256], F32)
        w0 = sb("a_w0", [1, 2, 256], F32)
        a0 = sb("a_a0", [1, 2, 256], F32)
        a0T = sb("a_a0T", [128, 2, 2], F32)
        kaT = sb("a_kaT", [128, 2], F32)
        kabc_f = pool.sb("a_kabc", [128, 256], F32)
        rkbc_f = pool.sb("a_rkbc", [128, 256], F32)
        r4 = lambda ap: ap.rearrange("p (h t) -> p h t", h=4)
        r2 = lambda ap: ap.rearrange("p (c t) -> p c t", c=2)
        bc4 = lambda ap: ap.unsqueeze(2).broadcast_to([64, 4, 64])
        P.dma("sync", w2[:], g.rwkv_w2[l].rearrange("d r n -> r d n"))
        P.dma("sync", a2[:], g.rwkv_a2[l].rearrange("d r n -> r d n"))
        P.dma("sync", w0[:], g.rwkv_w0[l:l + 1])
        P.dma("sync", a0[:], g.rwkv_a0[l:l + 1])
        P.dma("sync", a0T[:], g.a0T[:, l])
        P.dma("sync", kaT[:], g.kaT[:, l])
        P.dma("sync", kabc_f[:], g.rwkv_ka[l:l + 1, :].broadcast_to([128, 256]))
        P.dma("sync", rkbc_f[:], g.rwkv_rk[l:l + 1, :].broadcast_to([128, 256]))

        def chain(d):
            pb = PB
            _sb = pool.sb

            def sb(name, shape, dt):
                if shape[0] == 64:
                    return H(_sb(name, [128] + list(shape[1:]), dt), pb)
                return _sb(name, shape, dt)
            kabc, rkbc = H(kabc_f, pb), H(rkbc_f, pb)
            tri = H(g.triD, pb)
            identh = g.identD[pb:pb + 64, :]
            identbc = identh.unsqueeze(1).broadcast_to([64, 4, 64])
            psAF = psAF_shared
            psW, psM1, psM2, psM3 = H(psW_f, pb), H(psM1_f, pb), H(psM2_f, pb), H(psM3_f, pb)
            psI1, psI2 = H(psI1_f, pb), H(psI2_f, pb)
            psSuf = psW
            pWr = psW[:, 0:256]
            pAtm = pWr
            pY = r4(psW[:, 0:256])
            pAT_, pCI, pCE, pSn = r2(psAF[:, 0:128]), r2(psAF[:, 128:256]), r2(psAF[:, 256:384]), r2(psAF[:, 384:512])
            pSufR = psW[:, 256:512]
            pKb, pXm = r4(psM1[:, 0:256]), r4(psM1[:, 256:512])
            pKk, pRk = r4(psM2[:, 0:256]), r4(psM2[:, 256:512])
            pRb, pRU = r4(psM3[:, 0:256]), r4(psM3[:, 256:512])
            pU = r4(psI2[:, 256:512])
            F = sb("a_F", [128, 6, 64], F32)
            TM = sb("a_TM", [64, 1024], F32)
            tw = sb("a_tw", [32, 64], F32)
            ta = sb("a_ta", [32, 64], F32)
            Lw = sb("a_Lw", [64, 256], F32)
            atm = sb("a_atm", [64, 256], F32)
            aT = sb("a_aT", [128, 2, 64], F32)
            v2 = lambda ap: ap.rearrange("p (c t) -> p c t", c=2)
            CD = sb("a_CD", [128, 2, 2, 128], F32)
            E4 = sb("a_E4", [128, 2, 2, 128], F32)
            cumI, cumE = v2(CD[:, 1, 0, :]), v2(CD[:, 1, 1, :])
            Erg = v2(E4[:, 1, 0, :])
            Esrc = sb("a_Esrc", [128, 128], F32)
            O4 = sb("a_O4", [128, 2, 2, 128], BF16)
            rx, kkx, rg, kkg = v2(O4[:, 0, 0, :]), v2(O4[:, 0, 1, :]), v2(O4[:, 1, 0, :]), v2(O4[:, 1, 1, :])
            KB = sb("a_KB", [128, 2, 128], F32)
            kdT, bT = v2(KB[:, 0, :]), v2(KB[:, 1, :])
            KBx = sb("a_KBx", [128, 2, 128], BF16)
            kdx, bx = v2(KBx[:, 0, :]), v2(KBx[:, 1, :])
            YX = sb("a_YX", [64, 2, 4, 64], BF16)
            AR = sb("a_AR", [64, 2, 4, 64], BF16)
            Y, X, AkkT, ArkT = YX[:, 0], YX[:, 1], AR[:, 0], AR[:, 1]
            Esuf = sb("a_Esuf", [64, 256], F32)
            tf = sb("a_tf", [128, 2, 64], F32)
            tt_ = sb("a_tt", [64, 256], F32)
            kdtm = sb("a_kdtm", [64, 256], F32)
            kdst = sb("a_kdst", [64, 256], BF16)
            nbst = sb("a_nbst", [64, 256], BF16)
            bon4 = sb("a_bon4", [64, 4], F32)
            bonus = sb("a_bonus", [64, 256], F32)
            nArbT = sb("a_nArbT", [64, 4, 64], BF16)
            scr = dict(PPT=[sb("a_PPT0", [64, 8, 64], BF16), sb("a_PPT1", [64, 8, 64], BF16)],
                       Z=sb("a_Z", [64, 4, 64], BF16))
            scr["psI1"], scr["psI2"] = psI1, psI2
            RUs = sb("a_RUs", [64, 4, 64], BF16)
            Us = sb("a_Us", [64, 4, 64], BF16)
            Sst = sb("a_S", [128, 2, 64], F32)
            Sbf = sb("a_Sbf", [128, 2, 64], BF16)
            vbf = sb("a_vbf", [64, 256], BF16)
            osb = sb("a_osb", [64, 512], F32)
            rtm, ktm, vtm, kktm = TM[:, 0:256], TM[:, 256:512], TM[:, 512:768], TM[:, 768:1024]
            incl = tri[:, 2 * d, :]
            after = tri[:, 2 * d + 1, :]
            before = tri[:, 2 * (1 - d) + 1, :]
            inclbc = incl.unsqueeze(1).broadcast_to([64, 4, 64])
            afterbc = after.unsqueeze(1).broadcast_to([64, 4, 64])
            beforebc = before.unsqueeze(1).broadcast_to([64, 4, 64])
            last = 63 if d == 0 else 0
            ref = 32 if d == 0 else 31
            V(P, lambda e: e.memset(Sst[:], 0.0), [], [Sst])
            V(P, lambda e: e.memset(Sbf[:], 0.0), [], [Sbf])
            for n in chunk_order(d):
                ts = slice(n * 64, (n + 1) * 64)
                P.dma("sync", F[:], g.aF[:, :, ts].rearrange("w p t -> p w t"), reads=[], writes=[F])
                P.dma("gpsimd", TM[:], g.aTM[ts, :], reads=[], writes=[TM])
                P.dma("sync", tw[:], g.asm[32 * d:32 * d + 32, ts], reads=[], writes=[tw])
                P.dma("sync", ta[:], g.asm[64 + 32 * d:96 + 32 * d, ts], reads=[], writes=[ta])
                fh = lambda t, w_, h: t[(h % 2) * 64:(h % 2) * 64 + 64, w_ + h // 2, :]
                h2 = lambda t, h: t[(h % 2) * 64:(h % 2) * 64 + 64, h // 2, :]
                hs = lambda h: slice(h * 64, (h + 1) * 64)
                S(P, lambda e: e.activation(out=vbf[:], in_=vtm, func=AF.Copy), [TM], [vbf])
                yield
                MM(P, pWr, tw[:], w2[:, d, :], start=True, stop=False, reads=[tw, w2], writes=[psW])
                MM(P, pWr, g.ones1[:], w0[:, d, :], start=False, stop=True, reads=[g.ones1, w0], writes=[psW])
                S(P, lambda e: e.activation(out=Lw[:], in_=pWr, func=AF.Sigmoid), [psW], [Lw])
                MM(P, pAtm, ta[:], a2[:, d, :], start=True, stop=False, reads=[ta, a2], writes=[psW])
                MM(P, pAtm, g.ones1[:], a0[:, d, :], start=False, stop=True, reads=[g.ones1, a0], writes=[psW])
                S(P, lambda e: e.activation(out=atm[:], in_=pAtm, func=AF.Sigmoid), [psW], [atm])
                yield
                for c in range(2):
                    MM(P, pAT_[:, c, :], a2[:, d, c * 128:(c + 1) * 128], ta[:], reads=[a2, ta], writes=[psAF])
                for c in range(2):
                    S(P, lambda e: e.activation(out=aT[:, c, :], in_=pAT_[:, c, :], func=AF.Sigmoid, bias=a0T[:, d, c:c + 1]),
                      [psAF, a0T], [aT])
                yield
                for c in range(2):
                    MM(P, pCI[:, c, :], Lw[:, c * 128:(c + 1) * 128], incl, reads=[Lw, g.tri], writes=[psAF])
                for c in range(2):
                    MM(P, pCE[:, c, :], Lw[:, c * 128:(c + 1) * 128], before, reads=[Lw, g.tri], writes=[psAF])
                MM(P, pSufR, after, Lw[:], reads=[g.tri, Lw], writes=[psSuf])
                S(P, lambda e: e.activation(out=CD[:, 1, :, :], in_=psAF[:, 128:384].rearrange("p (a n) -> p a n", a=2), func=AF.Copy), [psAF], [CD])
                S(P, lambda e: e.activation(out=Esuf[:], in_=pSufR, func=AF.Exp, scale=-CW), [psSuf], [Esuf])
                for c in range(2):
                    V(P, lambda e: e.tensor_scalar(out=CD[:, 0, :, c * 64:(c + 1) * 64], in0=CD[:, 1, :, c * 64:(c + 1) * 64],
                                                   scalar1=cumI[:, c, ref:ref + 1], scalar2=None, op0=ALU.subtract), [CD], [CD])
                S(P, lambda e: e.activation(out=E4[:], in_=CD[:], func=AF.Exp, scale=-CW), [CD], [E4])
                S(P, lambda e: e.activation(out=Esrc[:], in_=CD[:, 0, 0, :], func=AF.Exp, scale=CW), [CD], [Esrc])
                for c in range(2):
                    V(P, lambda e: e.tensor_scalar(out=tf[:, c, :], in0=aT[:, c, :], scalar1=-1.0, scalar2=kaT[:, c:c + 1], op0=ALU.add,
                                                   op1=ALU.mult), [aT, kaT], [tf])
                V(P, lambda e: e.scalar_tensor_tensor(out=kdT[:], in0=tf[:], scalar=1.0, in1=F[:, 2:4, :], op0=ALU.add, op1=ALU.mult),
                  [tf, F], [kdT])
                V(P, lambda e: e.tensor_tensor(out=bT[:], in0=F[:, 4:6, :], in1=aT[:], op=ALU.mult), [F, aT], [bT])
                V(P, lambda e: e.tensor_tensor(out=O4[:], in0=E4[:],
                                               in1=F[:].rearrange("p (w c) t -> p w (c t)", w=3)[:, 0:3:2, :].unsqueeze(1).broadcast_to([128, 2, 2, 128]),
                                               op=ALU.mult), [E4, F], [O4])
                V(P, lambda e: e.tensor_tensor(out=KBx[:], in0=KB[:], in1=Esrc[:].unsqueeze(1).broadcast_to([128, 2, 128]), op=ALU.mult),
                  [KB, Esrc], [KBx])
                V(P, lambda e: e.scalar_tensor_tensor(out=tt_[:], in0=atm[:], scalar=-1.0, in1=kabc[:], op0=ALU.add, op1=ALU.mult),
                  [atm, kabc], [tt_])
                V(P, lambda e: e.scalar_tensor_tensor(out=kdtm[:], in0=tt_[:], scalar=1.0, in1=ktm, op0=ALU.add, op1=ALU.mult),
                  [tt_, TM], [kdtm])
                V(P, lambda e: e.tensor_tensor(out=kdst[:], in0=kdtm[:], in1=Esuf[:], op=ALU.mult), [kdtm, Esuf], [kdst])
                V(P, lambda e: e.tensor_tensor(out=tt_[:], in0=kktm, in1=atm[:], op=ALU.mult), [TM, atm], [tt_])
                V(P, lambda e: e.scalar_tensor_tensor(out=nbst[:], in0=tt_[:], scalar=-1.0, in1=Esuf[:], op0=ALU.mult, op1=ALU.mult),
                  [tt_, Esuf], [nbst])
                V(P, lambda e: e.tensor_tensor(out=tt_[:], in0=rtm, in1=kdtm[:], op=ALU.mult), [TM, kdtm, nbst], [tt_])
                V(P, lambda e: e.tensor_tensor(out=tt_[:], in0=tt_[:], in1=rkbc[:], op=ALU.mult), [tt_, rkbc], [tt_])
                V(P, lambda e: e.tensor_reduce(out=bon4[:], in_=tt_[:].rearrange("p (h d) -> p h d", h=4), axis=AX.X, op=ALU.add),
                  [tt_], [bon4])
                V(P, lambda e: e.tensor_tensor(out=bonus[:].rearrange("p (h d) -> p h d", h=4), in0=vtm.rearrange("p (h d) -> p h d", h=4),
                                               in1=bc4(bon4[:]), op=ALU.mult), [TM, bon4], [bonus])
                yield
                for h in (0, 2, 1, 3):
                    MM(P, pKb[:, h, :], h2(bx, h), h2(kkx, h), reads=[bx, kkx], writes=[psM1])
                for h in (0, 2, 1, 3):
                    MM(P, pXm[:, h, :], h2(kkx, h), h2(bx, h), reads=[bx, kkx], writes=[psM1])
                for h in (0, 2, 1, 3):
                    MM(P, pKk[:, h, :], h2(kdx, h), h2(kkx, h), reads=[kdx, kkx], writes=[psM2])
                for h in (0, 2, 1, 3):
                    MM(P, pRk[:, h, :], h2(kdx, h), h2(rx, h), reads=[kdx, rx], writes=[psM2])
                for h in (0, 2, 1, 3):
                    MM(P, pRb[:, h, :], h2(bx, h), h2(rx, h), reads=[bx, rx], writes=[psM3])
                V(P, lambda e: e.tensor_tensor(out=Y, in0=pKb, in1=beforebc, op=ALU.mult), [psM1, g.tri], [Y])
                V(P, lambda e: e.tensor_tensor(out=X, in0=pXm, in1=afterbc, op=ALU.mult), [psM1, g.tri], [X])
                V(P, lambda e: e.tensor_tensor(out=AkkT, in0=pKk, in1=beforebc, op=ALU.mult), [psM2, g.tri], [AkkT])
                V(P, lambda e: e.tensor_tensor(out=ArkT, in0=pRk, in1=inclbc, op=ALU.mult), [psM2, g.tri], [ArkT])
                V(P, lambda e: e.scalar_tensor_tensor(out=nArbT[:], in0=pRb, scalar=-1.0, in1=inclbc, op0=ALU.mult, op1=ALU.mult),
                  [psM3, g.tri], [nArbT])
                yield
                yield from neumann_inverse(g, scr, Y, X, identbc)
                Z = scr["Z"]
                yield
                for h in (0, 2, 1, 3):
                    hb, c = (h % 2) * 64, h // 2
                    MM(P, pRU[:, h, :], h2(kkg, h), Sbf[hb:hb + 64, c, :], start=True, stop=False, reads=[kkg, Sbf], writes=[psM3])
                    MM(P, pRU[:, h, :], AkkT[:, h, :], vbf[:, hs(h)], start=False, stop=True, reads=[AkkT, vbf], writes=[psM3])
                S(P, lambda e: e.activation(out=RUs[:], in_=pRU, func=AF.Copy), [psM3], [RUs])
                yield
                for h in range(4):
                    MM(P, pU[:, h, :], Z[:, h, :], RUs[:, h, :], reads=[Z, RUs], writes=[psI2])
                S(P, lambda e: e.activation(out=Us[:], in_=pU, func=AF.Copy), [psI2], [Us])
                yield
                for h in (0, 2, 1, 3):
                    hb, c = (h % 2) * 64, h // 2
                    MM(P, pY[:, h, :], h2(rg, h), Sbf[hb:hb + 64, c, :], start=True, stop=False, reads=[rg, Sbf], writes=[psW])
                    MM(P, pY[:, h, :], ArkT[:, h, :], vbf[:, hs(h)], start=False, stop=False, reads=[ArkT, vbf], writes=[psW])
                    MM(P, pY[:, h, :], nArbT[:, h, :], Us[:, h, :], start=False, stop=True, reads=[nArbT, Us], writes=[psW])
                for h in (0, 2, 1, 3):
                    hb, c = (h % 2) * 64, h // 2
                    MM(P, pSn[hb:hb + 64, c, :], kdst[:, hs(h)], vbf[:, hs(h)], start=True, stop=False, reads=[kdst, vbf], writes=[psAF])
                    MM(P, pSn[hb:hb + 64, c, :], nbst[:, hs(h)], Us[:, h, :], start=False, stop=True, reads=[nbst, Us], writes=[psAF])
                for c in range(2):
                    V(P, lambda e: e.scalar_tensor_tensor(out=Sst[:, c, :], in0=Sst[:, c, :], scalar=Erg[:, c, last:last + 1],
                                                          in1=pSn[:, c, :], op0=ALU.mult, op1=ALU.add), [Sst, Erg, psAF], [Sst])
                S(P, lambda e: e.activation(out=Sbf[:], in_=Sst[:], func=AF.Copy), [Sst], [Sbf])
                S(P, lambda e: e.activation(out=osb[:, 0:256], in_=psW[:, 0:256], func=AF.Copy), [psW], [osb])
                S(P, lambda e: e.activation(out=osb[:, 256:512], in_=bonus[:], func=AF.Copy), [bonus], [osb])
                P.dma("sync", g.ofw2[d][ts, :], osb[:], reads=[osb], writes=[("ofw", d)])
                yield

        return [chain(0), chain(1)]


def mixer_A_finish(g, l):
    nc, P = g.nc, g.P
    with Pool(nc) as pool:
        g2 = pool.sb("a_g2", [64, 256], F32)
        lngbc = pool.sb("a_lng", [64, 256], F32)
        lnbbc = pool.sb("a_lnb", [64, 256], F32)
        psSuf_2 = [pool.ps("a_fpS_a", [64, 512], F32), pool.ps("a_fpS_b", [64, 512], F32)]
        psAF_2 = [pool.ps("a_fpAF_a", [128, 512], F32), pool.ps("a_fpAF_b", [128, 512], F32)]
        bc4 = lambda ap: ap.unsqueeze(2).broadcast_to([64, 4, 64])
        P.dma("sync", g2[:], g.rwkv_g2[l])
        P.dma("sync", lngbc[:], g.rwkv_ln_g[l:l + 1, :].broadcast_to([64, 256]))
        P.dma("sync", lnbbc[:], g.rwkv_ln_b[l:l + 1, :].broadcast_to([64, 256]))
        osb_2 = [pool.sb("a_fo_a", [64, 512], F32), pool.sb("a_fo_b", [64, 512], F32)]
        ofl_2 = [pool.sb("a_fl_a", [64, 512], F32), pool.sb("a_fl_b", [64, 512], F32)]
        bonus_2 = [pool.sb("a_fbon_a", [64, 256], F32), pool.sb("a_fbon_b", [64, 256], F32)]
        sgT_2 = [pool.sb("a_fsgT_a", [64, 64], F32), pool.sb("a_fsgT_b", [64, 64], F32)]
        ysq_2 = [pool.sb("a_fysq_a", [64, 256], F32), pool.sb("a_fysq_b", [64, 256], F32)]
        yc_2 = [pool.sb("a_fyc_a", [64, 256], F32), pool.sb("a_fyc_b", [64, 256], F32)]
        ss4_2 = [pool.sb("a_fss4_a", [64, 4], F32), pool.sb("a_fss4_b", [64, 4], F32)]
        mean4_2 = [pool.sb("a_fmean4_a", [64, 4], F32), pool.sb("a_fmean4_b", [64, 4], F32)]
        gsb_2 = [pool.sb("a_fgsb_a", [64, 256], F32), pool.sb("a_fgsb_b", [64, 256], F32)]
        yTs_2 = [pool.sb("a_fyTs_a", [128, 2, 64], BF16), pool.sb("a_fyTs_b", [128, 2, 64], BF16)]
        for n in range(68):
            osb, ofl, bonus, sgT, ysq, yc, ss4, mean4, gsb, yTs, psSuf, psAF = osb_2[n % 2], ofl_2[n % 2], bonus_2[n % 2], sgT_2[n % 2], ysq_2[n % 2], yc_2[n % 2], ss4_2[n % 2], mean4_2[n % 2], gsb_2[n % 2], yTs_2[n % 2], psSuf_2[n % 2], psAF_2[n % 2]
            pGate = psSuf[:, 256:512]
            pTr = psAF[:, 0:128].rearrange("p (c t) -> p c t", c=2)
            ts = slice(n * 64, (n + 1) * 64)
            P.dma("sync", osb[:], g.ofw2[0][ts, :], reads=[], writes=[osb])
            P.dma("gpsimd", ofl[:], g.ofw2[1][ts, :], reads=[], writes=[ofl])
            P.dma("sync", sgT[:], g.asg[:, ts], reads=[], writes=[sgT])
            V(P, lambda e: e.tensor_tensor(out=osb[:], in0=osb[:], in1=ofl[:], op=ALU.add), [osb, ofl], [osb])
            V(P, lambda e: e.tensor_copy(out=bonus[:], in_=osb[:, 256:512]), [osb], [bonus])
            MM(P, pGate, sgT[:], g2[:], reads=[sgT, g2], writes=[psSuf])
            S(P, lambda e: e.activation(out=gsb[:], in_=pGate, func=AF.Copy), [psSuf], [gsb])
            y = osb[:, 0:256]
            y4 = y.rearrange("p (h d) -> p h d", h=4)
            V(P, lambda e: e.tensor_reduce(out=mean4[:], in_=y4, axis=AX.X, op=ALU.add), [osb], [mean4])
            V(P, lambda e: e.tensor_scalar(out=mean4[:], in0=mean4[:], scalar1=1.0 / 64, scalar2=None, op0=ALU.mult), [mean4], [mean4])
            V(P, lambda e: e.tensor_tensor(out=yc[:].rearrange("p (h d) -> p h d", h=4), in0=y4, in1=bc4(mean4[:]),
                                           op=ALU.subtract), [osb, mean4], [yc])
            V(P, lambda e: e.tensor_tensor(out=ysq[:], in0=yc[:], in1=yc[:], op=ALU.mult), [yc], [ysq])
            V(P, lambda e: e.tensor_reduce(out=ss4[:], in_=ysq[:].rearrange("p (h d) -> p h d", h=4), axis=AX.X, op=ALU.add),
              [ysq], [ss4])
            S(P, lambda e: e.activation(out=ss4[:], in_=ss4[:], func=AF.Sqrt, bias=64e-5, scale=1.0 / 64), [ss4], [ss4])
            V(P, lambda e: e.reciprocal(out=ss4[:], in_=ss4[:]), [ss4], [ss4])
            V(P, lambda e: e.tensor_tensor(out=ysq[:].rearrange("p (h d) -> p h d", h=4), in0=yc[:].rearrange("p (h d) -> p h d", h=4),
                                           in1=bc4(ss4[:]), op=ALU.mult), [yc, ss4], [ysq])
            V(P, lambda e: e.tensor_tensor(out=ysq[:], in0=ysq[:], in1=lngbc[:], op=ALU.mult), [ysq, lngbc], [ysq])
            V(P, lambda e: e.tensor_tensor(out=ysq[:], in0=ysq[:], in1=lnbbc[:], op=ALU.add), [ysq, lnbbc], [ysq])
            V(P, lambda e: e.tensor_tensor(out=ysq[:], in0=ysq[:], in1=bonus[:], op=ALU.add), [ysq, bonus], [ysq])
            V(P, lambda e: e.tensor_tensor(out=ysq[:], in0=ysq[:], in1=gsb[:], op=ALU.mult), [ysq, gsb], [ysq])
            for c2 in range(2):
                TR(P, pTr[:, c2, :], ysq[:, c2 * 128:(c2 + 1) * 128], g.ident[0:64, 0:64], reads=[ysq, g.ident], writes=[psAF])
            S(P, lambda e: e.activation(out=yTs[:], in_=pTr, func=AF.Copy), [psAF], [yTs])
            P.dma("sync", g.yT[0][:, :, ts], yTs[:], reads=[yTs], writes=[("yT", 0)])
    P.barrier()


def mixers_AB_sweeps(g, l):
    nc, P = g.nc, g.P
    with Pool(nc) as pool:
        banks = [pool.ps("ab_bank%d" % i, [128, 512], F32) for i in range(6)]
        psAF = pool.ps("ab_pAF", [128, 512], F32)
        psV = pool.ps("ab_pV", [128, 512], F32)
        ga = mixer_A_chains(g, l, pool, banks, psAF, 0)
        gb = mixer_B_chains(g, l, pool, banks, psV, 64)
        gens = [ga[0], gb[0], ga[1], gb[1]]
        if os.environ.get("KSEQ", ""):
            for gen in gens:
                for _ in gen:
                    pass
            gens = []
        while gens:
            for gen in list(gens):
                try:
                    next(gen)
                except StopIteration:
                    gens.remove(gen)
        P.barrier()


def chunk_order(d):
    lim = int(os.environ.get("KNCH", "68"))
    if d == 0:
        return list(range(68))[:lim]
    return ([3, 2, 1, 0] + list(range(67, 3, -1)))[:lim]


def mixer_C(g, l):
    nc, P = g.nc, g.P
    wm = g.wmix
    SC = 32.0 ** -0.5
    with Pool(nc) as pool:
        gw2 = pool.sb("c_gw2", [16, 2, 256], F32)
        gb = pool.sb("c_gb", [1, 2, 256], F32)
        psTM_f = pool.ps("c_pTM", [128, 512], F32)
        psP_f = pool.ps("c_pP", [128, 512], F32)
        psA_f = pool.ps("c_pA", [128, 512], F32)
        psF2 = [pool.ps("c_pF0", [128, 512], F32), pool.ps("c_pF1", [128, 512], F32)]
        psC2 = [pool.ps("c_pC0", [128, 512], F32), pool.ps("c_pC1", [128, 512], F32)]
        NCOL = 1056
        P.dma("gpsimd", wm[:, :, 0:NCOL], g.w_C[l].rearrange("(kc p) n -> p kc n", p=128))
        P.dma("sync", gw2[:], g.gw2P[l].rearrange("d r n -> r d n"))
        P.dma("sync", gb[:], g.gbP[l:l + 1])
        def chain(d):
            pb = 64 * d

            def tsb(name, shape, dt):
                if shape[0] == 64:
                    return H(pool.sb(name, [128] + list(shape[1:]), dt), pb)
                return pool.sb(name, shape, dt)
            tri = H(g.triD, pb)
            psF, psC = psF2[d], psC2[d]
            psTM, psP, psA = H(psTM_f, pb), H(psP_f, pb), H(psA_f, pb)
            pQ = psF[:, 0:128].rearrange("p (c t) -> p c t", c=2)
            pK = psF[:, 128:256].rearrange("p (c t) -> p c t", c=2)
            pXG = psF[0:16, 256:320]
            pPre = psP[:, 0:256]
            pSuf = psP[:, 256:512]
            pCum = psC[:, 0:128].rearrange("p (c t) -> p c t", c=2)
            pS = psC[:, 128:256].rearrange("p (c t) -> p c t", c=2)
            pAT = psA[:, 0:256].rearrange("p (h t) -> p h t", h=4)
            pO = psA[:, 256:512]
            xg_s = tsb("c_xg", [16, 64], F32)
            ktm_s = tsb("c_ktm", [64, 256], F32)
            qk_s = tsb("c_qks", [128, 256], F32)
            e1 = tsb("c_e1", [64, 256], F32)
            Lg = tsb("c_Lg", [64, 256], F32)
            cum = tsb("c_cum", [128, 2, 64], F32)
            dif = tsb("c_dif", [128, 2, 64], F32)
            Eq = tsb("c_Eq", [128, 2, 64], F32)
            Ek = tsb("c_Ek", [128, 2, 64], F32)
            Ein = tsb("c_Ein", [128, 2, 64], F32)
            Es = tsb("c_Es", [64, 256], F32)
            qx = tsb("c_qx", [128, 2, 64], BF16)
            kx = tsb("c_kx", [128, 2, 64], BF16)
            qin = tsb("c_qin", [128, 2, 64], BF16)
            ATm = tsb("c_ATm", [64, 4, 64], BF16)
            kst = tsb("c_kst", [64, 256], BF16)
            vs = tsb("c_vs", [64, 256], BF16)
            Sst = tsb("c_S", [128, 2, 64], F32)
            Sbf = tsb("c_Sbf", [128, 2, 64], BF16)
            osb = tsb("c_osb", [64, 256], F32)
            incl = tri[:, 2 * d, :]
            after = tri[:, 2 * d + 1, :]
            last = 63 if d == 0 else 0
            ref = 32 if d == 0 else 31
            V(P, lambda e: e.memset(Sst[:], 0.0), [], [Sst])
            V(P, lambda e: e.memset(Sbf[:], 0.0), [], [Sbf])
            for n in chunk_order(d):
                ts = slice(n * 64, (n + 1) * 64)
                for kc in range(8):
                    MM(P, psTM[:], g.hT[:, kc, ts], wm[:, kc, 256:768], start=(kc == 0), stop=(kc == 7), reads=[wm, g.hT])
                for c in range(2):
                    for kc in range(8):
                        MM(P, pQ[:, c, :], wm[:, kc, c * 128:(c + 1) * 128], g.hT[:, kc, ts], start=(kc == 0), stop=(kc == 7),
                           reads=[wm, g.hT], writes=[psF])
                for c in range(2):
                    for kc in range(8):
                        MM(P, pK[:, c, :], wm[:, kc, 256 + c * 128:256 + (c + 1) * 128], g.hT[:, kc, ts], start=(kc == 0),
                           stop=(kc == 7), reads=[wm, g.hT], writes=[psF])
                for kc in range(8):
                    MM(P, pXG, wm[:, kc, 1024 + 16 * d:1040 + 16 * d], g.hT[:, kc, ts], start=(kc == 0), stop=(kc == 7),
                       reads=[wm, g.hT], writes=[psF])
                S(P, lambda e: e.activation(out=xg_s[:], in_=pXG, func=AF.Copy), [psF], [xg_s])
                S(P, lambda e: e.activation(out=qk_s[:], in_=psF[:, 0:256], func=AF.Copy), [psF], [qk_s])
                S(P, lambda e: e.activation(out=ktm_s[:], in_=psTM[:, 0:256], func=AF.Copy), [psTM], [ktm_s])
                S(P, lambda e: e.activation(out=vs[:], in_=psTM[:, 256:512], func=AF.Copy), [psTM], [vs])
                yield
                MM(P, pPre, xg_s[:], gw2[:, d, :], start=True, stop=False, reads=[xg_s, gw2], writes=[psP])
                MM(P, pPre, g.ones1[:], gb[:, d, :], start=False, stop=True, reads=[g.ones1, gb], writes=[psP])
                S(P, lambda e: e.activation(out=e1[:], in_=pPre, func=AF.Exp, scale=-1.0), [psP], [e1])
                S(P, lambda e: e.activation(out=Lg[:], in_=e1[:], func=AF.Ln, bias=1.0), [e1], [Lg])
                yield
                for c in range(2):
                    MM(P, pCum[:, c, :], Lg[:, c * 128:(c + 1) * 128], incl, reads=[Lg, g.tri], writes=[psC])
                MM(P, pSuf, after, Lg[:], reads=[Lg, g.tri], writes=[psP])
                S(P, lambda e: e.activation(out=cum[:], in_=pCum, func=AF.Copy), [psC], [cum])
                S(P, lambda e: e.activation(out=Es[:], in_=pSuf, func=AF.Exp, scale=-1.0 / 16), [psP], [Es])
                for c in range(2):
                    V(P, lambda e, c=c: e.tensor_scalar(out=dif[:, c, :], in0=cum[:, c, :], scalar1=cum[:, c, ref:ref + 1],
                                                        scalar2=None, op0=ALU.subtract), [cum], [dif])
                S(P, lambda e: e.activation(out=Eq[:], in_=dif[:], func=AF.Exp, scale=-1.0 / 16), [dif], [Eq])
                S(P, lambda e: e.activation(out=Ek[:], in_=dif[:], func=AF.Exp, scale=1.0 / 16), [dif], [Ek])
                S(P, lambda e: e.activation(out=Ein[:], in_=cum[:], func=AF.Exp, scale=-1.0 / 16), [cum], [Ein])
                V(P, lambda e: e.scalar_tensor_tensor(out=qx[:], in0=qk_s[:, 0:128].rearrange("p (c t) -> p c t", c=2), scalar=SC, in1=Eq[:], op0=ALU.mult, op1=ALU.mult),
                  [qk_s, Eq], [qx])
                V(P, lambda e: e.scalar_tensor_tensor(out=qin[:], in0=qk_s[:, 0:128].rearrange("p (c t) -> p c t", c=2), scalar=SC, in1=Ein[:], op0=ALU.mult, op1=ALU.mult),
                  [qk_s, Ein], [qin])
                V(P, lambda e: e.tensor_tensor(out=kx[:], in0=qk_s[:, 128:256].rearrange("p (c t) -> p c t", c=2), in1=Ek[:], op=ALU.mult), [qk_s, Ek], [kx])
                yield
                for h in (0, 2, 1, 3):
                    c, hb = h // 2, (h % 2) * 64
                    MM(P, pAT[:, h, :], kx[hb:hb + 64, c, :], qx[hb:hb + 64, c, :], reads=[kx, qx], writes=[psA])
                V(P, lambda e: e.tensor_tensor(out=ATm[:], in0=pAT, in1=incl.unsqueeze(1).broadcast_to([64, 4, 64]),
                                               op=ALU.mult), [psA, g.tri], [ATm])
                V(P, lambda e: e.tensor_tensor(out=kst[:], in0=ktm_s[:], in1=Es[:], op=ALU.mult), [ktm_s, Es], [kst])
                yield
                for h in (0, 2, 1, 3):
                    c, hb = h // 2, (h % 2) * 64
                    hs = slice(h * 64, (h + 1) * 64)
                    MM(P, pO[:, hs], qin[hb:hb + 64, c, :], Sbf[hb:hb + 64, c, :], start=True, stop=False,
                       reads=[qin, Sbf], writes=[psA])
                    MM(P, pO[:, hs], ATm[:, h, :], vs[:, hs], start=False, stop=True, reads=[ATm, vs], writes=[psA])
                for h in (0, 2, 1, 3):
                    c, hb = h // 2, (h % 2) * 64
                    hs = slice(h * 64, (h + 1) * 64)
                    MM(P, pS[hb:hb + 64, c, :], kst[:, hs], vs[:, hs], reads=[kst, vs], writes=[psC])
                for c in range(2):
                    V(P, lambda e, c=c: e.scalar_tensor_tensor(out=Sst[:, c, :], in0=Sst[:, c, :], scalar=Ein[:, c, last:last + 1],
                                                               in1=pS[:, c, :], op0=ALU.mult, op1=ALU.add),
                      [Sst, Ein, psC], [Sst])
                S(P, lambda e: e.activation(out=Sbf[:], in_=Sst[:], func=AF.Copy), [Sst], [Sbf])
                S(P, lambda e: e.activation(out=osb[:], in_=pO, func=AF.Copy), [psA], [osb])
                P.dma("sync", g.ofw2[d][ts, 0:256], osb[:], reads=[osb], writes=[("ofw", d)])

        gens = [chain(0), chain(1)]
        if 'C' in os.environ.get('KSEQ', ''):
            for gen in gens:
                for _ in gen:
                    pass
            gens = []
        while gens:
            for gen in list(gens):
                try:
                    next(gen)
                except StopIteration:
                    gens.remove(gen)
        P.barrier()
    with Pool(nc) as pool:
        ngbc = pool.sb("c_ng", [64, 256], F32)
        psG_2 = [pool.ps("c_pG_a", [64, 512], F32), pool.ps("c_pG_b", [64, 512], F32)]
        pt = pool.ps("c_pt", [128, 4, 128], F32)
        P.dma("sync", ngbc[:], g.ngC[l:l + 1, :].broadcast_to([64, 256]))
        osb_2 = [pool.sb("c_fo_a", [64, 256], F32), pool.sb("c_fo_b", [64, 256], F32)]
        ofl_2 = [pool.sb("c_fl_a", [64, 256], F32), pool.sb("c_fl_b", [64, 256], F32)]
        ysq_2 = [pool.sb("c_fysq_a", [64, 256], F32), pool.sb("c_fysq_b", [64, 256], F32)]
        ss4_2 = [pool.sb("c_fss4_a", [64, 4], F32), pool.sb("c_fss4_b", [64, 4], F32)]
        sg_2 = [pool.sb("c_fsg_a", [64, 256], F32), pool.sb("c_fsg_b", [64, 256], F32)]
        yTs_2 = [pool.sb("c_fyTs_a", [128, 2, 64], BF16), pool.sb("c_fyTs_b", [128, 2, 64], BF16)]
        for n in range(68):
            osb, ofl, ysq, ss4, sg, yTs, psG = osb_2[n % 2], ofl_2[n % 2], ysq_2[n % 2], ss4_2[n % 2], sg_2[n % 2], yTs_2[n % 2], psG_2[n % 2]
            pG = psG[:, 0:256]
            ts = slice(n * 64, (n + 1) * 64)
            P.dma("sync", osb[:], g.ofw2[0][ts, 0:256], reads=[], writes=[osb])
            P.dma("gpsimd", ofl[:], g.ofw2[1][ts, 0:256], reads=[], writes=[ofl])
            for kc in range(8):
                MM(P, pG, g.hT[:, kc, ts], wm[:, kc, 768:1024], start=(kc == 0), stop=(kc == 7), reads=[wm, g.hT], writes=[psG])
            V(P, lambda e: e.tensor_tensor(out=osb[:], in0=osb[:], in1=ofl[:], op=ALU.add), [osb, ofl], [osb])
            rms_gate_finish(g, osb, ysq, ss4, ngbc, pG, psG, sg, pt, yTs, 2, ts, 1e-6)
    P.barrier()


def rms_gate_finish(g, y, ysq, ss4, ngbc, pG, pGkey, sg, pt, yTs, mi, ts, eps):
    P = g.P
    V(P, lambda e: e.tensor_tensor(out=ysq[:], in0=y[:], in1=y[:], op=ALU.mult), [y], [ysq])
    V(P, lambda e: e.tensor_reduce(out=ss4[:], in_=ysq[:].rearrange("p (h d) -> p h d", h=4), axis=AX.X, op=ALU.add),
      [ysq], [ss4])
    S(P, lambda e: e.activation(out=ss4[:], in_=ss4[:], func=AF.Sqrt, bias=eps, scale=1.0 / 64), [ss4], [ss4])
    V(P, lambda e: e.reciprocal(out=ss4[:], in_=ss4[:]), [ss4], [ss4])
    V(P, lambda e: e.tensor_tensor(out=ysq[:].rearrange("p (h d) -> p h d", h=4), in0=y[:].rearrange("p (h d) -> p h d", h=4),
                                   in1=ss4[:].unsqueeze(2).broadcast_to([64, 4, 64]), op=ALU.mult), [y, ss4], [ysq])
    V(P, lambda e: e.tensor_tensor(out=ysq[:], in0=ysq[:], in1=ngbc[:], op=ALU.mult), [ysq, ngbc], [ysq])
    if pG is not None:
        S(P, lambda e: e.activation(out=sg[:], in_=pG, func=AF.Silu), [pGkey], [sg])
    V(P, lambda e: e.tensor_tensor(out=ysq[:], in0=ysq[:], in1=sg[:], op=ALU.mult), [ysq, sg], [ysq])
    for c2 in range(2):
        TR(P, pt[:, c2, 0:64], ysq[:, c2 * 128:(c2 + 1) * 128], g.ident[0:64, 0:64], reads=[ysq, g.ident], writes=[pt])
    S(P, lambda e: e.activation(out=yTs[:], in_=pt[:, 0:2, 0:64], func=AF.Copy), [pt], [yTs])
    P.dma("sync", g.yT[mi][:, :, ts], yTs[:], reads=[yTs], writes=[("yT", mi)])


def phase3_merge(g, l, xsrc, with_ctx):
    nc, P = g.nc, g.P
    with Pool(nc) as pool:
        wg = pool.sb("wg", [128, 8, 4096], BF16)
        wbr = pool.sb("wbr", [128, 8, 1024], BF16)
        wo = pool.sb("wo", [128, 8, 1024], BF16)
        yb = pool.sb("p3y", [128, 4, 2, 256], BF16)
        gts = [pool.sb("p3g0", [128, 256], F32), pool.sb("p3g1", [128, 256], F32)]
        tmps = [pool.sb("p3t0", [128, 256], F32), pool.sb("p3t1", [128, 256], F32)]
        accf = pool.sb("p3acc", [128, 8, 256], F32)
        accb = pool.sb("p3accb", [128, 8, 256], BF16)
        xt = pool.sb("p3x", [128, 1024], F32)
        ot = g.xn
        ss2 = pool.sb("p3ss", [128, 2], F32)
        pgs = [pool.ps("p3pg0", [128, 256], F32), pool.ps("p3pg1", [128, 256], F32)]
        pzs = [pool.ps("p3pz0", [128, 256], F32), pool.ps("p3pz1", [128, 256], F32)]
        pout = pool.ps("p3po", [128, 2, 512], F32)
        for kc in range(8):
            P.dma("gpsimd", wg[:, kc, :], g.w_in[l, kc * 128:(kc + 1) * 128, 3312:7408])
        P.dma("gpsimd", wbr[:], g.w_branch[l].rearrange("i (c p) n -> p (i c) n", p=128))
        P.dma("gpsimd", wo[:], g.w_out[l].rearrange("(kc p) n -> p kc n", p=128))
        for grp in range(NG):
            if grp == 0 and not with_ctx:
                continue
            j = 1 if grp == 0 else 0
            ts = slice(grp * 256, (grp + 1) * 256)
            for i in range(4):
                P.dma("sync" if i % 2 == 0 else "gpsimd", yb[:, i, :, :], g.yT[i][:, :, ts], reads=[("yT", i)], writes=[yb])
            for i in range(4):
                for fc in range(8):
                    pg, pz, gt, tmp = pgs[fc % 2], pzs[fc % 2], gts[fc % 2], tmps[fc % 2]
                    for kc in range(8):
                        MM(P, pg[:], wg[:, kc, i * 1024 + fc * 128:i * 1024 + (fc + 1) * 128], g.hT[:, kc, ts],
                           start=(kc == 0), stop=(kc == 7), reads=[wg, g.hT])
                    for c2 in range(2):
                        MM(P, pz[:], wbr[:, i * 2 + c2, fc * 128:(fc + 1) * 128], yb[:, i, c2, :], start=(c2 == 0),
                           stop=(c2 == 1), reads=[wbr, yb])
                    S(P, lambda e, i=i, fc=fc: e.activation(out=gt[:], in_=pg[:], func=AF.Sigmoid,
                                                            bias=g.gbT[:, l, i, fc:fc + 1]), [pg, g.gbT], [gt])
                    if i == 0:
                        V(P, lambda e, fc=fc: e.tensor_tensor(out=accf[:, fc, :], in0=gt[:], in1=pz[:], op=ALU.mult),
                          [gt, pz], [("accf", fc)])
                    else:
                        V(P, lambda e: e.tensor_tensor(out=tmp[:], in0=gt[:], in1=pz[:], op=ALU.mult), [gt, pz], [tmp])
                        V(P, lambda e, fc=fc, i=i: e.tensor_tensor(out=(accb if i == 3 else accf)[:, fc, :],
                                                                   in0=accf[:, fc, :], in1=tmp[:], op=ALU.add),
                          [("accf", fc), tmp], [("accf", fc), accb] if i == 3 else [("accf", fc)])
            for tt in range(2):
                ti = grp * 2 + tt
                P.dma("sync", xt[:], xsrc[ti * 128:(ti + 1) * 128, :], reads=[("x", l)], writes=[xt])
                for half in range(2):
                    if half == 0:
                        V(P, lambda e: e.memset(ss2[:], 0.0), [], [ss2])
                    for fc in range(8):
                        MM(P, pout[:, half, :], accb[:, fc, tt * 128:(tt + 1) * 128], wo[:, fc, half * 512:(half + 1) * 512],
                           start=(fc == 0), stop=(fc == 7), reads=[accb, wo], writes=[("pout", half)])
                    S(P, lambda e, half=half: e.activation(out=ot[:, half * 512:(half + 1) * 512], in_=pout[:, half, :],
                                                           func=AF.Square, accum_out=ss2[:, half:half + 1]),
                      [("pout", half)], [ot, ss2])
                residual_update(g, pout, "pout", ss2, xt, g.Gbc, j, ot)
                P.dma("sync", g.xmid[ti * 128:(ti + 1) * 128, :], xt[:], reads=[xt], writes=[("xmid", l)])
    P.barrier()


def residual_update(g, pout, pkey, ss2, xt, Gbc, j, ot):
    P = g.P
    rstd = g.rstd2
    V(P, lambda e: e.tensor_tensor(out=rstd[:], in0=ss2[:, 0:1], in1=ss2[:, 1:2], op=ALU.add), [ss2], [rstd])
    S(P, lambda e: e.activation(out=rstd[:], in_=rstd[:], func=AF.Sqrt, bias=1e-6, scale=1.0 / 1024), [rstd], [rstd])
    V(P, lambda e: e.reciprocal(out=rstd[:], in_=rstd[:]), [rstd], [rstd])
    for half in range(2):
        hs = slice(half * 512, (half + 1) * 512)
        V(P, lambda e, half=half, hs=hs: e.scalar_tensor_tensor(out=ot[:, hs], in0=pout[:, half, :], scalar=rstd[:, 0:1],
                                                               in1=Gbc[:, j, hs], op0=ALU.mult, op1=ALU.mult),
          [(pkey, half), rstd, Gbc, ot], [ot])
    V(P, lambda e: e.tensor_tensor(out=xt[:], in0=xt[:], in1=ot[:], op=ALU.add), [xt, ot], [xt])


def phase4_ffn(g, l, xdst_fn, with_ctx):
    nc, P = g.nc, g.P
    with Pool(nc) as pool:
        g.tp = [pool.ps("tpa", [128, 4, 128], F32), pool.ps("tpb", [128, 4, 128], F32)]
        w1 = pool.sb("w1", [128, 8, 2 * FH], BF16)
        w2 = pool.sb("w2", [128, 22, 1024], BF16)
        xa = pool.sb("p4x0", [128, 1024], F32)
        xb = pool.sb("p4x1", [128, 1024], F32)
        h2T = pool.sb("p4h", [128, 8, 256], BF16)
        uT = pool.sb("p4u", [128, 22, 256], BF16)
        sgs = [pool.sb("p4s0", [128, 256], F32), pool.sb("p4s1", [128, 256], F32)]
        ot = pool.sb("p4o", [128, 1024], F32)
        ss2 = pool.sb("p4ss", [128, 2], F32)
        pgs = [pool.ps("p4pg0", [128, 256], F32), pool.ps("p4pg1", [128, 256], F32)]
        pus = [pool.ps("p4pu0", [128, 256], F32), pool.ps("p4pu1", [128, 256], F32)]
        pout = pool.ps("p4po", [128, 2, 512], F32)
        for kc in range(8):
            P.dma("gpsimd", w1[:, kc, :], g.ffn_w1[l, kc * 128:(kc + 1) * 128, :], writes=[("w1", kc)])
        P.dma("gpsimd", w2[:, 0:11, :], g.ffn_w2[l, 0:1408, :].rearrange("(c p) n -> p c n", p=128), writes=[("w2", 0)])
        P.dma("gpsimd", w2[:, 11:22, :], g.ffn_w2[l, 1408:2816, :].rearrange("(c p) n -> p c n", p=128), writes=[("w2", 1)])
        w1k = [("w1", kc) for kc in range(8)]
        w2k = [("w2", 0), ("w2", 1)]
        xt2 = [xa, xb]
        for grp in range(NG):
            if grp == 0 and not with_ctx:
                continue
            j = 1 if grp == 0 else 0
            for tt in range(2):
                ti = grp * 2 + tt
                P.dma("sync", xt2[tt][:], g.xmid[ti * 128:(ti + 1) * 128, :], reads=[("xmid", l)], writes=[xt2[tt]])
                norm_transpose(g, xt2[tt], j, g.gain2, g.modT[:, 24:32, :], h2T, tt * 128, "p4")
            for hc in range(22):
                pg, pu, sg = pgs[hc % 2], pus[hc % 2], sgs[hc % 2]
                for kc in range(8):
                    MM(P, pg[:], w1[:, kc, hc * 128:(hc + 1) * 128], h2T[:, kc, :], start=(kc == 0), stop=(kc == 7),
                       reads=w1k + [h2T])
                for kc in range(8):
                    MM(P, pu[:], w1[:, kc, FH + hc * 128:FH + (hc + 1) * 128], h2T[:, kc, :], start=(kc == 0), stop=(kc == 7),
                       reads=w1k + [h2T])
                S(P, lambda e: e.activation(out=sg[:], in_=pg[:], func=AF.Silu), [pg], [sg])
                V(P, lambda e, hc=hc: e.tensor_tensor(out=uT[:, hc, :], in0=sg[:], in1=pu[:], op=ALU.mult), [sg, pu], [uT])
            for tt in range(2):
                ti = grp * 2 + tt
                xt = xt2[tt]
                for half in range(2):
                    if half == 0:
                        V(P, lambda e: e.memset(ss2[:], 0.0), [], [ss2])
                    for hc in range(22):
                        MM(P, pout[:, half, :], uT[:, hc, tt * 128:(tt + 1) * 128], w2[:, hc, half * 512:(half + 1) * 512],
                           start=(hc == 0), stop=(hc == 21), reads=w2k + [uT], writes=[("pout4", half)])
                    S(P, lambda e, half=half: e.activation(out=ot[:, half * 512:(half + 1) * 512], in_=pout[:, half, :],
                                                           func=AF.Square, accum_out=ss2[:, half:half + 1]),
                      [("pout4", half)], [ot, ss2])
                residual_update(g, pout, "pout4", ss2, xt, g.Gbc, j, ot)
                dst, dkey = xdst_fn(ti)
                if dst is not None:
                    P.dma("sync", dst, xt[:], reads=[xt], writes=[dkey])
    P.barrier()


def build_program(debug=False):
    nc = bass.Bass("TRN2", target_bir_lowering=False)
    g = Ctx()
    g.nc = nc
    g.P = P = Prog(nc)
    g.debug = debug

    def din(name, shape, dt=F32):
        return nc.dram_tensor(name, list(shape), dt, kind="ExternalInput").ap()

    g.xin = din("xin", [T, D])
    g.cvec = din("cvec", [128, 8, 2])
    g.ada_w = din("ada_w", [L, D, 6144])
    g.ada_b = din("ada_b", [L, 6144])
    g.norm_g = din("norm_g", [L, 4, D])
    g.ngT_d = din("ngT", [128, L, 4, 8])
    g.gbT_d = din("gbT", [128, L, 4, 8])
    g.w_in = din("w_in", [L, D, 7408])
    g.w_D = din("w_D", [L, D, 896])
    g.w_branch = din("w_branch", [L, 4, 256, D])
    g.w_out = din("w_out", [L, D, D])
    g.ffn_w1 = din("ffn_w1", [L, D, 2 * FH])
    g.ffn_w2 = din("ffn_w2", [L, FH, D])
    g.attn_sink = din("attn_sink", [L, 4])
    g.w_C = din("w_C", [L, D, 1056])
    g.gw2P = din("gw2P", [L, 2, 16, 256])
    g.gbP = din("gbP", [L, 2, 256])
    g.ngC = din("ngC", [L, 256])
    g.tri_d = din("tri", [64, 4, 64])
    g.convT = din("convT", [128, L, 6, 7])
    g.muT = din("muT", [128, L, 8])
    g.kkwT = din("kkwT", [128, L, 2])
    g.kaT = din("kaT", [128, L, 2])
    g.a0T = din("a0T", [128, L, 2, 2])
    g.rwkv_w2 = din("rwkv_w2", [L, 2, 32, 256])
    g.rwkv_a2 = din("rwkv_a2", [L, 2, 32, 256])
    g.rwkv_w0 = din("rwkv_w0", [L, 2, 256])
    g.rwkv_a0 = din("rwkv_a0", [L, 2, 256])
    g.rwkv_g2 = din("rwkv_g2", [L, 64, 256])
    g.rwkv_ka = din("rwkv_ka", [L, 256])
    g.rwkv_rk = din("rwkv_rk", [L, 256])
    g.rwkv_ln_g = din("rwkv_ln_g", [L, 256])
    g.rwkv_ln_b = din("rwkv_ln_b", [L, 256])
    g.blk_d = din("blk", [128, 128])
    g.gdn_dt_bias = din("gdn_dt_bias", [L, 2, 4])
    g.gdn_a_log = din("gdn_a_log", [L, 2, 4])
    g.ngB = din("ngB", [L, 256])
    g.ident_d = din("ident", [128, 128])
    g.ropec = din("ropec", [128, T])
    g.ropes = din("ropes", [128, T])
    g.maskP_d = din("maskP", [128, 128])
    g.maskN_d = din("maskN", [128, 128])
    if debug:
        g.ydbg = din("ydbg", [3, 128, 2, T], BF16)
    out = nc.dram_tensor("out", [4096, D], F32, kind="ExternalOutput").ap()

    g.modD = [nc.dram_tensor("modD%d" % l, [2, 6144], F32).ap() for l in range(L)]
    dk = dict(kind="ExternalOutput") if debug else {}
    g.xs = nc.dram_tensor("xs", [T, D], F32, **dk).ap()
    g.xmid = nc.dram_tensor("xmid", [T, D], F32, **dk).ap()
    g.yT = [nc.dram_tensor("yT%d" % i, [128, 2, T], BF16, **dk).ap() for i in range(4)]
    g.ofw2 = [nc.dram_tensor("ofw%d" % i, [T, 512], F32).ap() for i in range(2)]
    g.ofwB = [nc.dram_tensor("ofwB%d" % i, [T, 256], F32).ap() for i in range(2)]
    g.bqk = nc.dram_tensor("bqk", [4, 128, T], BF16).ap()
    g.aF = nc.dram_tensor("aF", [6, 128, T], F32).ap()
    g.aTM = nc.dram_tensor("aTM", [T, 1024], F32).ap()
    g.asm = nc.dram_tensor("asm", [128, T], F32).ap()
    g.asg = nc.dram_tensor("asg", [64, T], F32).ap()
    g.bkv = nc.dram_tensor("bkv", [T, 512], F32).ap()
    g.bgate = nc.dram_tensor("bgate", [T, 256], F32).ap()
    g.bba = nc.dram_tensor("bba", [T, 16], F32).ap()

    A = nc.alloc_sbuf_tensor
    g.ident = A("ident_s", [128, 128], F32)
    g.maskP = A("maskP_s", [128, 128], BF16)
    g.maskN = A("maskN_s", [128, 128], BF16)
    g.sc = A("sc", [128, 8, 2], F32)
    g.tri = A("tri_s", [64, 4, 64], F32)
    g.triD = A("triD_s", [128, 4, 64], F32)
    g.identD = A("identD_s", [128, 64], F32)
    g.ones128 = A("ones128", [128, 128], F32)
    g.ones1 = A("ones1", [1, 64], F32)
    g.ones64 = A("ones64", [64, 128], F32)
    g.blk = A("blk_s", [128, 128], F32)
    g.modT = A("modT", [128, 48, 2], F32)
    g.ngT = A("ngT_s", [128, L, 4, 8], F32)
    g.gbT = A("gbT_s", [128, L, 4, 8], F32)
    g.gain1 = A("gain1", [128, 8, 2], F32)
    g.gain2 = A("gain2", [128, 8, 2], F32)
    g.Gbc = A("Gbc", [128, 2, 1024], F32)
    g.xn = A("xn", [128, 1024], F32)
    g.ss = A("ss", [128, 1], F32)
    g.rstd = A("rstd", [128, 1], F32)
    g.rstd2 = A("rstd2", [128, 1], F32)

    P.dma("sync", g.ident[:], g.ident_d)
    P.dma("gpsimd", g.maskP[:], g.maskP_d)
    P.dma("gpsimd", g.maskN[:], g.maskN_d)
    P.dma("sync", g.sc[:], g.cvec)
    P.dma("sync", g.tri[:], g.tri_d)
    P.dma("sync", g.triD[0:64], g.tri_d)
    P.dma("sync", g.triD[64:128], g.tri_d)
    P.dma("sync", g.identD[0:64], g.ident_d[0:64, 0:64])
    P.dma("sync", g.identD[64:128], g.ident_d[0:64, 0:64])
    V(P, lambda e: e.memset(g.ones128[:], 1.0), [], [g.ones128])
    V(P, lambda e: e.memset(g.ones1[:], 1.0), [], [g.ones1])
    V(P, lambda e: e.memset(g.ones64[:], 1.0), [], [g.ones64])
    P.dma("sync", g.blk[:], g.blk_d)
    P.dma("sync", g.ngT[:], g.ngT_d)
    P.dma("sync", g.gbT[:], g.gbT_d)
    S(P, lambda e: e.activation(out=g.sc[:], in_=g.sc[:], func=AF.Silu), [g.sc], [g.sc])

    for l in range(1 if debug else L):
        last = l == L - 1
        with_ctx = not last
        xsrc = g.xin if l == 0 else g.xs
        phase0_mod(g, l)
        with Pool(nc) as pool:
            hT = pool.sb("hT", [128, 8, T], BF16)
            g.hT = hT
            phase1_h(g, l, xsrc)
            with Pool(nc) as pool:
                wmix = pool.sb("wmix", [128, 8, 1088], BF16)
                g.wmix = wmix
                mixer_D(g, l, with_ctx)
                mixer_C(g, l)
                mixer_B_pre(g, l)
                mixer_A_pre(g, l)
            mixers_AB_sweeps(g, l)
            mixer_B_finish(g, l)
            mixer_A_finish(g, l)
            load_gbc(g, l, 2, 1)
            phase3_merge(g, l, xsrc, with_ctx)
        P.emit()
        if last:
            def xdst(ti):
                if ti < 2:
                    return None, None
                return out[(ti - 2) * 128:(ti - 1) * 128, :], "out"
        else:
            def xdst(ti):
                return g.xs[ti * 128:(ti + 1) * 128, :], ("x", l + 1)
        load_gbc(g, l, 5, 3)
        phase4_ffn(g, l, xdst, with_ctx)
        P.emit()
    P.barrier()
    P.emit()
    P.close()
    return nc


def _consts():
    ident = np.eye(128, dtype=np.float32)
    kk = np.arange(128)[:, None]
    qq = np.arange(128)[None, :]
    maskP = (kk >= qq).astype(np.float32)
    maskN = (kk <= qq).astype(np.float32)
    t = np.arange(4096)
    rows = (t // 64).astype(np.float32)
    cols = (t % 64).astype(np.float32)
    inv = (10000.0 ** (-np.arange(16, dtype=np.float32) / 16)).astype(np.float32)
    cosT = np.ones((64, T), np.float32)
    sinT = np.zeros((64, T), np.float32)
    for d in range(64):
        pos = rows if d < 32 else cols
        f = inv[d % 16]
        ang = (pos * f).astype(np.float32)
        cosT[d, 256:] = np.cos(ang)
        sgn = -1.0 if (d % 32) < 16 else 1.0
        sinT[d, 256:] = sgn * np.sin(ang)
    ropec = np.concatenate([cosT, cosT], 0)
    ropes = np.concatenate([sinT, sinT], 0)
    jj = np.arange(64)[:, None]
    ii = np.arange(64)[None, :]
    tri = np.stack([(jj <= ii), (jj > ii), (jj >= ii), (jj < ii)], 1).astype(np.float32)
    blk = np.kron(np.eye(2, dtype=np.float32), np.ones((64, 64), np.float32))
    return dict(blk=blk, tri=np.ascontiguousarray(tri), ident=ident, maskP=maskP, maskN=maskN, ropec=np.ascontiguousarray(ropec), ropes=np.ascontiguousarray(ropes))


def _layout_weights(inp):
    w_in = inp["w_in"]
    o = {}
    d0 = 960 + 1040 + 800
    q = w_in[:, :, d0:d0 + 256].reshape(L, D, 4, 64)
    k = w_in[:, :, d0 + 256:d0 + 384].reshape(L, D, 2, 64)
    v = w_in[:, :, d0 + 384:d0 + 512]
    perm = np.array([(d + 16) if (d % 32) < 16 else (d - 16) for d in range(64)])
    qa = np.concatenate([q[:, :, 0], q[:, :, 2]], -1)
    qb = np.concatenate([q[:, :, 1], q[:, :, 3]], -1)
    kk = k.reshape(L, D, 128)
    qr = q[..., perm]
    kr = k[..., perm]
    qra = np.concatenate([qr[:, :, 0], qr[:, :, 2]], -1)
    qrb = np.concatenate([qr[:, :, 1], qr[:, :, 3]], -1)
    krr = kr.reshape(L, D, 128)
    c0 = 960 + 1040
    Z = np.zeros((L, D, 4, 32), np.float32)
    qc = w_in[:, :, c0:c0 + 128].reshape(L, D, 4, 32)
    kc_ = w_in[:, :, c0 + 128:c0 + 256].reshape(L, D, 4, 32)
    qP = np.concatenate([qc, Z], -1).reshape(L, D, 256)
    kP = np.concatenate([kc_, Z], -1).reshape(L, D, 256)
    o["w_C"] = np.ascontiguousarray(np.concatenate([qP, kP, w_in[:, :, c0 + 256:c0 + 512], w_in[:, :, c0 + 544:c0 + 800],
                                                    w_in[:, :, c0 + 512:c0 + 544]], -1))
    gw2 = inp["gla_gw2"].reshape(L, 2, 16, 4, 32)
    o["gw2P"] = np.ascontiguousarray(np.concatenate([gw2, np.zeros_like(gw2)], -1).reshape(L, 2, 16, 256))
    gbb = inp["gla_gb"].reshape(L, 2, 4, 32)
    o["gbP"] = np.ascontiguousarray(np.concatenate([gbb, np.zeros_like(gbb)], -1).reshape(L, 2, 256))
    o["convT"] = np.ascontiguousarray(inp["gdn_conv"].reshape(L, 7, 6, 128).transpose(3, 0, 2, 1))
    o["ngB"] = np.ascontiguousarray(np.tile(inp["gdn_norm_g"], (1, 4)))
    o["gdn_dt_bias"] = inp["gdn_dt_bias"]
    o["gdn_a_log"] = inp["gdn_a_log"]
    mu = np.zeros((L, 1024), np.float32)
    mu[:, :960] = inp["rwkv_mu"]
    o["muT"] = np.ascontiguousarray(mu.reshape(L, 8, 128).transpose(2, 0, 1))
    o["kkwT"] = np.ascontiguousarray(inp["rwkv_kk"].reshape(L, 2, 128).transpose(2, 0, 1))
    o["kaT"] = np.ascontiguousarray(inp["rwkv_ka"].reshape(L, 2, 128).transpose(2, 0, 1))
    o["a0T"] = np.ascontiguousarray(inp["rwkv_a0"].reshape(L, 2, 2, 128).transpose(3, 0, 1, 2))
    for nm in ("rwkv_w2", "rwkv_a2", "rwkv_w0", "rwkv_a0", "rwkv_g2", "rwkv_ka", "rwkv_ln_g", "rwkv_ln_b"):
        o[nm] = inp[nm]
    o["rwkv_rk"] = np.ascontiguousarray(inp["rwkv_rk"].reshape(L, 256))
    o["ngC"] = np.ascontiguousarray(np.tile(inp["gla_norm_g"], (1, 4)))
    o["w_D"] = np.ascontiguousarray(np.concatenate([qa, qb, kk, qra, qrb, krr, v], -1))
    return o


_NC_CACHE = {}


def kernel(**inp):
    inp = {k: np.asarray(v) for k, v in inp.items()}
    debug = bool(int(os.environ.get("KDEBUG", "0")))
    if debug not in _NC_CACHE:
        _NC_CACHE[debug] = build_program(debug)
    nc = _NC_CACHE[debug]
    cst = _consts()
    lw = _layout_weights(inp)
    ngT = np.ascontiguousarray(inp["norm_g"].reshape(L, 4, 8, 128).transpose(3, 0, 1, 2))
    gbT = np.ascontiguousarray(inp["gate_b"].reshape(L, 4, 8, 128).transpose(3, 0, 1, 2))
    shared = dict(ada_w=inp["ada_w"], ada_b=inp["ada_b"], norm_g=inp["norm_g"], ngT=ngT, gbT=gbT, w_in=inp["w_in"],
                  w_branch=inp["w_branch"], w_out=inp["w_out"], ffn_w1=inp["ffn_w1"], ffn_w2=inp["ffn_w2"],
                  attn_sink=inp["attn_sink"], **cst, **lw)
    in_maps = []
    for b in range(8):
        m = dict(shared)
        m["xin"] = np.ascontiguousarray(np.concatenate([inp["ctx"][b], inp["x"][b]], 0))
        cv = np.stack([inp["c"][b], inp["c_ctx"]], -1)
        m["cvec"] = np.ascontiguousarray(cv.reshape(8, 128, 2).transpose(1, 0, 2))
        if debug:
            m["ydbg"] = inp["_ydbg"][b]
        in_maps.append(m)
    ncores = int(os.environ.get("KCORES", "8"))
    res = run_bass_kernel_spmd(nc, in_maps[:ncores], core_ids=list(range(ncores)))
    if debug:
        return res.results[0]
    outs = [res.results[b]["out"] for b in range(ncores)]
    while len(outs) < 8:
        outs.append(np.zeros_like(outs[0]))
    return np.stack(outs, 0).astype(np.float32)
```

```python
import os
from contextlib import ExitStack
import numpy as np
import concourse.bass as bass
import concourse.mybir as mybir
from concourse.bass_utils import run_bass_kernel_spmd

F32 = mybir.dt.float32
BF16 = mybir.dt.bfloat16
ALU = mybir.AluOpType
AF = mybir.ActivationFunctionType
AX = mybir.AxisListType

SEM_CAP = 30000
NDMA_SLOTS = 6
L = 2
D = 1024
T = 4352
NT = T // 128
NG = T // 256
FH = 2816


class _Rec:
    def __init__(self):
        self.call = None

    def __getattr__(self, name):
        def f(*a, **kw):
            self.call = (name, a, kw)
            return self
        return f


class Prog:
    ENGS = ("tensor", "vector", "scalar", "gpsimd", "sync")

    def __init__(self, nc):
        self.nc = nc
        self.ops = {e: [] for e in self.ENGS}
        self.cnt = {e: 0 for e in self.ENGS}
        self.seen = {e: {} for e in self.ENGS}
        self.res = {}
        self.pe_rt = {}
        self.sems = {}
        self.dma_n = {e: 0 for e in self.ENGS}
        self._stack = []
        self.n_inst = 0

    def sem(self, key):
        if key not in self.sems:
            cm = self.nc.semaphore("s_%s" % "_".join(str(k) for k in key))
            self.sems[key] = cm.__enter__()
            self._stack.append(cm)
        return self.sems[key]

    def _key(self, r):
        if isinstance(r, (tuple, str)):
            return r
        t = getattr(r, "tensor", r)
        return getattr(t, "name", None) or id(t)

    def _deps(self, reads, writes):
        need = {}
        for r in reads:
            w, _ = self.res.get(self._key(r), ({}, {}))
            for s, v in w.items():
                need[s] = max(need.get(s, 0), v)
        for r in writes:
            w, rd = self.res.get(self._key(r), ({}, {}))
            for d in (w, rd):
                for s, v in d.items():
                    need[s] = max(need.get(s, 0), v)
        return need

    def _commit(self, reads, writes, tok):
        s, v = tok
        for r in reads:
            w, rd = self.res.setdefault(self._key(r), ({}, {}))
            rd[s] = max(rd.get(s, 0), v)
        for r in writes:
            self.res[self._key(r)] = ({s: v}, {})

    def _waits(self, eng, need, skip_pe=False):
        out = []
        seen = self.seen[eng]
        for s, v in need.items():
            if skip_pe and s[0] == "tensor":
                continue
            if seen.get(s, 0) >= v:
                continue
            seen[s] = v
            out.append((s, v))
        return out

    def op(self, eng, fn, reads=(), writes=(), acc=False, rowtile=None):
        need = self._deps(reads, writes)
        if eng == "tensor":
            wk = [self._key(w) for w in writes]
            if rowtile is not None and wk and all(self.pe_rt.get(k) == rowtile for k in wk):
                acc = True
            for k in wk:
                self.pe_rt[k] = rowtile
        waits = self._waits(eng, need, skip_pe=(eng == "tensor" and acc))
        i = self.cnt[eng]
        self.cnt[eng] += 1
        key = (eng, i // SEM_CAP)
        self.sem(key)
        tok = (key, i % SEM_CAP + 1)
        rec = _Rec()
        fn(rec)
        self.ops[eng].append((waits, rec.call, (key, 1)))
        self._commit(reads, writes, tok)
        self.n_inst += 1
        return tok

    def dma(self, eng, out, in_, reads=None, writes=None, **kw):
        reads = [in_] if reads is None else reads
        writes = [out] if writes is None else writes
        need = self._deps(reads, writes)
        j = self.dma_n[eng]
        self.dma_n[eng] += 1
        key = ("dma", eng, j % NDMA_SLOTS)
        self.sem(key)
        val = 16 * (j // NDMA_SLOTS + 1)
        if val > 16:
            need[key] = max(need.get(key, 0), val - 16)
        waits = self._waits(eng, need)
        self.ops[eng].append((waits, ("dma_start", (), dict(out=out, in_=in_, **kw)), (key, 16)))
        self._commit(reads, writes, (key, val))
        self.n_inst += 1

    def _all_tokens(self):
        need = {}
        for e in self.ENGS:
            n = self.dma_n[e]
            for slot in range(min(n, NDMA_SLOTS)):
                need[("dma", e, slot)] = 16 * (((n - 1 - slot) // NDMA_SLOTS) + 1)
            if self.cnt[e]:
                i = self.cnt[e] - 1
                need[(e, i // SEM_CAP)] = i % SEM_CAP + 1
        return need

    def barrier(self):
        need = self._all_tokens()
        for e in self.ENGS:
            waits = self._waits(e, dict(need))
            if waits:
                self.ops[e].append((waits, None, None))
        self.res = {}

    def emit(self):
        nc = self.nc
        with nc.Block() as block:
            def mk(ename):
                lst = self.ops[ename]

                def body(e):
                    for waits, fn, inc in lst:
                        for s, v in waits:
                            e.wait_ge(self.sems[s], v)
                        if fn is not None:
                            getattr(e, fn[0])(*fn[1], **fn[2]).then_inc(self.sems[inc[0]], inc[1])
                return body
            for ename in self.ENGS:
                if self.ops[ename]:
                    getattr(block, ename)(mk(ename))
        self.ops = {e: [] for e in self.ENGS}

    def close(self):
        for cm in reversed(self._stack):
            cm.__exit__(None, None, None)


class Pool:
    def __init__(self, nc):
        self.nc = nc
        self.es = ExitStack()

    def __enter__(self):
        self.es.__enter__()
        return self

    def __exit__(self, *a):
        return self.es.__exit__(*a)

    _n = [0]

    def sb(self, name, shape, dt):
        Pool._n[0] += 1
        return self.es.enter_context(self.nc.sbuf_tensor("%s_%d" % (name, Pool._n[0]), shape, dt))

    def ps(self, name, shape, dt):
        Pool._n[0] += 1
        return self.es.enter_context(self.nc.psum_tensor("%s_%d" % (name, Pool._n[0]), shape, dt))


class Ctx:
    pass


class AV:
    def __init__(self, ap, name):
        self.ap, self.name = ap, name

    def __getitem__(self, key):
        return self.ap[key]


class H:
    def __init__(self, t, pb):
        self.t, self.pb = t, pb
        self.name = "%s@%d" % (t.name, pb)

    def __getitem__(self, key):
        if not isinstance(key, tuple):
            key = (key,)
        k0 = key[0]
        a = 0 if k0.start is None else k0.start
        b = 64 if k0.stop is None else k0.stop
        return self.t[(slice(self.pb + a, self.pb + b),) + tuple(key[1:])]


def V(P, fn, reads, writes):
    return P.op("vector", fn, reads, writes)


def S(P, fn, reads, writes):
    return P.op("scalar", fn, reads, writes)


def MM(P, out, lhsT, rhs, start=True, stop=True, reads=None, writes=None):
    rd = [lhsT, rhs] if reads is None else reads
    wr = [out] if writes is None else writes
    return P.op("tensor", lambda e: e.matmul(out, lhsT=lhsT, rhs=rhs, start=start, stop=stop),
                rd, wr, acc=(not start) and lhsT.shape[0] == 128, rowtile=(lhsT.base_partition(), lhsT.shape[0]))


def TR(P, out, in_, ident, reads=None, writes=None):
    rd = [in_, ident] if reads is None else reads
    wr = [out] if writes is None else writes
    return P.op("tensor", lambda e: e.transpose(out=out, in_=in_, identity=ident), rd, wr,
                rowtile=(in_.base_partition(), in_.shape[0]))


def phase0_mod(g, l):
    nc, P = g.nc, g.P
    with Pool(nc) as pool:
        w0 = pool.sb("adaw0", [128, 8, 512], F32)
        w1 = pool.sb("adaw1", [128, 8, 512], F32)
        modrow = pool.sb("modrow", [2, 6144], F32)
        adab = pool.sb("adab", [2, 6144], F32)
        ps = pool.ps("modps", [2, 512], F32)
        wb = [w0, w1]
        P.dma("sync", adab[:], g.ada_b[l:l + 1, :].broadcast_to([2, 6144]))
        for j in range(12):
            w = wb[j % 2]
            P.dma("sync" if j % 2 == 0 else "gpsimd", w[:],
                  g.ada_w[l, :, j * 512:(j + 1) * 512].rearrange("(kc p) n -> p kc n", p=128))
            for kc in range(8):
                MM(P, ps[:], g.sc[:, kc, :], w[:, kc, :], start=(kc == 0), stop=(kc == 7))
            V(P, lambda e, j=j: e.tensor_tensor(out=modrow[:, j * 512:(j + 1) * 512], in0=ps[:],
                                                in1=adab[:, j * 512:(j + 1) * 512], op=ALU.add),
              [ps, adab], [modrow])
        P.dma("sync", g.modD[l], modrow[:])
        with Pool(nc) as pool:
            tp = pool.ps("modtp", [128, 48, 2], F32)
            for col in range(48):
                TR(P, tp[:, col, :], modrow[:, col * 128:(col + 1) * 128], g.ident[0:2, 0:2])
            V(P, lambda e: e.tensor_copy(out=g.modT[:], in_=tp[:]), [tp], [g.modT])
        mT = g.modT
        ng = g.ngT
        for j in range(2):
            V(P, lambda e, j=j: e.scalar_tensor_tensor(out=g.gain1[:, :, j], in0=mT[:, 8:16, j], scalar=1.0,
                                                       in1=ng[:, l, 0, :], op0=ALU.add, op1=ALU.mult),
              [mT, ng], [g.gain1])
            V(P, lambda e, j=j: e.scalar_tensor_tensor(out=g.gain2[:, :, j], in0=mT[:, 32:40, j], scalar=1.0,
                                                       in1=ng[:, l, 2, :], op0=ALU.add, op1=ALU.mult),
              [mT, ng], [g.gain2])
    P.barrier()


def load_gbc(g, l, slot, gi):
    nc, P = g.nc, g.P
    with Pool(nc) as pool:
        tmpbc = pool.sb("tmpbc", [128, 1024], F32)
        P.dma("sync", tmpbc[:], g.norm_g[l, gi:gi + 1, :].broadcast_to([128, 1024]))
        for j in range(2):
            P.dma("sync", g.Gbc[:, j, :], g.modD[l][j:j + 1, slot * 1024:(slot + 1) * 1024].broadcast_to([128, 1024]),
                  reads=[g.modD[l]], writes=[g.Gbc])
            V(P, lambda e, j=j: e.tensor_tensor(out=g.Gbc[:, j, :], in0=g.Gbc[:, j, :], in1=tmpbc[:], op=ALU.mult),
              [g.Gbc, tmpbc], [g.Gbc])
        P.barrier()


def norm_transpose(g, xt, j, gain, shift, dst, col0, tag):
    nc, P = g.nc, g.P
    ss, rstd, xn = g.ss, g.rstd, g.xn
    sq = xn
    V(P, lambda e: e.memset(ss[:], 0.0), [], [ss])
    S(P, lambda e: e.activation(out=sq[:], in_=xt[:], func=AF.Square, accum_out=ss[:]), [xt, ss], [sq, ss])
    S(P, lambda e: e.activation(out=rstd[:], in_=ss[:], func=AF.Sqrt, bias=1e-6, scale=1.0 / 1024), [ss], [rstd])
    V(P, lambda e: e.reciprocal(out=rstd[:], in_=rstd[:]), [rstd], [rstd])
    V(P, lambda e: e.tensor_scalar(out=xn[:], in0=xt[:], scalar1=rstd[:, 0:1], scalar2=None, op0=ALU.mult),
      [xt, rstd], [xn])
    for half in range(2):
        tp = g.tp[half]
        for q in range(4):
            kc = half * 4 + q
            TR(P, tp[:, q, :], xn[:, kc * 128:(kc + 1) * 128], g.ident[:])
        for q in range(4):
            kc = half * 4 + q
            if q % 2 == 0:
                V(P, lambda e, kc=kc, q=q, tp=tp: e.tensor_scalar(
                    out=dst[:, kc, col0:col0 + 128], in0=tp[:, q, :], scalar1=gain[:, kc, j:j + 1],
                    scalar2=shift[:, kc, j:j + 1], op0=ALU.mult, op1=ALU.add), [tp, gain, shift], [dst])
            else:
                S(P, lambda e, kc=kc, q=q, tp=tp: e.activation(
                    out=dst[:, kc, col0:col0 + 128], in_=tp[:, q, :], func=AF.Identity,
                    bias=shift[:, kc, j:j + 1], scale=gain[:, kc, j:j + 1]), [tp, gain, shift], [dst])


def phase1_h(g, l, xsrc):
    nc, P = g.nc, g.P
    with Pool(nc) as pool:
        g.tp = [pool.ps("tpa", [128, 4, 128], F32), pool.ps("tpb", [128, 4, 128], F32)]
        x0 = pool.sb("p1x0", [128, 1024], F32)
        x1 = pool.sb("p1x1", [128, 1024], F32)
        xb = [x0, x1]
        for i in range(NT):
            xt = xb[i % 2]
            P.dma("sync", xt[:], xsrc[i * 128:(i + 1) * 128, :], reads=[("x", l)], writes=[xt])
            j = 1 if i < 2 else 0
            norm_transpose(g, xt, j, g.gain1, g.modT[:, 0:8, :], g.hT, i * 128, "p1")
    P.barrier()


def mixer_D(g, l, with_ctx):
    nc, P = g.nc, g.P
    wm = g.wmix
    with Pool(nc) as pool:
        Q = pool.sb("dq", [128, 2, T], BF16)
        K = pool.sb("dk", [128, T], BF16)
        Vt = pool.sb("dv", [128, NT, 2, 72], BF16)
        cosb = pool.sb("dcos", [128, 256], F32)
        sinb = pool.sb("dsin", [128, 256], F32)
        t1 = pool.sb("dt1", [128, 256], F32)
        t2 = pool.sb("dt2", [128, 256], F32)
        E0 = pool.sb("dE", [128, 2, 128], BF16)
        E1 = pool.sb("dE1", [128, 2, 128], BF16)
        ytm = pool.sb("dy", [128, 256], F32)
        yTs = pool.sb("dyT", [128, 2, 128], BF16)
        den = pool.sb("dden", [128, 4], F32)
        es = pool.sb("des", [128, 4], F32)
        pa = pool.ps("dpa", [128, 256], F32)
        pb = pool.ps("dpb", [128, 256], F32)
        psA = pool.ps("dps", [128, 2, 128], F32)
        psB = pool.ps("dps2", [128, 2, 128], F32)
        poA = pool.ps("dpoA", [128, 2, 128], F32)
        poB = pool.ps("dpoB", [128, 2, 128], F32)
        pt = pool.ps("d_pt", [128, 4, 128], F32)
        pos = [poA, poB]
        P.dma("gpsimd", wm[:, :, 0:896], g.w_D[l].rearrange("(kc p) n -> p kc n", p=128))
        P.dma("sync", es[:], g.attn_sink[l:l + 1, :].broadcast_to([128, 4]))
        S(P, lambda e: e.activation(out=es[:], in_=es[:], func=AF.Exp), [es], [es])
        V(P, lambda e: e.memset(Vt[:], 1.0), [], [Vt])
        for grp in range(NG):
            ts = slice(grp * 256, (grp + 1) * 256)
            P.dma("sync", cosb[:], g.ropec[:, ts])
            P.dma("gpsimd", sinb[:], g.ropes[:, ts])
            for which in range(3):
                for kc in range(8):
                    MM(P, pa[:], wm[:, kc, which * 128:(which + 1) * 128], g.hT[:, kc, ts], start=(kc == 0), stop=(kc == 7),
                       reads=[wm, g.hT])
                for kc in range(8):
                    MM(P, pb[:], wm[:, kc, 384 + which * 128:384 + (which + 1) * 128], g.hT[:, kc, ts], start=(kc == 0),
                       stop=(kc == 7), reads=[wm, g.hT])
                dst = Q[:, which, ts] if which < 2 else K[:, ts]
                V(P, lambda e: e.tensor_tensor(out=t1[:], in0=pa[:], in1=cosb[:], op=ALU.mult), [pa, cosb], [t1])
                V(P, lambda e: e.tensor_tensor(out=t2[:], in0=pb[:], in1=sinb[:], op=ALU.mult), [pb, sinb], [t2])
                V(P, lambda e, dst=dst: e.tensor_tensor(out=dst, in0=t1[:], in1=t2[:], op=ALU.add), [t1, t2],
                  [Q if which < 2 else K])
            for tt in range(2):
                ti = grp * 2 + tt
                for kc in range(8):
                    MM(P, pa[:, 0:128], g.hT[:, kc, ti * 128:(ti + 1) * 128], wm[:, kc, 768:896], start=(kc == 0),
                       stop=(kc == 7), reads=[wm, g.hT], writes=[pa])
                S(P, lambda e, ti=ti: e.activation(out=Vt[:, ti, :, 0:64],
                                                   in_=pa[:, 0:128].rearrange("p (a b) -> p a b", a=2), func=AF.Copy),
                  [pa], [Vt])
        Eb = [E0, E1]
        ne = 0
        for qi in range(NT):
            if qi < 2:
                if not with_ctx:
                    continue
                keys = [(0, None), (1, None)]
            else:
                n = qi - 2
                keys = [(0, None), (1, None)]
                if n > 0:
                    keys.append((qi - 1, g.maskP))
                keys.append((qi, None))
                if n < 31:
                    keys.append((qi + 1, g.maskN))
            qs = slice(qi * 128, (qi + 1) * 128)
            for j in range(2):
                jb = j * 64
                for idx, (kt, msk) in enumerate(keys):
                    ps = psA if (ne % 2 == 0) else psB
                    E = Eb[ne % 2]
                    ne += 1
                    MM(P, ps[:], K[jb:jb + 64, kt * 128:(kt + 1) * 128], Q[jb:jb + 64, :, qs], reads=[K, Q])
                    S(P, lambda e, ps=ps, E=E: e.activation(out=E[:], in_=ps[:], func=AF.Exp, scale=0.125), [ps], [E])
                    if msk is not None:
                        V(P, lambda e, E=E, msk=msk: e.tensor_tensor(out=E[:], in0=E[:],
                                                                     in1=msk[:].unsqueeze(1).broadcast_to([128, 2, 128]),
                                                                     op=ALU.mult), [E, msk], [E])
                    for gq in range(2):
                        hh = j * 2 + gq
                        MM(P, pos[gq][:, j, 0:65], E[:, gq, :], Vt[:, kt, j, 0:65], start=(idx == 0), stop=(idx == len(keys) - 1),
                           reads=[E, Vt], writes=[pos[gq]])
            pk = [poA, poB]
            y4 = ytm[:].rearrange("p (j q d) -> p j q d", j=2, q=2)
            for gq in range(2):
                V(P, lambda e, gq=gq: e.tensor_tensor(out=den[:, gq:4:2], in0=pos[gq][:, :, 64], in1=es[:, gq:4:2], op=ALU.add),
                  pk + [es], [den])
            V(P, lambda e: e.reciprocal(out=den[:], in_=den[:]), [den], [den])
            for gq in range(2):
                V(P, lambda e, gq=gq: e.tensor_tensor(out=y4[:, :, gq, :], in0=pos[gq][:, :, 0:64],
                                                      in1=den[:, gq:4:2].unsqueeze(2).broadcast_to([128, 2, 64]), op=ALU.mult),
                  pk + [den], [ytm] + pk)
            for c2 in range(2):
                TR(P, pt[:, c2, :], ytm[:, c2 * 128:(c2 + 1) * 128], g.ident[:])
            S(P, lambda e: e.activation(out=yTs[:], in_=pt[:, 0:2, :], func=AF.Copy), [pt], [yTs])
            P.dma("sync", g.yT[3][:, :, qs], yTs[:], reads=[yTs], writes=[("yT", 3)])
    P.barrier()


def neumann_inverse(g, pool_t, Y, X, ident_bc_ap):
    P = g.P
    PPT, Z, psI1, psI2 = pool_t["PPT"], pool_t["Z"], pool_t["psI1"], pool_t["psI2"]
    pP = psI1[:, 0:256].rearrange("p (h t) -> p h t", h=4)
    pPT = psI1[:, 256:512].rearrange("p (h t) -> p h t", h=4)
    pPZ = psI2[:, 0:256].rearrange("p (h t) -> p h t", h=4)
    V(P, lambda e: e.tensor_tensor(out=Z[:], in0=ident_bc_ap, in1=Y[:], op=ALU.subtract), [Y, g.ident], [Z])
    curP = lambda h: Y[:, h, :]
    curPT = lambda h: X[:, h, :]
    rd = [Y, X]
    for k in range(1, 6):
        T2 = PPT[k % 2]
        if k < 5:
            for h in range(4):
                MM(P, pP[:, h, :], curPT(h), curP(h), reads=rd, writes=[psI1])
        for h in range(4):
            MM(P, pPT[:, h, :], curP(h), curPT(h), reads=rd, writes=[psI1])
        if k < 5:
            S(P, lambda e: e.activation(out=T2[:], in_=psI1[:, 0:512].rearrange("p (h t) -> p h t", h=8), func=AF.Copy),
              [psI1], [T2])
        else:
            S(P, lambda e: e.activation(out=T2[:, 4:8, :], in_=pPT, func=AF.Copy), [psI1], [T2])
        yield
        for h in range(4):
            MM(P, pPZ[:, h, :], T2[:, 4 + h, :], Z[:, h, :], reads=[T2, Z], writes=[psI2])
        V(P, lambda e: e.tensor_tensor(out=Z[:], in0=Z[:], in1=pPZ, op=ALU.add), [Z, psI2], [Z])
        if k < 5:
            yield
        curP = lambda h, T2=T2: T2[:, h, :]
        curPT = lambda h, T2=T2: T2[:, 4 + h, :]
        rd = [T2]


def mixer_B_pre(g, l):
    nc, P = g.nc, g.P
    wm = g.wmix
    P.dma("gpsimd", wm[:, :, 0:1040], g.w_in[l, :, 960:2000].rearrange("(kc p) n -> p kc n", p=128))
    with Pool(nc) as pool:
        convw = pool.sb("b_convw", [128, 6, 7], F32)
        xpad_2 = [pool.sb("b_xpad0", [128, 262], F32), pool.sb("b_xpad1", [128, 262], F32)]
        acc_2 = [pool.sb("b_acc0", [128, 256], F32), pool.sb("b_acc1", [128, 256], F32)]
        sT_2 = [pool.sb("b_sT0", [128, 256], F32), pool.sb("b_sT1", [128, 256], F32)]
        sq_2 = [pool.sb("b_sq0", [128, 256], F32), pool.sb("b_sq1", [128, 256], F32)]
        rst_2 = [pool.sb("b_rst0", [128, 256], F32), pool.sb("b_rst1", [128, 256], F32)]
        qn_2 = [pool.sb("b_qn0", [128, 256], F32), pool.sb("b_qn1", [128, 256], F32)]
        qnb_2 = [pool.sb("b_qnb0", [128, 256], BF16), pool.sb("b_qnb1", [128, 256], BF16)]
        stage_2 = [pool.sb("b_stage0", [128, 2, 512], F32), pool.sb("b_stage1", [128, 2, 512], F32)]
        baall = pool.sb("b_baall", [128, NT, 16], F32)
        dtb8 = pool.sb("b_dtb8", [128, 8], F32)
        aex8 = pool.sb("b_aex8", [128, 8], F32)
        pp_2 = [pool.ps("b_pp0", [128, 512], F32), pool.ps("b_pp1", [128, 512], F32)]
        pss_2 = [pool.ps("b_pss0", [128, 512], F32), pool.ps("b_pss1", [128, 512], F32)]
        ptr_2 = [pool.ps("b_ptr0", [128, 512], F32), pool.ps("b_ptr1", [128, 512], F32)]
        P.dma("sync", convw[:], g.convT[:, l])
        for grp in range(NG):
            t0 = grp * 256
            s0, s1 = (0, 256) if grp == 0 else (256, T)
            lo, hi = max(t0 - 3, s0), min(t0 + 259, s1)
            o0, o1 = lo - (t0 - 3), hi - (t0 - 3)
            stage = stage_2[grp % 2]
            for c in range(6):
                q = c % 2
                xpad, acc, sT, sq, rst, qn, qnb = xpad_2[q], acc_2[q], sT_2[q], sq_2[q], rst_2[q], qn_2[q], qnb_2[q]
                pp, pss, ptr = pp_2[q], pss_2[q], ptr_2[q]
                for kc in range(8):
                    MM(P, pp[:, o0:o1], wm[:, kc, c * 128:(c + 1) * 128], g.hT[:, kc, lo:hi], start=(kc == 0), stop=(kc == 7),
                       reads=[wm, g.hT], writes=[pp])
                V(P, lambda e: e.memset(xpad[:], 0.0), [], [xpad])
                S(P, lambda e: e.activation(out=xpad[:, o0:o1], in_=pp[:, o0:o1], func=AF.Copy), [pp], [xpad])
                V(P, lambda e: e.tensor_scalar(out=acc[:], in0=xpad[:, 0:256], scalar1=convw[:, c, 0:1], scalar2=None,
                                               op0=ALU.mult), [xpad, convw], [acc])
                for k in range(1, 7):
                    V(P, lambda e: e.scalar_tensor_tensor(out=acc[:], in0=xpad[:, k:k + 256], scalar=convw[:, c, k:k + 1],
                                                          in1=acc[:], op0=ALU.mult, op1=ALU.add), [xpad, convw, acc], [acc])
                S(P, lambda e: e.activation(out=sT[:], in_=acc[:], func=AF.Silu), [acc], [sT])
                src = sT
                if c < 4:
                    V(P, lambda e: e.tensor_tensor(out=sq[:], in0=sT[:], in1=sT[:], op=ALU.mult), [sT], [sq])
                    MM(P, pss[:, 0:256], g.blk[:], sq[:], reads=[g.blk, sq], writes=[pss])
                    S(P, lambda e: e.activation(out=rst[:], in_=pss[:, 0:256], func=AF.Sqrt, bias=1e-6), [pss], [rst])
                    V(P, lambda e: e.reciprocal(out=rst[:], in_=rst[:]), [rst], [rst])
                    V(P, lambda e: e.scalar_tensor_tensor(out=qn[:], in0=sT[:], scalar=(0.125 if c < 2 else 1.0), in1=rst[:],
                                                          op0=ALU.mult, op1=ALU.mult), [sT, rst], [qn])
                    S(P, lambda e: e.activation(out=qnb[:], in_=qn[:], func=AF.Copy), [qn], [qnb])
                    P.dma("sync", g.bqk[c, :, t0:t0 + 256], qnb[:], reads=[qnb], writes=["bqk"])
                    src = qn
                if c >= 2:
                    for tt in range(2):
                        TR(P, ptr[:, tt * 128:(tt + 1) * 128], src[:, tt * 128:(tt + 1) * 128], g.ident[:], reads=[src, g.ident],
                           writes=[ptr])
                    S(P, lambda e: e.activation(out=stage[:, :, (c - 2) * 128:(c - 1) * 128],
                                                in_=ptr[:, 0:256].rearrange("p (a b) -> p a b", a=2), func=AF.Copy),
                      [ptr], [stage])
            pss = pss_2[0]
            for tt in range(2):
                r0 = t0 + tt * 128
                P.dma("sync", g.bkv[r0:r0 + 128, :], stage[:, tt, :], reads=[stage], writes=["bkv"])
                for kc in range(8):
                    MM(P, pss[:, 256:272], g.hT[:, kc, r0:r0 + 128], wm[:, kc, 768:784], start=(kc == 0), stop=(kc == 7),
                       reads=[wm, g.hT], writes=[pss])
                S(P, lambda e: e.activation(out=baall[:, grp * 2 + tt, :], in_=pss[:, 256:272], func=AF.Copy), [pss], [baall])
        P.dma("sync", dtb8[:], g.gdn_dt_bias[l:l + 1].rearrange("o d h -> o (d h)").broadcast_to([128, 8]))
        P.dma("sync", aex8[:], g.gdn_a_log[l:l + 1].rearrange("o d h -> o (d h)").broadcast_to([128, 8]))
        S(P, lambda e: e.activation(out=aex8[:], in_=aex8[:], func=AF.Exp), [aex8], [aex8])
        bc8 = lambda t: t[:].unsqueeze(1).broadcast_to([128, NT, 8])
        V(P, lambda e: e.tensor_tensor(out=baall[:, :, 8:16], in0=baall[:, :, 8:16], in1=bc8(dtb8), op=ALU.add), [baall, dtb8], [baall])
        S(P, lambda e: e.activation(out=baall[:, :, 8:16], in_=baall[:, :, 8:16], func=AF.Exp), [baall], [baall])
        S(P, lambda e: e.activation(out=baall[:, :, 8:16], in_=baall[:, :, 8:16], func=AF.Ln, bias=1.0), [baall], [baall])
        V(P, lambda e: e.tensor_tensor(out=baall[:, :, 8:16], in0=baall[:, :, 8:16], in1=bc8(aex8), op=ALU.mult), [baall, aex8], [baall])
        S(P, lambda e: e.activation(out=baall[:, :, 0:8], in_=baall[:, :, 0:8], func=AF.Sigmoid), [baall], [baall])
        P.dma("sync", g.bba.rearrange("(t p) c -> p t c", p=128), baall[:], reads=[baall], writes=["bba"])
    P.barrier()
    with Pool(nc) as pool:
        pgs = [pool.ps("b_gp0", [128, 512], F32), pool.ps("b_gp1", [128, 512], F32)]
        sgs = [pool.sb("b_gs0", [128, 256], F32), pool.sb("b_gs1", [128, 256], F32)]
        for ti in range(NT):
            pg, sgt = pgs[ti % 2], sgs[ti % 2]
            for kc in range(8):
                MM(P, pg[:, 0:256], g.hT[:, kc, ti * 128:(ti + 1) * 128], wm[:, kc, 784:1040], start=(kc == 0), stop=(kc == 7),
                   reads=[wm, g.hT], writes=[pg])
            S(P, lambda e: e.activation(out=sgt[:], in_=pg[:, 0:256], func=AF.Silu), [pg], [sgt])
            P.dma("sync", g.bgate[ti * 128:(ti + 1) * 128, :], sgt[:], reads=[sgt], writes=["bgate"])
    P.barrier()


def mixer_B_chains(g, l, pool, banks, psV_shared, PB):
    nc, P = g.nc, g.P
    if True:
        psGC_f, psK2_f, psI1_f, psI2_f, psS2_f, psVA_f = banks
        r4 = lambda ap: ap.rearrange("p (h t) -> p h t", h=4)
        bc4 = lambda ap: ap.unsqueeze(2).broadcast_to([64, 4, 64])
        def chain(d):
            pb = PB

            def sb(name, shape, dt):
                if shape[0] == 64:
                    return H(pool.sb(name, [128] + list(shape[1:]), dt), pb)
                return pool.sb(name, shape, dt)
            tri = H(g.triD, pb)
            identh = g.identD[pb:pb + 64, :]
            psVf = psV_shared
            psV = H(psVf, pb)
            psGC, psK2, psI1, psI2 = H(psGC_f, pb), H(psK2_f, pb), H(psI1_f, pb), H(psI2_f, pb)
            psS2, psVA = H(psS2_f, pb), H(psVA_f, pb)
            pGC, pBR = r4(psGC[:, 0:256]), r4(psGC[:, 256:512])
            pKK, pQK = r4(psK2[:, 0:256]), r4(psK2[:, 256:512])
            pKS, pQS = r4(psS2[:, 0:256]), r4(psS2[:, 256:512])
            pVN, pAV = r4(psVA[:, 0:256]), r4(psVA[:, 256:512])
            pSn = psVf[:, 16:144].rearrange("p (c t) -> p c t", c=2)
            identbc = identh.unsqueeze(1).broadcast_to([64, 4, 64])
            qk = sb("b_qk", [128, 4, 64], BF16)
            kv = sb("b_kv", [64, 512], F32)
            ba = sb("b_ba", [64, 16], F32)
            gcum = sb("b_gcum", [64, 4], F32)
            Eg = sb("b_Eg", [64, 4], F32)
            Es = sb("b_Es", [64, 4], F32)
            be = sb("b_be", [64, 4], F32)
            decbc = sb("b_dec", [128, 4], F32)
            Gbc = sb("b_Gbc", [64, 4, 64], F32)
            Bbc = sb("b_Bbc", [64, 4, 64], F32)
            nd = sb("b_nd", [64, 4, 64], F32)
            ndt = sb("b_ndt", [64, 4, 64], F32)
            DT = sb("b_DT", [64, 4, 64], F32)
            Dn = sb("b_Dn", [64, 4, 64], F32)
            Y = sb("b_Y", [64, 4, 64], F32)
            X = sb("b_X", [64, 4, 64], F32)
            AT = sb("b_AT", [64, 4, 64], BF16)
            tmp = sb("b_tmp", [64, 4, 64], F32)
            tmp2 = sb("b_tmp2", [64, 4, 64], F32)
            scr = dict(PPT=[sb("b_PPT0", [64, 8, 64], F32), sb("b_PPT1", [64, 8, 64], F32)],
                       Z=sb("b_Z", [64, 4, 64], F32))
            scr["psI1"], scr["psI2"] = psI1, psI2
            Vb = sb("b_Vb", [64, 4, 64], F32)
            R = sb("b_R", [64, 4, 64], F32)
            vn = sb("b_vn", [64, 4, 64], BF16)
            kst = sb("b_kst", [64, 4, 64], BF16)
            Sst = sb("b_S", [128, 2, 64], F32)
            Sbf = sb("b_Sbf", [128, 2, 64], BF16)
            osb = sb("b_osb", [64, 256], F32)
            incl = tri[:, 2 * d, :]
            after = tri[:, 2 * d + 1, :]
            before = tri[:, 2 * (1 - d) + 1, :]
            inclbc = incl.unsqueeze(1).broadcast_to([64, 4, 64])
            afterbc = after.unsqueeze(1).broadcast_to([64, 4, 64])
            beforebc = before.unsqueeze(1).broadcast_to([64, 4, 64])
            V(P, lambda e: e.memset(Sst[:], 0.0), [], [Sst])
            V(P, lambda e: e.memset(Sbf[:], 0.0), [], [Sbf])
            for n in chunk_order(d):
                ts = slice(n * 64, (n + 1) * 64)
                P.dma("sync", qk[:], g.bqk[:, :, ts].rearrange("w p t -> p w t"), reads=[], writes=[qk])
                P.dma("gpsimd", kv[:], g.bkv[ts, :], reads=[], writes=[kv])
                P.dma("sync", ba[:], g.bba[ts, :], reads=[], writes=[ba])
                qT = lambda h: qk[(h % 2) * 64:(h % 2) * 64 + 64, h // 2, :]
                kT = lambda h: qk[(h % 2) * 64:(h % 2) * 64 + 64, 2 + h // 2, :]
                vtm = kv[:, 256:512].rearrange("p (h t) -> p h t", h=4)
                ktm = kv[:, 0:256].rearrange("p (h t) -> p h t", h=4)
                beta = AV(ba[:, 4 * d:4 * d + 4], ba.name)
                gneg = AV(ba[:, 8 + 4 * d:12 + 4 * d], ba.name)
                MM(P, psV[0:64, 0:4], incl, gneg[:], reads=[g.tri, gneg], writes=[psV])
                MM(P, psV[0:64, 4:8], after, gneg[:], reads=[g.tri, gneg], writes=[psV])
                MM(P, psVf[:, 8:12], g.ones128[pb:pb + 64, :], gneg[:], reads=[g.ones128, gneg], writes=[psV, psVf])
                S(P, lambda e: e.activation(out=gcum[:], in_=psV[0:64, 0:4], func=AF.Copy), [psV], [gcum])
                S(P, lambda e: e.activation(out=Eg[:], in_=psV[0:64, 0:4], func=AF.Exp, scale=-1.0), [psV], [Eg])
                S(P, lambda e: e.activation(out=Es[:], in_=psV[0:64, 4:8], func=AF.Exp, scale=-1.0), [psV], [Es])
                S(P, lambda e: e.activation(out=decbc[:], in_=psVf[:, 8:12], func=AF.Exp, scale=-1.0), [psV, psVf], [decbc])
                V(P, lambda e: e.tensor_tensor(out=be[:], in0=beta[:], in1=Eg[:], op=ALU.mult), [beta, Eg], [be])
                V(P, lambda e: e.tensor_copy(out=Gbc[:], in_=bc4(gneg[:])), [gneg], [Gbc])
                V(P, lambda e: e.tensor_copy(out=Bbc[:], in_=bc4(beta[:])), [beta], [Bbc])
                yield
                for h in range(4):
                    MM(P, pGC[:, h, :], Gbc[:, h, :], incl, reads=[Gbc, g.tri], writes=[psGC])
                V(P, lambda e: e.tensor_tensor(out=nd[:], in0=pGC, in1=bc4(gcum[:]), op=ALU.subtract), [psGC, gcum], [nd])
                V(P, lambda e: e.tensor_scalar(out=ndt[:], in0=nd[:], scalar1=0.0, scalar2=None, op0=ALU.max), [nd], [ndt])
                S(P, lambda e: e.activation(out=DT[:], in_=ndt[:], func=AF.Exp, scale=-1.0), [ndt], [DT])
                V(P, lambda e: e.tensor_scalar(out=ndt[:], in0=nd[:], scalar1=0.0, scalar2=None, op0=ALU.min), [nd, DT], [ndt])
                S(P, lambda e: e.activation(out=Dn[:], in_=ndt[:], func=AF.Exp), [ndt], [Dn])
                yield
                for h in range(4):
                    MM(P, pBR[:, h, :], Bbc[:, h, :], identh, reads=[Bbc, g.ident], writes=[psGC])
                for h in (0, 2, 1, 3):
                    MM(P, pKK[:, h, :], kT(h), kT(h), reads=[qk], writes=[psK2])
                for h in (0, 2, 1, 3):
                    MM(P, pQK[:, h, :], kT(h), qT(h), reads=[qk], writes=[psK2])
                V(P, lambda e: e.tensor_tensor(out=tmp[:], in0=pKK, in1=DT[:], op=ALU.mult), [psK2, DT], [tmp])
                V(P, lambda e: e.tensor_tensor(out=tmp[:], in0=tmp[:], in1=beforebc, op=ALU.mult), [tmp, g.tri], [tmp])
                V(P, lambda e: e.tensor_tensor(out=Y[:], in0=tmp[:], in1=pBR, op=ALU.mult), [tmp, psGC], [Y])
                V(P, lambda e: e.tensor_tensor(out=tmp2[:], in0=pKK, in1=Dn[:], op=ALU.mult), [psK2, Dn], [tmp2])
                V(P, lambda e: e.tensor_tensor(out=tmp2[:], in0=tmp2[:], in1=afterbc, op=ALU.mult), [tmp2, g.tri], [tmp2])
                V(P, lambda e: e.tensor_tensor(out=X[:], in0=tmp2[:], in1=bc4(beta[:]), op=ALU.mult), [tmp2, beta], [X])
                V(P, lambda e: e.tensor_tensor(out=tmp[:], in0=pQK, in1=DT[:], op=ALU.mult), [psK2, DT], [tmp])
                V(P, lambda e: e.tensor_tensor(out=AT[:], in0=tmp[:], in1=inclbc, op=ALU.mult), [tmp, g.tri], [AT])
                yield
                yield from neumann_inverse(g, scr, Y, X, identbc)
                Z = scr["Z"]
                yield
                for h in (0, 2, 1, 3):
                    hb, c = (h % 2) * 64, h // 2
                    MM(P, pKS[:, h, :], kT(h), Sbf[hb:hb + 64, c, :], reads=[qk, Sbf], writes=[psS2])
                for h in (0, 2, 1, 3):
                    hb, c = (h % 2) * 64, h // 2
                    MM(P, pQS[:, h, :], qT(h), Sbf[hb:hb + 64, c, :], reads=[qk, Sbf], writes=[psS2])
                V(P, lambda e: e.tensor_tensor(out=Vb[:], in0=vtm, in1=bc4(beta[:]), op=ALU.mult), [kv, beta], [Vb])
                V(P, lambda e: e.tensor_tensor(out=tmp2[:], in0=pKS, in1=bc4(be[:]), op=ALU.mult), [psS2, be], [tmp2])
                V(P, lambda e: e.tensor_tensor(out=R[:], in0=Vb[:], in1=tmp2[:], op=ALU.subtract), [Vb, tmp2], [R])
                V(P, lambda e: e.tensor_tensor(out=tmp[:], in0=pQS, in1=bc4(Eg[:]), op=ALU.mult), [psS2, Eg], [tmp])
                yield
                for h in range(4):
                    MM(P, pVN[:, h, :], Z[:, h, :], R[:, h, :], reads=[Z, R], writes=[psVA])
                S(P, lambda e: e.activation(out=vn[:], in_=pVN, func=AF.Copy), [psVA], [vn])
                yield
                for h in range(4):
                    MM(P, pAV[:, h, :], AT[:, h, :], vn[:, h, :], reads=[AT, vn], writes=[psVA])
                V(P, lambda e: e.tensor_tensor(out=osb[:].rearrange("p (h t) -> p h t", h=4), in0=tmp[:], in1=pAV, op=ALU.add),
                  [tmp, psVA], [osb])
                V(P, lambda e: e.tensor_tensor(out=kst[:], in0=ktm, in1=bc4(Es[:]), op=ALU.mult), [kv, Es], [kst])
                yield
                for h in (0, 2, 1, 3):
                    hb, c = (h % 2) * 64, h // 2
                    MM(P, pSn[hb:hb + 64, c, :], kst[:, h, :], vn[:, h, :], reads=[kst, vn], writes=[psV, psVf])
                for h in range(4):
                    hb, c = (h % 2) * 64, h // 2
                    V(P, lambda e: e.scalar_tensor_tensor(out=Sst[hb:hb + 64, c, :], in0=Sst[hb:hb + 64, c, :],
                                                          scalar=decbc[hb:hb + 64, h:h + 1], in1=pSn[hb:hb + 64, c, :],
                                                          op0=ALU.mult, op1=ALU.add), [Sst, decbc, psV, psVf], [Sst])
                S(P, lambda e: e.activation(out=Sbf[:], in_=Sst[:], func=AF.Copy), [Sst], [Sbf])
                P.dma("sync", g.ofwB[d][ts, :], osb[:], reads=[osb], writes=[("ofwB", d)])

        return [chain(0), chain(1)]


def mixer_B_finish(g, l):
    nc, P = g.nc, g.P
    with Pool(nc) as pool:
        ngbc = pool.sb("b_ng", [64, 256], F32)
        psI2_2 = [pool.ps("b_fpG_a", [64, 512], F32), pool.ps("b_fpG_b", [64, 512], F32)]
        pt_2 = [pool.ps("b_pt_a", [128, 4, 128], F32), pool.ps("b_pt_b", [128, 4, 128], F32)]
        P.dma("sync", ngbc[:], g.ngB[l:l + 1, :].broadcast_to([64, 256]))
        osb_2 = [pool.sb("b_fo_a", [64, 256], F32), pool.sb("b_fo_b", [64, 256], F32)]
        ofl_2 = [pool.sb("b_fl_a", [64, 256], F32), pool.sb("b_fl_b", [64, 256], F32)]
        ysq_2 = [pool.sb("b_fysq_a", [64, 256], F32), pool.sb("b_fysq_b", [64, 256], F32)]
        ss4_2 = [pool.sb("b_fss4_a", [64, 4], F32), pool.sb("b_fss4_b", [64, 4], F32)]
        sg_2 = [pool.sb("b_fsg_a", [64, 256], F32), pool.sb("b_fsg_b", [64, 256], F32)]
        yTs_2 = [pool.sb("b_fyTs_a", [128, 2, 64], BF16), pool.sb("b_fyTs_b", [128, 2, 64], BF16)]
        for n in range(68):
            osb, ofl, ysq, ss4, sg, yTs, psI2, pt = osb_2[n % 2], ofl_2[n % 2], ysq_2[n % 2], ss4_2[n % 2], sg_2[n % 2], yTs_2[n % 2], psI2_2[n % 2], pt_2[n % 2]
            pG = psI2[:, 256:512]
            ts = slice(n * 64, (n + 1) * 64)
            P.dma("sync", osb[:], g.ofwB[0][ts, :], reads=[], writes=[osb])
            P.dma("gpsimd", ofl[:], g.ofwB[1][ts, :], reads=[], writes=[ofl])
            P.dma("sync", sg[:], g.bgate[ts, :], reads=[], writes=[sg])
            V(P, lambda e: e.tensor_tensor(out=osb[:], in0=osb[:], in1=ofl[:], op=ALU.add), [osb, ofl], [osb])
            rms_gate_finish(g, osb, ysq, ss4, ngbc, None, None, sg, pt, yTs, 1, ts, 1e-6)
    P.barrier()


def mixer_A_pre(g, l):
    nc, P = g.nc, g.P
    wm = g.wmix
    CW = float(np.exp(-0.5))
    P.dma("gpsimd", wm[:, :, 0:960], g.w_in[l, :, 0:960].rearrange("(kc p) n -> p kc n", p=128))
    with Pool(nc) as pool:
        mu = pool.sb("a_mu", [128, 8], F32)
        omu = pool.sb("a_omu", [128, 8], F32)
        hmu = pool.sb("a_hmu", [128, 8], F32)
        kkw = pool.sb("a_kkw", [128, 2], F32)
        xpad_2 = [pool.sb("a_xpad0", [128, 258], F32), pool.sb("a_xpad1", [128, 258], F32)]
        acc_2 = [pool.sb("a_acc0", [128, 256], F32), pool.sb("a_acc1", [128, 256], F32)]
        t1_2 = [pool.sb("a_t10", [128, 256], F32), pool.sb("a_t11", [128, 256], F32)]
        sq_2 = [pool.sb("a_sq0", [128, 256], F32), pool.sb("a_sq1", [128, 256], F32)]
        rst_2 = [pool.sb("a_rst0", [128, 256], F32), pool.sb("a_rst1", [128, 256], F32)]
        kkn_2 = [pool.sb("a_kkn0", [128, 256], F32), pool.sb("a_kkn1", [128, 256], F32)]
        stage_2 = [pool.sb("a_stage0", [128, 2, 1024], F32), pool.sb("a_stage1", [128, 2, 1024], F32)]
        pp_2 = [pool.ps("a_pp0", [128, 512], F32), pool.ps("a_pp1", [128, 512], F32)]
        pss_2 = [pool.ps("a_pss0", [128, 512], F32), pool.ps("a_pss1", [128, 512], F32)]
        ptr_2 = [pool.ps("a_ptr0", [128, 512], F32), pool.ps("a_ptr1", [128, 512], F32)]
        P.dma("sync", mu[:], g.muT[:, l])
        P.dma("sync", kkw[:], g.kkwT[:, l])
        V(P, lambda e: e.tensor_scalar(out=omu[:], in0=mu[:], scalar1=-1.0, scalar2=1.0, op0=ALU.mult, op1=ALU.add), [mu], [omu])
        V(P, lambda e: e.tensor_scalar(out=hmu[:], in0=mu[:], scalar1=0.5, scalar2=None, op0=ALU.mult), [mu], [hmu])

        def to_tm(src, col0, ptr, stage):
            for tt in range(2):
                TR(P, ptr[:, tt * 128:(tt + 1) * 128], src[:, tt * 128:(tt + 1) * 128], g.ident[:], reads=[src, g.ident], writes=[ptr])
            S(P, lambda e: e.activation(out=stage[:, :, col0:col0 + 128], in_=ptr[:, 0:256].rearrange("p (a b) -> p a b", a=2),
                                        func=AF.Copy), [ptr], [stage])

        for grp in range(NG):
            t0 = grp * 256
            s0, s1 = (0, 256) if grp == 0 else (256, T)
            lo, hi = max(t0 - 1, s0), min(t0 + 257, s1)
            o0, o1 = lo - (t0 - 1), hi - (t0 - 1)
            stage = stage_2[grp % 2]
            for c in range(8):
                q = c % 2
                xpad, acc, t1, sq, rst, kkn = xpad_2[q], acc_2[q], t1_2[q], sq_2[q], rst_2[q], kkn_2[q]
                pp, pss, ptr = pp_2[q], pss_2[q], ptr_2[q]
                npart = 128 if c < 7 else 64
                ncol = 128 if c < 7 else 64
                for kc in range(8):
                    MM(P, pp[0:npart, o0:o1], wm[:, kc, c * 128:c * 128 + ncol], g.hT[:, kc, lo:hi], start=(kc == 0), stop=(kc == 7),
                       reads=[wm, g.hT], writes=[pp])
                V(P, lambda e: e.memset(xpad[:], 0.0), [], [xpad])
                S(P, lambda e: e.activation(out=xpad[0:npart, o0:o1], in_=pp[0:npart, o0:o1], func=AF.Copy), [pp], [xpad])
                V(P, lambda e: e.tensor_scalar(out=acc[:], in0=xpad[:, 1:257], scalar1=omu[:, c:c + 1], scalar2=None, op0=ALU.mult),
                  [xpad, omu], [acc])
                V(P, lambda e: e.tensor_tensor(out=t1[:], in0=xpad[:, 0:256], in1=xpad[:, 2:258], op=ALU.add), [xpad], [t1])
                V(P, lambda e: e.scalar_tensor_tensor(out=acc[:], in0=t1[:], scalar=hmu[:, c:c + 1], in1=acc[:], op0=ALU.mult,
                                                      op1=ALU.add), [t1, hmu, acc], [acc])
                if c < 2:
                    P.dma("sync", g.aF[c, :, t0:t0 + 256], acc[:], reads=[acc], writes=["aF"])
                    to_tm(acc, c * 128, ptr, stage)
                elif c < 4:
                    P.dma("sync", g.aF[c, :, t0:t0 + 256], acc[:], reads=[acc], writes=["aF"])
                    to_tm(acc, c * 128, ptr, stage)
                    V(P, lambda e: e.tensor_scalar(out=kkn[:], in0=acc[:], scalar1=kkw[:, c - 2:c - 1], scalar2=None, op0=ALU.mult),
                      [acc, kkw], [kkn])
                    V(P, lambda e: e.tensor_tensor(out=sq[:], in0=kkn[:], in1=kkn[:], op=ALU.mult), [kkn], [sq])
                    MM(P, pss[:, 0:256], g.blk[:], sq[:], reads=[g.blk, sq], writes=[pss])
                    S(P, lambda e: e.activation(out=rst[:], in_=pss[:, 0:256], func=AF.Sqrt, bias=1e-6), [pss], [rst])
                    V(P, lambda e: e.reciprocal(out=rst[:], in_=rst[:]), [rst], [rst])
                    V(P, lambda e: e.tensor_tensor(out=kkn[:], in0=kkn[:], in1=rst[:], op=ALU.mult), [kkn, rst], [kkn])
                    P.dma("sync", g.aF[c + 2, :, t0:t0 + 256], kkn[:], reads=[kkn], writes=["aF"])
                    to_tm(kkn, 768 + (c - 2) * 128, ptr, stage)
                elif c < 6:
                    to_tm(acc, c * 128, ptr, stage)
                elif c == 6:
                    S(P, lambda e: e.activation(out=acc[0:64, :], in_=acc[0:64, :], func=AF.Tanh), [acc], [acc])
                    P.dma("sync", g.asm[:, t0:t0 + 256], acc[:], reads=[acc], writes=["asm"])
                else:
                    S(P, lambda e: e.activation(out=acc[0:64, :], in_=acc[0:64, :], func=AF.Sigmoid), [acc], [acc])
                    P.dma("sync", g.asg[:, t0:t0 + 256], acc[0:64, :], reads=[acc], writes=["asg"])
            for tt in range(2):
                r0 = t0 + tt * 128
                P.dma("sync", g.aTM[r0:r0 + 128, :], stage[:, tt, :], reads=[stage], writes=["aTM"])
    P.barrier()


def mixer_A_chains(g, l, pool, banks, psAF_shared, PB):
    nc, P = g.nc, g.P
    CW = float(np.exp(-0.5))
    if True:
        psW_f, psM1_f, psM2_f, psM3_f, psI1_f, psI2_f = banks
        sb = pool.sb
        w2 = sb("a_w2", [32, 2, 256], F32)
        a2 = sb("a_a2", [32, 2, 256], F32)
        w0 = sb("a_w0", [1, 2, 256], F32)
        a0 = sb("a_a0", [1, 2, 256], F32)
        a0T = sb("a_a0T", [128, 2, 2], F32)
        kaT = sb("a_kaT", [128, 2], F32)
        kabc_f = pool.sb("a_kabc", [128, 256], F32)
        rkbc_f = pool.sb("a_rkbc", [128, 256], F32)
        r4 = lambda ap: ap.rearrange("p (h t) -> p h t", h=4)
        r2 = lambda ap: ap.rearrange("p (c t) -> p c t", c=2)
        bc4 = lambda ap: ap.unsqueeze(2).broadcast_to([64, 4, 64])
        P.dma("sync", w2[:], g.rwkv_w2[l].rearrange("d r n -> r d n"))
        P.dma("sync", a2[:], g.rwkv_a2[l].rearrange("d r n -> r d n"))
        P.dma("sync", w0[:], g.rwkv_w0[l:l + 1])
        P.dma("sync", a0[:], g.rwkv_a0[l:l + 1])
        P.dma("sync", a0T[:], g.a0T[:, l])
        P.dma("sync", kaT[:], g.kaT[:, l])
        P.dma("sync", kabc_f[:], g.rwkv_ka[l:l + 1, :].broadcast_to([128, 256]))
        P.dma("sync", rkbc_f[:], g.rwkv_rk[l:l + 1, :].broadcast_to([128, 256]))

        def chain(d):
            pb = PB
            _sb = pool.sb

            def sb(name, shape, dt):
                if shape[0] == 64:
                    return H(_sb(name, [128] + list(shape[1:]), dt), pb)
                return _sb(name, shape, dt)
            kabc, rkbc = H(kabc_f, pb), H(rkbc_f, pb)
            tri = H(g.triD, pb)
            identh = g.identD[pb:pb + 64, :]
            identbc = identh.unsqueeze(1).broadcast_to([64, 4, 64])
            psAF = psAF_shared
            psW, psM1, psM2, psM3 = H(psW_f, pb), H(psM1_f, pb), H(psM2_f, pb), H(psM3_f, pb)
            psI1, psI2 = H(psI1_f, pb), H(psI2_f, pb)
            psSuf = psW
            pWr = psW[:, 0:256]
            pAtm = pWr
            pY = r4(psW[:, 0:256])
            pAT_, pCI, pCE, pSn = r2(psAF[:, 0:128]), r2(psAF[:, 128:256]), r2(psAF[:, 256:384]), r2(psAF[:, 384:512])
            pSufR = psW[:, 256:512]
            pKb, pXm = r4(psM1[:, 0:256]), r4(psM1[:, 256:512])
            pKk, pRk = r4(psM2[:, 0:256]), r4(psM2[:, 256:512])
            pRb, pRU = r4(psM3[:, 0:256]), r4(psM3[:, 256:512])
            pU = r4(psI2[:, 256:512])
            F = sb("a_F", [128, 6, 64], F32)
            TM = sb("a_TM", [64, 1024], F32)
            tw = sb("a_tw", [32, 64], F32)
            ta = sb("a_ta", [32, 64], F32)
            Lw = sb("a_Lw", [64, 256], F32)
            atm = sb("a_atm", [64, 256], F32)
            aT = sb("a_aT", [128, 2, 64], F32)
            v2 = lambda ap: ap.rearrange("p (c t) -> p c t", c=2)
            CD = sb("a_CD", [128, 2, 2, 128], F32)
            E4 = sb("a_E4", [128, 2, 2, 128], F32)
            cumI, cumE = v2(CD[:, 1, 0, :]), v2(CD[:, 1, 1, :])
            Erg = v2(E4[:, 1, 0, :])
            Esrc = sb("a_Esrc", [128, 128], F32)
            O4 = sb("a_O4", [128, 2, 2, 128], BF16)
            rx, kkx, rg, kkg = v2(O4[:, 0, 0, :]), v2(O4[:, 0, 1, :]), v2(O4[:, 1, 0, :]), v2(O4[:, 1, 1, :])
            KB = sb("a_KB", [128, 2, 128], F32)
            kdT, bT = v2(KB[:, 0, :]), v2(KB[:, 1, :])
            KBx = sb("a_KBx", [128, 2, 128], BF16)
            kdx, bx = v2(KBx[:, 0, :]), v2(KBx[:, 1, :])
            YX = sb("a_YX", [64, 2, 4, 64], BF16)
            AR = sb("a_AR", [64, 2, 4, 64], BF16)
            Y, X, AkkT, ArkT = YX[:, 0], YX[:, 1], AR[:, 0], AR[:, 1]
            Esuf = sb("a_Esuf", [64, 256], F32)
            tf = sb("a_tf", [128, 2, 64], F32)
            tt_ = sb("a_tt", [64, 256], F32)
            kdtm = sb("a_kdtm", [64, 256], F32)
            kdst = sb("a_kdst", [64, 256], BF16)
            nbst = sb("a_nbst", [64, 256], BF16)
            bon4 = sb("a_bon4", [64, 4], F32)
            bonus = sb("a_bonus", [64, 256], F32)
            nArbT = sb("a_nArbT", [64, 4, 64], BF16)
            scr = dict(PPT=[sb("a_PPT0", [64, 8, 64], BF16), sb("a_PPT1", [64, 8, 64], BF16)],
                       Z=sb("a_Z", [64, 4, 64], BF16))
            scr["psI1"], scr["psI2"] = psI1, psI2
            RUs = sb("a_RUs", [64, 4, 64], BF16)
            Us = sb("a_Us", [64, 4, 64], BF16)
            Sst = sb("a_S", [128, 2, 64], F32)
            Sbf = sb("a_Sbf", [128, 2, 64], BF16)
            vbf = sb("a_vbf", [64, 256], BF16)
            osb = sb("a_osb", [64, 512], F32)
            rtm, ktm, vtm, kktm = TM[:, 0:256], TM[:, 256:512], TM[:, 512:768], TM[:, 768:1024]
            incl = tri[:, 2 * d, :]
            after = tri[:, 2 * d + 1, :]
            before = tri[:, 2 * (1 - d) + 1, :]
            inclbc = incl.unsqueeze(1).broadcast_to([64, 4, 64])
            afterbc = after.unsqueeze(1).broadcast_to([64, 4, 64])
            beforebc = before.unsqueeze(1).broadcast_to([64, 4, 64])
            last = 63 if d == 0 else 0
            ref = 32 if d == 0 else 31
            V(P, lambda e: e.memset(Sst[:], 0.0), [], [Sst])
            V(P, lambda e: e.memset(Sbf[:], 0.0), [], [Sbf])
            for n in chunk_order(d):
                ts = slice(n * 64, (n + 1) * 64)
                P.dma("sync", F[:], g.aF[:, :, ts].rearrange("w p t -> p w t"), reads=[], writes=[F])
                P.dma("gpsimd", TM[:], g.aTM[ts, :], reads=[], writes=[TM])
                P.dma("sync", tw[:], g.asm[32 * d:32 * d + 32, ts], reads=[], writes=[tw])
                P.dma("sync", ta[:], g.asm[64 + 32 * d:96 + 32 * d, ts], reads=[], writes=[ta])
                fh = lambda t, w_, h: t[(h % 2) * 64:(h % 2) * 64 + 64, w_ + h // 2, :]
                h2 = lambda t, h: t[(h % 2) * 64:(h % 2) * 64 + 64, h // 2, :]
                hs = lambda h: slice(h * 64, (h + 1) * 64)
                S(P, lambda e: e.activation(out=vbf[:], in_=vtm, func=AF.Copy), [TM], [vbf])
                yield
                MM(P, pWr, tw[:], w2[:, d, :], start=True, stop=False, reads=[tw, w2], writes=[psW])
                MM(P, pWr, g.ones1[:], w0[:, d, :], start=False, stop=True, reads=[g.ones1, w0], writes=[psW])
                S(P, lambda e: e.activation(out=Lw[:], in_=pWr, func=AF.Sigmoid), [psW], [Lw])
                MM(P, pAtm, ta[:], a2[:, d, :], start=True, stop=False, reads=[ta, a2], writes=[psW])
                MM(P, pAtm, g.ones1[:], a0[:, d, :], start=False, stop=True, reads=[g.ones1, a0], writes=[psW])
                S(P, lambda e: e.activation(out=atm[:], in_=pAtm, func=AF.Sigmoid), [psW], [atm])
                yield
                for c in range(2):
                    MM(P, pAT_[:, c, :], a2[:, d, c * 128:(c + 1) * 128], ta[:], reads=[a2, ta], writes=[psAF])
                for c in range(2):
                    S(P, lambda e: e.activation(out=aT[:, c, :], in_=pAT_[:, c, :], func=AF.Sigmoid, bias=a0T[:, d, c:c + 1]),
                      [psAF, a0T], [aT])
                yield
                for c in range(2):
                    MM(P, pCI[:, c, :], Lw[:, c * 128:(c + 1) * 128], incl, reads=[Lw, g.tri], writes=[psAF])
                for c in range(2):
                    MM(P, pCE[:, c, :], Lw[:, c * 128:(c + 1) * 128], before, reads=[Lw, g.tri], writes=[psAF])
                MM(P, pSufR, after, Lw[:], reads=[g.tri, Lw], writes=[psSuf])
                S(P, lambda e: e.activation(out=CD[:, 1, :, :], in_=psAF[:, 128:384].rearrange("p (a n) -> p a n", a=2), func=AF.Copy), [psAF], [CD])
                S(P, lambda e: e.activation(out=Esuf[:], in_=pSufR, func=AF.Exp, scale=-CW), [psSuf], [Esuf])
                for c in range(2):
                    V(P, lambda e: e.tensor_scalar(out=CD[:, 0, :, c * 64:(c + 1) * 64], in0=CD[:, 1, :, c * 64:(c + 1) * 64],
                                                   scalar1=cumI[:, c, ref:ref + 1], scalar2=None, op0=ALU.subtract), [CD], [CD])
                S(P, lambda e: e.activation(out=E4[:], in_=CD[:], func=AF.Exp, scale=-CW), [CD], [E4])
                S(P, lambda e: e.activation(out=Esrc[:], in_=CD[:, 0, 0, :], func=AF.Exp, scale=CW), [CD], [Esrc])
                for c in range(2):
                    V(P, lambda e: e.tensor_scalar(out=tf[:, c, :], in0=aT[:, c, :], scalar1=-1.0, scalar2=kaT[:, c:c + 1], op0=ALU.add,
                                                   op1=ALU.mult), [aT, kaT], [tf])
                V(P, lambda e: e.scalar_tensor_tensor(out=kdT[:], in0=tf[:], scalar=1.0, in1=F[:, 2:4, :], op0=ALU.add, op1=ALU.mult),
                  [tf, F], [kdT])
                V(P, lambda e: e.tensor_tensor(out=bT[:], in0=F[:, 4:6, :], in1=aT[:], op=ALU.mult), [F, aT], [bT])
                V(P, lambda e: e.tensor_tensor(out=O4[:], in0=E4[:],
                                               in1=F[:].rearrange("p (w c) t -> p w (c t)", w=3)[:, 0:3:2, :].unsqueeze(1).broadcast_to([128, 2, 2, 128]),
                                               op=ALU.mult), [E4, F], [O4])
                V(P, lambda e: e.tensor_tensor(out=KBx[:], in0=KB[:], in1=Esrc[:].unsqueeze(1).broadcast_to([128, 2, 128]), op=ALU.mult),
                  [KB, Esrc], [KBx])
                V(P, lambda e: e.scalar_tensor_tensor(out=tt_[:], in0=atm[:], scalar=-1.0, in1=kabc[:], op0=ALU.add, op1=ALU.mult),
                  [atm, kabc], [tt_])
                V(P, lambda e: e.scalar_tensor_tensor(out=kdtm[:], in0=tt_[:], scalar=1.0, in1=ktm, op0=ALU.add, op1=ALU.mult),
                  [tt_, TM], [kdtm])
                V(P, lambda e: e.tensor_tensor(out=kdst[:], in0=kdtm[:], in1=Esuf[:], op=ALU.mult), [kdtm, Esuf], [kdst])
                V(P, lambda e: e.tensor_tensor(out=tt_[:], in0=kktm, in1=atm[:], op=ALU.mult), [TM, atm], [tt_])
                V(P, lambda e: e.scalar_tensor_tensor(out=nbst[:], in0=tt_[:], scalar=-1.0, in1=Esuf[:], op0=ALU.mult, op1=ALU.mult),
                  [tt_, Esuf], [nbst])
                V(P, lambda e: e.tensor_tensor(out=tt_[:], in0=rtm, in1=kdtm[:], op=ALU.mult), [TM, kdtm, nbst], [tt_])
                V(P, lambda e: e.tensor_tensor(out=tt_[:], in0=tt_[:], in1=rkbc[:], op=ALU.mult), [tt_, rkbc], [tt_])
                V(P, lambda e: e.tensor_reduce(out=bon4[:], in_=tt_[:].rearrange("p (h d) -> p h d", h=4), axis=AX.X, op=ALU.add),
                  [tt_], [bon4])
                V(P, lambda e: e.tensor_tensor(out=bonus[:].rearrange("p (h d) -> p h d", h=4), in0=vtm.rearrange("p (h d) -> p h d", h=4),
                                               in1=bc4(bon4[:]), op=ALU.mult), [TM, bon4], [bonus])
                yield
                for h in (0, 2, 1, 3):
                    MM(P, pKb[:, h, :], h2(bx, h), h2(kkx, h), reads=[bx, kkx], writes=[psM1])
                for h in (0, 2, 1, 3):
                    MM(P, pXm[:, h, :], h2(kkx, h), h2(bx, h), reads=[bx, kkx], writes=[psM1])
                for h in (0, 2, 1, 3):
                    MM(P, pKk[:, h, :], h2(kdx, h), h2(kkx, h), reads=[kdx, kkx], writes=[psM2])
                for h in (0, 2, 1, 3):
                    MM(P, pRk[:, h, :], h2(kdx, h), h2(rx, h), reads=[kdx, rx], writes=[psM2])
                for h in (0, 2, 1, 3):
                    MM(P, pRb[:, h, :], h2(bx, h), h2(rx, h), reads=[bx, rx], writes=[psM3])
                V(P, lambda e: e.tensor_tensor(out=Y, in0=pKb, in1=beforebc, op=ALU.mult), [psM1, g.tri], [Y])
                V(P, lambda e: e.tensor_tensor(out=X, in0=pXm, in1=afterbc, op=ALU.mult), [psM1, g.tri], [X])
                V(P, lambda e: e.tensor_tensor(out=AkkT, in0=pKk, in1=beforebc, op=ALU.mult), [psM2, g.tri], [AkkT])
                V(P, lambda e: e.tensor_tensor(out=ArkT, in0=pRk, in1=inclbc, op=ALU.mult), [psM2, g.tri], [ArkT])
                V(P, lambda e: e.scalar_tensor_tensor(out=nArbT[:], in0=pRb, scalar=-1.0, in1=inclbc, op0=ALU.mult, op1=ALU.mult),
                  [psM3, g.tri], [nArbT])
                yield
                yield from neumann_inverse(g, scr, Y, X, identbc)
                Z = scr["Z"]
                yield
                for h in (0, 2, 1, 3):
                    hb, c = (h % 2) * 64, h // 2
                    MM(P, pRU[:, h, :], h2(kkg, h), Sbf[hb:hb + 64, c, :], start=True, stop=False, reads=[kkg, Sbf], writes=[psM3])
                    MM(P, pRU[:, h, :], AkkT[:, h, :], vbf[:, hs(h)], start=False, stop=True, reads=[AkkT, vbf], writes=[psM3])
                S(P, lambda e: e.activation(out=RUs[:], in_=pRU, func=AF.Copy), [psM3], [RUs])
                yield
                for h in range(4):
                    MM(P, pU[:, h, :], Z[:, h, :], RUs[:, h, :], reads=[Z, RUs], writes=[psI2])
                S(P, lambda e: e.activation(out=Us[:], in_=pU, func=AF.Copy), [psI2], [Us])
                yield
                for h in (0, 2, 1, 3):
                    hb, c = (h % 2) * 64, h // 2
                    MM(P, pY[:, h, :], h2(rg, h), Sbf[hb:hb + 64, c, :], start=True, stop=False, reads=[rg, Sbf], writes=[psW])
                    MM(P, pY[:, h, :], ArkT[:, h, :], vbf[:, hs(h)], start=False, stop=False, reads=[ArkT, vbf], writes=[psW])
                    MM(P, pY[:, h, :], nArbT[:, h, :], Us[:, h, :], start=False, stop=True, reads=[nArbT, Us], writes=[psW])
                for h in (0, 2, 1, 3):
                    hb, c = (h % 2) * 64, h // 2
                    MM(P, pSn[hb:hb + 64, c, :], kdst[:, hs(h)], vbf[:, hs(h)], start=True, stop=False, reads=[kdst, vbf], writes=[psAF])
                    MM(P, pSn[hb:hb + 64, c, :], nbst[:, hs(h)], Us[:, h, :], start=False, stop=True, reads=[nbst, Us], writes=[psAF])
                for c in range(2):
                    V(P, lambda e: e.scalar_tensor_tensor(out=Sst[:, c, :], in0=Sst[:, c, :], scalar=Erg[:, c, last:last + 1],
                                                          in1=pSn[:, c, :], op0=ALU.mult, op1=ALU.add), [Sst, Erg, psAF], [Sst])
                S(P, lambda e: e.activation(out=Sbf[:], in_=Sst[:], func=AF.Copy), [Sst], [Sbf])
                S(P, lambda e: e.activation(out=osb[:, 0:256], in_=psW[:, 0:256], func=AF.Copy), [psW], [osb])
                V(P, lambda e: e.tensor_copy(out=osb[:, 256:512], in_=bonus[:]), [bonus], [osb])
                P.dma("sync", g.ofw2[d][ts, :], osb[:], reads=[osb], writes=[("ofw", d)])
                yield

        return [chain(0), chain(1)]


def mixer_A_finish(g, l):
    nc, P = g.nc, g.P
    with Pool(nc) as pool:
        g2 = pool.sb("a_g2", [64, 256], F32)
        lngbc = pool.sb("a_lng", [64, 256], F32)
        lnbbc = pool.sb("a_lnb", [64, 256], F32)
        psSuf_2 = [pool.ps("a_fpS_a", [64, 512], F32), pool.ps("a_fpS_b", [64, 512], F32)]
        psAF_2 = [pool.ps("a_fpAF_a", [128, 512], F32), pool.ps("a_fpAF_b", [128, 512], F32)]
        bc4 = lambda ap: ap.unsqueeze(2).broadcast_to([64, 4, 64])
        P.dma("sync", g2[:], g.rwkv_g2[l])
        P.dma("sync", lngbc[:], g.rwkv_ln_g[l:l + 1, :].broadcast_to([64, 256]))
        P.dma("sync", lnbbc[:], g.rwkv_ln_b[l:l + 1, :].broadcast_to([64, 256]))
        osb_2 = [pool.sb("a_fo_a", [64, 512], F32), pool.sb("a_fo_b", [64, 512], F32)]
        ofl_2 = [pool.sb("a_fl_a", [64, 512], F32), pool.sb("a_fl_b", [64, 512], F32)]
        bonus_2 = [pool.sb("a_fbon_a", [64, 256], F32), pool.sb("a_fbon_b", [64, 256], F32)]
        sgT_2 = [pool.sb("a_fsgT_a", [64, 64], F32), pool.sb("a_fsgT_b", [64, 64], F32)]
        ysq_2 = [pool.sb("a_fysq_a", [64, 256], F32), pool.sb("a_fysq_b", [64, 256], F32)]
        yc_2 = [pool.sb("a_fyc_a", [64, 256], F32), pool.sb("a_fyc_b", [64, 256], F32)]
        ss4_2 = [pool.sb("a_fss4_a", [64, 4], F32), pool.sb("a_fss4_b", [64, 4], F32)]
        mean4_2 = [pool.sb("a_fmean4_a", [64, 4], F32), pool.sb("a_fmean4_b", [64, 4], F32)]
        gsb_2 = [pool.sb("a_fgsb_a", [64, 256], F32), pool.sb("a_fgsb_b", [64, 256], F32)]
        yTs_2 = [pool.sb("a_fyTs_a", [128, 2, 64], BF16), pool.sb("a_fyTs_b", [128, 2, 64], BF16)]
        for n in range(68):
            osb, ofl, bonus, sgT, ysq, yc, ss4, mean4, gsb, yTs, psSuf, psAF = osb_2[n % 2], ofl_2[n % 2], bonus_2[n % 2], sgT_2[n % 2], ysq_2[n % 2], yc_2[n % 2], ss4_2[n % 2], mean4_2[n % 2], gsb_2[n % 2], yTs_2[n % 2], psSuf_2[n % 2], psAF_2[n % 2]
            pGate = psSuf[:, 256:512]
            pTr = psAF[:, 0:128].rearrange("p (c t) -> p c t", c=2)
            ts = slice(n * 64, (n + 1) * 64)
            P.dma("sync", osb[:], g.ofw2[0][ts, :], reads=[], writes=[osb])
            P.dma("gpsimd", ofl[:], g.ofw2[1][ts, :], reads=[], writes=[ofl])
            P.dma("sync", sgT[:], g.asg[:, ts], reads=[], writes=[sgT])
            V(P, lambda e: e.tensor_tensor(out=osb[:], in0=osb[:], in1=ofl[:], op=ALU.add), [osb, ofl], [osb])
            V(P, lambda e: e.tensor_copy(out=bonus[:], in_=osb[:, 256:512]), [osb], [bonus])
            MM(P, pGate, sgT[:], g2[:], reads=[sgT, g2], writes=[psSuf])
            S(P, lambda e: e.activation(out=gsb[:], in_=pGate, func=AF.Copy), [psSuf], [gsb])
            y = osb[:, 0:256]
            y4 = y.rearrange("p (h d) -> p h d", h=4)
            V(P, lambda e: e.tensor_reduce(out=mean4[:], in_=y4, axis=AX.X, op=ALU.add), [osb], [mean4])
            V(P, lambda e: e.tensor_scalar(out=mean4[:], in0=mean4[:], scalar1=1.0 / 64, scalar2=None, op0=ALU.mult), [mean4], [mean4])
            V(P, lambda e: e.tensor_tensor(out=yc[:].rearrange("p (h d) -> p h d", h=4), in0=y4, in1=bc4(mean4[:]),
                                           op=ALU.subtract), [osb, mean4], [yc])
            V(P, lambda e: e.tensor_tensor(out=ysq[:], in0=yc[:], in1=yc[:], op=ALU.mult), [yc], [ysq])
            V(P, lambda e: e.tensor_reduce(out=ss4[:], in_=ysq[:].rearrange("p (h d) -> p h d", h=4), axis=AX.X, op=ALU.add),
              [ysq], [ss4])
            S(P, lambda e: e.activation(out=ss4[:], in_=ss4[:], func=AF.Sqrt, bias=64e-5, scale=1.0 / 64), [ss4], [ss4])
            V(P, lambda e: e.reciprocal(out=ss4[:], in_=ss4[:]), [ss4], [ss4])
            V(P, lambda e: e.tensor_tensor(out=ysq[:].rearrange("p (h d) -> p h d", h=4), in0=yc[:].rearrange("p (h d) -> p h d", h=4),
                                           in1=bc4(ss4[:]), op=ALU.mult), [yc, ss4], [ysq])
            V(P, lambda e: e.tensor_tensor(out=ysq[:], in0=ysq[:], in1=lngbc[:], op=ALU.mult), [ysq, lngbc], [ysq])
            V(P, lambda e: e.tensor_tensor(out=ysq[:], in0=ysq[:], in1=lnbbc[:], op=ALU.add), [ysq, lnbbc], [ysq])
            V(P, lambda e: e.tensor_tensor(out=ysq[:], in0=ysq[:], in1=bonus[:], op=ALU.add), [ysq, bonus], [ysq])
            V(P, lambda e: e.tensor_tensor(out=ysq[:], in0=ysq[:], in1=gsb[:], op=ALU.mult), [ysq, gsb], [ysq])
            for c2 in range(2):
                TR(P, pTr[:, c2, :], ysq[:, c2 * 128:(c2 + 1) * 128], g.ident[0:64, 0:64], reads=[ysq, g.ident], writes=[psAF])
            S(P, lambda e: e.activation(out=yTs[:], in_=pTr, func=AF.Copy), [psAF], [yTs])
            P.dma("sync", g.yT[0][:, :, ts], yTs[:], reads=[yTs], writes=[("yT", 0)])
    P.barrier()


def mixers_AB_sweeps(g, l):
    nc, P = g.nc, g.P
    with Pool(nc) as pool:
        banks = [pool.ps("ab_bank%d" % i, [128, 512], F32) for i in range(6)]
        psAF = pool.ps("ab_pAF", [128, 512], F32)
        psV = pool.ps("ab_pV", [128, 512], F32)
        ga = mixer_A_chains(g, l, pool, banks, psAF, 0)
        gb = mixer_B_chains(g, l, pool, banks, psV, 64)
        gens = [ga[0], gb[0], ga[1], gb[1]]
        if os.environ.get("KSEQ", ""):
            for gen in gens:
                for _ in gen:
                    pass
            gens = []
        while gens:
            for gen in list(gens):
                try:
                    next(gen)
                except StopIteration:
                    gens.remove(gen)
        P.barrier()


def chunk_order(d):
    lim = int(os.environ.get("KNCH", "68"))
    if d == 0:
        return list(range(68))[:lim]
    return ([3, 2, 1, 0] + list(range(67, 3, -1)))[:lim]


def mixer_C(g, l):
    nc, P = g.nc, g.P
    wm = g.wmix
    SC = 32.0 ** -0.5
    with Pool(nc) as pool:
        gw2 = pool.sb("c_gw2", [16, 2, 256], F32)
        gb = pool.sb("c_gb", [1, 2, 256], F32)
        psTM_f = pool.ps("c_pTM", [128, 512], F32)
        psP_f = pool.ps("c_pP", [128, 512], F32)
        psA_f = pool.ps("c_pA", [128, 512], F32)
        psF2 = [pool.ps("c_pF0", [128, 512], F32), pool.ps("c_pF1", [128, 512], F32)]
        psC2 = [pool.ps("c_pC0", [128, 512], F32), pool.ps("c_pC1", [128, 512], F32)]
        NCOL = 1056
        P.dma("gpsimd", wm[:, :, 0:NCOL], g.w_C[l].rearrange("(kc p) n -> p kc n", p=128))
        P.dma("sync", gw2[:], g.gw2P[l].rearrange("d r n -> r d n"))
        P.dma("sync", gb[:], g.gbP[l:l + 1])
        def chain(d):
            pb = 64 * d

            def tsb(name, shape, dt):
                if shape[0] == 64:
                    return H(pool.sb(name, [128] + list(shape[1:]), dt), pb)
                return pool.sb(name, shape, dt)
            tri = H(g.triD, pb)
            psF, psC = psF2[d], psC2[d]
            psTM, psP, psA = H(psTM_f, pb), H(psP_f, pb), H(psA_f, pb)
            pQ = psF[:, 0:128].rearrange("p (c t) -> p c t", c=2)
            pK = psF[:, 128:256].rearrange("p (c t) -> p c t", c=2)
            pXG = psF[0:16, 256:320]
            pPre = psP[:, 0:256]
            pSuf = psP[:, 256:512]
            pCum = psC[:, 0:128].rearrange("p (c t) -> p c t", c=2)
            pS = psC[:, 128:256].rearrange("p (c t) -> p c t", c=2)
            pAT = psA[:, 0:256].rearrange("p (h t) -> p h t", h=4)
            pO = psA[:, 256:512]
            xg_s = tsb("c_xg", [16, 64], F32)
            ktm_s = tsb("c_ktm", [64, 256], F32)
            qk_s = tsb("c_qks", [128, 256], F32)
            e1 = tsb("c_e1", [64, 256], F32)
            Lg = tsb("c_Lg", [64, 256], F32)
            cum = tsb("c_cum", [128, 2, 64], F32)
            dif = tsb("c_dif", [128, 2, 64], F32)
            Eq = tsb("c_Eq", [128, 2, 64], F32)
            Ek = tsb("c_Ek", [128, 2, 64], F32)
            Ein = tsb("c_Ein", [128, 2, 64], F32)
            Es = tsb("c_Es", [64, 256], F32)
            qx = tsb("c_qx", [128, 2, 64], BF16)
            kx = tsb("c_kx", [128, 2, 64], BF16)
            qin = tsb("c_qin", [128, 2, 64], BF16)
            ATm = tsb("c_ATm", [64, 4, 64], BF16)
            kst = tsb("c_kst", [64, 256], BF16)
            vs = tsb("c_vs", [64, 256], BF16)
            Sst = tsb("c_S", [128, 2, 64], F32)
            Sbf = tsb("c_Sbf", [128, 2, 64], BF16)
            osb = tsb("c_osb", [64, 256], F32)
            incl = tri[:, 2 * d, :]
            after = tri[:, 2 * d + 1, :]
            last = 63 if d == 0 else 0
            ref = 32 if d == 0 else 31
            V(P, lambda e: e.memset(Sst[:], 0.0), [], [Sst])
            V(P, lambda e: e.memset(Sbf[:], 0.0), [], [Sbf])
            for n in chunk_order(d):
                ts = slice(n * 64, (n + 1) * 64)
                for kc in range(8):
                    MM(P, psTM[:], g.hT[:, kc, ts], wm[:, kc, 256:768], start=(kc == 0), stop=(kc == 7), reads=[wm, g.hT])
                for c in range(2):
                    for kc in range(8):
                        MM(P, pQ[:, c, :], wm[:, kc, c * 128:(c + 1) * 128], g.hT[:, kc, ts], start=(kc == 0), stop=(kc == 7),
                           reads=[wm, g.hT], writes=[psF])
                for c in range(2):
                    for kc in range(8):
                        MM(P, pK[:, c, :], wm[:, kc, 256 + c * 128:256 + (c + 1) * 128], g.hT[:, kc, ts], start=(kc == 0),
                           stop=(kc == 7), reads=[wm, g.hT], writes=[psF])
                for kc in range(8):
                    MM(P, pXG, wm[:, kc, 1024 + 16 * d:1040 + 16 * d], g.hT[:, kc, ts], start=(kc == 0), stop=(kc == 7),
                       reads=[wm, g.hT], writes=[psF])
                S(P, lambda e: e.activation(out=xg_s[:], in_=pXG, func=AF.Copy), [psF], [xg_s])
                S(P, lambda e: e.activation(out=qk_s[:], in_=psF[:, 0:256], func=AF.Copy), [psF], [qk_s])
                S(P, lambda e: e.activation(out=ktm_s[:], in_=psTM[:, 0:256], func=AF.Copy), [psTM], [ktm_s])
                S(P, lambda e: e.activation(out=vs[:], in_=psTM[:, 256:512], func=AF.Copy), [psTM], [vs])
                yield
                MM(P, pPre, xg_s[:], gw2[:, d, :], start=True, stop=False, reads=[xg_s, gw2], writes=[psP])
                MM(P, pPre, g.ones1[:], gb[:, d, :], start=False, stop=True, reads=[g.ones1, gb], writes=[psP])
                S(P, lambda e: e.activation(out=e1[:], in_=pPre, func=AF.Exp, scale=-1.0), [psP], [e1])
                S(P, lambda e: e.activation(out=Lg[:], in_=e1[:], func=AF.Ln, bias=1.0), [e1], [Lg])
                yield
                for c in range(2):
                    MM(P, pCum[:, c, :], Lg[:, c * 128:(c + 1) * 128], incl, reads=[Lg, g.tri], writes=[psC])
                MM(P, pSuf, after, Lg[:], reads=[Lg, g.tri], writes=[psP])
                S(P, lambda e: e.activation(out=cum[:], in_=pCum, func=AF.Copy), [psC], [cum])
                S(P, lambda e: e.activation(out=Es[:], in_=pSuf, func=AF.Exp, scale=-1.0 / 16), [psP], [Es])
                for c in range(2):
                    V(P, lambda e, c=c: e.tensor_scalar(out=dif[:, c, :], in0=cum[:, c, :], scalar1=cum[:, c, ref:ref + 1],
                                                        scalar2=None, op0=ALU.subtract), [cum], [dif])
                S(P, lambda e: e.activation(out=Eq[:], in_=dif[:], func=AF.Exp, scale=-1.0 / 16), [dif], [Eq])
                S(P, lambda e: e.activation(out=Ek[:], in_=dif[:], func=AF.Exp, scale=1.0 / 16), [dif], [Ek])
                S(P, lambda e: e.activation(out=Ein[:], in_=cum[:], func=AF.Exp, scale=-1.0 / 16), [cum], [Ein])
                V(P, lambda e: e.scalar_tensor_tensor(out=qx[:], in0=qk_s[:, 0:128].rearrange("p (c t) -> p c t", c=2), scalar=SC, in1=Eq[:], op0=ALU.mult, op1=ALU.mult),
                  [qk_s, Eq], [qx])
                V(P, lambda e: e.scalar_tensor_tensor(out=qin[:], in0=qk_s[:, 0:128].rearrange("p (c t) -> p c t", c=2), scalar=SC, in1=Ein[:], op0=ALU.mult, op1=ALU.mult),
                  [qk_s, Ein], [qin])
                V(P, lambda e: e.tensor_tensor(out=kx[:], in0=qk_s[:, 128:256].rearrange("p (c t) -> p c t", c=2), in1=Ek[:], op=ALU.mult), [qk_s, Ek], [kx])
                yield
                for h in (0, 2, 1, 3):
                    c, hb = h // 2, (h % 2) * 64
                    MM(P, pAT[:, h, :], kx[hb:hb + 64, c, :], qx[hb:hb + 64, c, :], reads=[kx, qx], writes=[psA])
                V(P, lambda e: e.tensor_tensor(out=ATm[:], in0=pAT, in1=incl.unsqueeze(1).broadcast_to([64, 4, 64]),
                                               op=ALU.mult), [psA, g.tri], [ATm])
                V(P, lambda e: e.tensor_tensor(out=kst[:], in0=ktm_s[:], in1=Es[:], op=ALU.mult), [ktm_s, Es], [kst])
                yield
                for h in (0, 2, 1, 3):
                    c, hb = h // 2, (h % 2) * 64
                    hs = slice(h * 64, (h + 1) * 64)
                    MM(P, pO[:, hs], qin[hb:hb + 64, c, :], Sbf[hb:hb + 64, c, :], start=True, stop=False,
                       reads=[qin, Sbf], writes=[psA])
                    MM(P, pO[:, hs], ATm[:, h, :], vs[:, hs], start=False, stop=True, reads=[ATm, vs], writes=[psA])
                for h in (0, 2, 1, 3):
                    c, hb = h // 2, (h % 2) * 64
                    hs = slice(h * 64, (h + 1) * 64)
                    MM(P, pS[hb:hb + 64, c, :], kst[:, hs], vs[:, hs], reads=[kst, vs], writes=[psC])
                for c in range(2):
                    V(P, lambda e, c=c: e.scalar_tensor_tensor(out=Sst[:, c, :], in0=Sst[:, c, :], scalar=Ein[:, c, last:last + 1],
                                                               in1=pS[:, c, :], op0=ALU.mult, op1=ALU.add),
                      [Sst, Ein, psC], [Sst])
                S(P, lambda e: e.activation(out=Sbf[:], in_=Sst[:], func=AF.Copy), [Sst], [Sbf])
                S(P, lambda e: e.activation(out=osb[:], in_=pO, func=AF.Copy), [psA], [osb])
                P.dma("sync", g.ofw2[d][ts, 0:256], osb[:], reads=[osb], writes=[("ofw", d)])

        gens = [chain(0), chain(1)]
        if 'C' in os.environ.get('KSEQ', ''):
            for gen in gens:
                for _ in gen:
                    pass
            gens = []
        while gens:
            for gen in list(gens):
                try:
                    next(gen)
                except StopIteration:
                    gens.remove(gen)
        P.barrier()
    with Pool(nc) as pool:
        ngbc = pool.sb("c_ng", [64, 256], F32)
        psG_2 = [pool.ps("c_pG_a", [64, 512], F32), pool.ps("c_pG_b", [64, 512], F32)]
        pt = pool.ps("c_pt", [128, 4, 128], F32)
        P.dma("sync", ngbc[:], g.ngC[l:l + 1, :].broadcast_to([64, 256]))
        osb_2 = [pool.sb("c_fo_a", [64, 256], F32), pool.sb("c_fo_b", [64, 256], F32)]
        ofl_2 = [pool.sb("c_fl_a", [64, 256], F32), pool.sb("c_fl_b", [64, 256], F32)]
        ysq_2 = [pool.sb("c_fysq_a", [64, 256], F32), pool.sb("c_fysq_b", [64, 256], F32)]
        ss4_2 = [pool.sb("c_fss4_a", [64, 4], F32), pool.sb("c_fss4_b", [64, 4], F32)]
        sg_2 = [pool.sb("c_fsg_a", [64, 256], F32), pool.sb("c_fsg_b", [64, 256], F32)]
        yTs_2 = [pool.sb("c_fyTs_a", [128, 2, 64], BF16), pool.sb("c_fyTs_b", [128, 2, 64], BF16)]
        for n in range(68):
            osb, ofl, ysq, ss4, sg, yTs, psG = osb_2[n % 2], ofl_2[n % 2], ysq_2[n % 2], ss4_2[n % 2], sg_2[n % 2], yTs_2[n % 2], psG_2[n % 2]
            pG = psG[:, 0:256]
            ts = slice(n * 64, (n + 1) * 64)
            P.dma("sync", osb[:], g.ofw2[0][ts, 0:256], reads=[], writes=[osb])
            P.dma("gpsimd", ofl[:], g.ofw2[1][ts, 0:256], reads=[], writes=[ofl])
            for kc in range(8):
                MM(P, pG, g.hT[:, kc, ts], wm[:, kc, 768:1024], start=(kc == 0), stop=(kc == 7), reads=[wm, g.hT], writes=[psG])
            V(P, lambda e: e.tensor_tensor(out=osb[:], in0=osb[:], in1=ofl[:], op=ALU.add), [osb, ofl], [osb])
            rms_gate_finish(g, osb, ysq, ss4, ngbc, pG, psG, sg, pt, yTs, 2, ts, 1e-6)
    P.barrier()


def rms_gate_finish(g, y, ysq, ss4, ngbc, pG, pGkey, sg, pt, yTs, mi, ts, eps):
    P = g.P
    V(P, lambda e: e.tensor_tensor(out=ysq[:], in0=y[:], in1=y[:], op=ALU.mult), [y], [ysq])
    V(P, lambda e: e.tensor_reduce(out=ss4[:], in_=ysq[:].rearrange("p (h d) -> p h d", h=4), axis=AX.X, op=ALU.add),
      [ysq], [ss4])
    S(P, lambda e: e.activation(out=ss4[:], in_=ss4[:], func=AF.Sqrt, bias=eps, scale=1.0 / 64), [ss4], [ss4])
    V(P, lambda e: e.reciprocal(out=ss4[:], in_=ss4[:]), [ss4], [ss4])
    V(P, lambda e: e.tensor_tensor(out=ysq[:].rearrange("p (h d) -> p h d", h=4), in0=y[:].rearrange("p (h d) -> p h d", h=4),
                                   in1=ss4[:].unsqueeze(2).broadcast_to([64, 4, 64]), op=ALU.mult), [y, ss4], [ysq])
    V(P, lambda e: e.tensor_tensor(out=ysq[:], in0=ysq[:], in1=ngbc[:], op=ALU.mult), [ysq, ngbc], [ysq])
    if pG is not None:
        S(P, lambda e: e.activation(out=sg[:], in_=pG, func=AF.Silu), [pGkey], [sg])
    V(P, lambda e: e.tensor_tensor(out=ysq[:], in0=ysq[:], in1=sg[:], op=ALU.mult), [ysq, sg], [ysq])
    for c2 in range(2):
        TR(P, pt[:, c2, 0:64], ysq[:, c2 * 128:(c2 + 1) * 128], g.ident[0:64, 0:64], reads=[ysq, g.ident], writes=[pt])
    S(P, lambda e: e.activation(out=yTs[:], in_=pt[:, 0:2, 0:64], func=AF.Copy), [pt], [yTs])
    P.dma("sync", g.yT[mi][:, :, ts], yTs[:], reads=[yTs], writes=[("yT", mi)])


def phase3_merge(g, l, xsrc, with_ctx):
    nc, P = g.nc, g.P
    with Pool(nc) as pool:
        wg = pool.sb("wg", [128, 8, 4096], BF16)
        wbr = pool.sb("wbr", [128, 8, 1024], BF16)
        wo = pool.sb("wo", [128, 8, 1024], BF16)
        yb = pool.sb("p3y", [128, 4, 2, 256], BF16)
        gts = [pool.sb("p3g0", [128, 256], F32), pool.sb("p3g1", [128, 256], F32)]
        tmps = [pool.sb("p3t0", [128, 256], F32), pool.sb("p3t1", [128, 256], F32)]
        accf = pool.sb("p3acc", [128, 8, 256], F32)
        accb = pool.sb("p3accb", [128, 8, 256], BF16)
        xt = pool.sb("p3x", [128, 1024], F32)
        ot = g.xn
        ss2 = pool.sb("p3ss", [128, 2], F32)
        pgs = [pool.ps("p3pg0", [128, 256], F32), pool.ps("p3pg1", [128, 256], F32)]
        pzs = [pool.ps("p3pz0", [128, 256], F32), pool.ps("p3pz1", [128, 256], F32)]
        pout = pool.ps("p3po", [128, 2, 512], F32)
        for kc in range(8):
            P.dma("gpsimd", wg[:, kc, :], g.w_in[l, kc * 128:(kc + 1) * 128, 3312:7408])
        P.dma("gpsimd", wbr[:], g.w_branch[l].rearrange("i (c p) n -> p (i c) n", p=128))
        P.dma("gpsimd", wo[:], g.w_out[l].rearrange("(kc p) n -> p kc n", p=128))
        for grp in range(NG):
            if grp == 0 and not with_ctx:
                continue
            j = 1 if grp == 0 else 0
            ts = slice(grp * 256, (grp + 1) * 256)
            for i in range(4):
                P.dma("sync" if i % 2 == 0 else "gpsimd", yb[:, i, :, :], g.yT[i][:, :, ts], reads=[("yT", i)], writes=[yb])
            for i in range(4):
                for fc in range(8):
                    pg, pz, gt, tmp = pgs[fc % 2], pzs[fc % 2], gts[fc % 2], tmps[fc % 2]
                    for kc in range(8):
                        MM(P, pg[:], wg[:, kc, i * 1024 + fc * 128:i * 1024 + (fc + 1) * 128], g.hT[:, kc, ts],
                           start=(kc == 0), stop=(kc == 7), reads=[wg, g.hT])
                    for c2 in range(2):
                        MM(P, pz[:], wbr[:, i * 2 + c2, fc * 128:(fc + 1) * 128], yb[:, i, c2, :], start=(c2 == 0),
                           stop=(c2 == 1), reads=[wbr, yb])
                    S(P, lambda e, i=i, fc=fc: e.activation(out=gt[:], in_=pg[:], func=AF.Sigmoid,
                                                            bias=g.gbT[:, l, i, fc:fc + 1]), [pg, g.gbT], [gt])
                    if i == 0:
                        V(P, lambda e, fc=fc: e.tensor_tensor(out=accf[:, fc, :], in0=gt[:], in1=pz[:], op=ALU.mult),
                          [gt, pz], [("accf", fc)])
                    else:
                        V(P, lambda e: e.tensor_tensor(out=tmp[:], in0=gt[:], in1=pz[:], op=ALU.mult), [gt, pz], [tmp])
                        V(P, lambda e, fc=fc, i=i: e.tensor_tensor(out=(accb if i == 3 else accf)[:, fc, :],
                                                                   in0=accf[:, fc, :], in1=tmp[:], op=ALU.add),
                          [("accf", fc), tmp], [("accf", fc), accb] if i == 3 else [("accf", fc)])
            for tt in range(2):
                ti = grp * 2 + tt
                P.dma("sync", xt[:], xsrc[ti * 128:(ti + 1) * 128, :], reads=[("x", l)], writes=[xt])
                for half in range(2):
                    if half == 0:
                        V(P, lambda e: e.memset(ss2[:], 0.0), [], [ss2])
                    for fc in range(8):
                        MM(P, pout[:, half, :], accb[:, fc, tt * 128:(tt + 1) * 128], wo[:, fc, half * 512:(half + 1) * 512],
                           start=(fc == 0), stop=(fc == 7), reads=[accb, wo], writes=[("pout", half)])
                    S(P, lambda e, half=half: e.activation(out=ot[:, half * 512:(half + 1) * 512], in_=pout[:, half, :],
                                                           func=AF.Square, accum_out=ss2[:, half:half + 1]),
                      [("pout", half)], [ot, ss2])
                residual_update(g, pout, "pout", ss2, xt, g.Gbc, j, ot)
                P.dma("sync", g.xmid[ti * 128:(ti + 1) * 128, :], xt[:], reads=[xt], writes=[("xmid", l)])
    P.barrier()


def residual_update(g, pout, pkey, ss2, xt, Gbc, j, ot):
    P = g.P
    rstd = g.rstd2
    V(P, lambda e: e.tensor_tensor(out=rstd[:], in0=ss2[:, 0:1], in1=ss2[:, 1:2], op=ALU.add), [ss2], [rstd])
    S(P, lambda e: e.activation(out=rstd[:], in_=rstd[:], func=AF.Sqrt, bias=1e-6, scale=1.0 / 1024), [rstd], [rstd])
    V(P, lambda e: e.reciprocal(out=rstd[:], in_=rstd[:]), [rstd], [rstd])
    for half in range(2):
        hs = slice(half * 512, (half + 1) * 512)
        V(P, lambda e, half=half, hs=hs: e.scalar_tensor_tensor(out=ot[:, hs], in0=pout[:, half, :], scalar=rstd[:, 0:1],
                                                               in1=Gbc[:, j, hs], op0=ALU.mult, op1=ALU.mult),
          [(pkey, half), rstd, Gbc, ot], [ot])
    V(P, lambda e: e.tensor_tensor(out=xt[:], in0=xt[:], in1=ot[:], op=ALU.add), [xt, ot], [xt])


def phase4_ffn(g, l, xdst_fn, with_ctx):
    nc, P = g.nc, g.P
    with Pool(nc) as pool:
        g.tp = [pool.ps("tpa", [128, 4, 128], F32), pool.ps("tpb", [128, 4, 128], F32)]
        w1 = pool.sb("w1", [128, 8, 2 * FH], BF16)
        w2 = pool.sb("w2", [128, 22, 1024], BF16)
        xa = pool.sb("p4x0", [128, 1024], F32)
        xb = pool.sb("p4x1", [128, 1024], F32)
        h2T = pool.sb("p4h", [128, 8, 256], BF16)
        uT = pool.sb("p4u", [128, 22, 256], BF16)
        sgs = [pool.sb("p4s0", [128, 256], F32), pool.sb("p4s1", [128, 256], F32)]
        ot = pool.sb("p4o", [128, 1024], F32)
        ss2 = pool.sb("p4ss", [128, 2], F32)
        pgs = [pool.ps("p4pg0", [128, 256], F32), pool.ps("p4pg1", [128, 256], F32)]
        pus = [pool.ps("p4pu0", [128, 256], F32), pool.ps("p4pu1", [128, 256], F32)]
        pout = pool.ps("p4po", [128, 2, 512], F32)
        for kc in range(8):
            P.dma("gpsimd", w1[:, kc, :], g.ffn_w1[l, kc * 128:(kc + 1) * 128, :], writes=[("w1", kc)])
        P.dma("gpsimd", w2[:, 0:11, :], g.ffn_w2[l, 0:1408, :].rearrange("(c p) n -> p c n", p=128), writes=[("w2", 0)])
        P.dma("gpsimd", w2[:, 11:22, :], g.ffn_w2[l, 1408:2816, :].rearrange("(c p) n -> p c n", p=128), writes=[("w2", 1)])
        w1k = [("w1", kc) for kc in range(8)]
        w2k = [("w2", 0), ("w2", 1)]
        xt2 = [xa, xb]
        for grp in range(NG):
            if grp == 0 and not with_ctx:
                continue
            j = 1 if grp == 0 else 0
            for tt in range(2):
                ti = grp * 2 + tt
                P.dma("sync", xt2[tt][:], g.xmid[ti * 128:(ti + 1) * 128, :], reads=[("xmid", l)], writes=[xt2[tt]])
                norm_transpose(g, xt2[tt], j, g.gain2, g.modT[:, 24:32, :], h2T, tt * 128, "p4")
            for hc in range(22):
                pg, pu, sg = pgs[hc % 2], pus[hc % 2], sgs[hc % 2]
                for kc in range(8):
                    MM(P, pg[:], w1[:, kc, hc * 128:(hc + 1) * 128], h2T[:, kc, :], start=(kc == 0), stop=(kc == 7),
                       reads=w1k + [h2T])
                for kc in range(8):
                    MM(P, pu[:], w1[:, kc, FH + hc * 128:FH + (hc + 1) * 128], h2T[:, kc, :], start=(kc == 0), stop=(kc == 7),
                       reads=w1k + [h2T])
                S(P, lambda e: e.activation(out=sg[:], in_=pg[:], func=AF.Silu), [pg], [sg])
                V(P, lambda e, hc=hc: e.tensor_tensor(out=uT[:, hc, :], in0=sg[:], in1=pu[:], op=ALU.mult), [sg, pu], [uT])
            for tt in range(2):
                ti = grp * 2 + tt
                xt = xt2[tt]
                for half in range(2):
                    if half == 0:
                        V(P, lambda e: e.memset(ss2[:], 0.0), [], [ss2])
                    for hc in range(22):
                        MM(P, pout[:, half, :], uT[:, hc, tt * 128:(tt + 1) * 128], w2[:, hc, half * 512:(half + 1) * 512],
                           start=(hc == 0), stop=(hc == 21), reads=w2k + [uT], writes=[("pout4", half)])
                    S(P, lambda e, half=half: e.activation(out=ot[:, half * 512:(half + 1) * 512], in_=pout[:, half, :],
                                                           func=AF.Square, accum_out=ss2[:, half:half + 1]),
                      [("pout4", half)], [ot, ss2])
                residual_update(g, pout, "pout4", ss2, xt, g.Gbc, j, ot)
                dst, dkey = xdst_fn(ti)
                if dst is not None:
                    P.dma("sync", dst, xt[:], reads=[xt], writes=[dkey])
    P.barrier()


def build_program(debug=False):
    nc = bass.Bass("TRN2", target_bir_lowering=False)
    g = Ctx()
    g.nc = nc
    g.P = P = Prog(nc)
    g.debug = debug

    def din(name, shape, dt=F32):
        return nc.dram_tensor(name, list(shape), dt, kind="ExternalInput").ap()

    g.xin = din("xin", [T, D])
    g.cvec = din("cvec", [128, 8, 2])
    g.ada_w = din("ada_w", [L, D, 6144])
    g.ada_b = din("ada_b", [L, 6144])
    g.norm_g = din("norm_g", [L, 4, D])
    g.ngT_d = din("ngT", [128, L, 4, 8])
    g.gbT_d = din("gbT", [128, L, 4, 8])
    g.w_in = din("w_in", [L, D, 7408])
    g.w_D = din("w_D", [L, D, 896])
    g.w_branch = din("w_branch", [L, 4, 256, D])
    g.w_out = din("w_out", [L, D, D])
    g.ffn_w1 = din("ffn_w1", [L, D, 2 * FH])
    g.ffn_w2 = din("ffn_w2", [L, FH, D])
    g.attn_sink = din("attn_sink", [L, 4])
    g.w_C = din("w_C", [L, D, 1056])
    g.gw2P = din("gw2P", [L, 2, 16, 256])
    g.gbP = din("gbP", [L, 2, 256])
    g.ngC = din("ngC", [L, 256])
    g.tri_d = din("tri", [64, 4, 64])
    g.convT = din("convT", [128, L, 6, 7])
    g.muT = din("muT", [128, L, 8])
    g.kkwT = din("kkwT", [128, L, 2])
    g.kaT = din("kaT", [128, L, 2])
    g.a0T = din("a0T", [128, L, 2, 2])
    g.rwkv_w2 = din("rwkv_w2", [L, 2, 32, 256])
    g.rwkv_a2 = din("rwkv_a2", [L, 2, 32, 256])
    g.rwkv_w0 = din("rwkv_w0", [L, 2, 256])
    g.rwkv_a0 = din("rwkv_a0", [L, 2, 256])
    g.rwkv_g2 = din("rwkv_g2", [L, 64, 256])
    g.rwkv_ka = din("rwkv_ka", [L, 256])
    g.rwkv_rk = din("rwkv_rk", [L, 256])
    g.rwkv_ln_g = din("rwkv_ln_g", [L, 256])
    g.rwkv_ln_b = din("rwkv_ln_b", [L, 256])
    g.blk_d = din("blk", [128, 128])
    g.gdn_dt_bias = din("gdn_dt_bias", [L, 2, 4])
    g.gdn_a_log = din("gdn_a_log", [L, 2, 4])
    g.ngB = din("ngB", [L, 256])
    g.ident_d = din("ident", [128, 128])
    g.ropec = din("ropec", [128, T])
    g.ropes = din("ropes", [128, T])
    g.maskP_d = din("maskP", [128, 128])
    g.maskN_d = din("maskN", [128, 128])
    if debug:
        g.ydbg = din("ydbg", [3, 128, 2, T], BF16)
    out = nc.dram_tensor("out", [4096, D], F32, kind="ExternalOutput").ap()

    g.modD = [nc.dram_tensor("modD%d" % l, [2, 6144], F32).ap() for l in range(L)]
    dk = dict(kind="ExternalOutput") if debug else {}
    g.xs = nc.dram_tensor("xs", [T, D], F32, **dk).ap()
    g.xmid = nc.dram_tensor("xmid", [T, D], F32, **dk).ap()
    g.yT = [nc.dram_tensor("yT%d" % i, [128, 2, T], BF16, **dk).ap() for i in range(4)]
    g.ofw2 = [nc.dram_tensor("ofw%d" % i, [T, 512], F32).ap() for i in range(2)]
    g.ofwB = [nc.dram_tensor("ofwB%d" % i, [T, 256], F32).ap() for i in range(2)]
    g.bqk = nc.dram_tensor("bqk", [4, 128, T], BF16).ap()
    g.aF = nc.dram_tensor("aF", [6, 128, T], F32).ap()
    g.aTM = nc.dram_tensor("aTM", [T, 1024], F32).ap()
    g.asm = nc.dram_tensor("asm", [128, T], F32).ap()
    g.asg = nc.dram_tensor("asg", [64, T], F32).ap()
    g.bkv = nc.dram_tensor("bkv", [T, 512], F32).ap()
    g.bgate = nc.dram_tensor("bgate", [T, 256], F32).ap()
    g.bba = nc.dram_tensor("bba", [T, 16], F32).ap()

    A = nc.alloc_sbuf_tensor
    g.ident = A("ident_s", [128, 128], F32)
    g.maskP = A("maskP_s", [128, 128], BF16)
    g.maskN = A("maskN_s", [128, 128], BF16)
    g.sc = A("sc", [128, 8, 2], F32)
    g.tri = A("tri_s", [64, 4, 64], F32)
    g.triD = A("triD_s", [128, 4, 64], F32)
    g.identD = A("identD_s", [128, 64], F32)
    g.ones128 = A("ones128", [128, 128], F32)
    g.ones1 = A("ones1", [1, 64], F32)
    g.ones64 = A("ones64", [64, 128], F32)
    g.blk = A("blk_s", [128, 128], F32)
    g.modT = A("modT", [128, 48, 2], F32)
    g.ngT = A("ngT_s", [128, L, 4, 8], F32)
    g.gbT = A("gbT_s", [128, L, 4, 8], F32)
    g.gain1 = A("gain1", [128, 8, 2], F32)
    g.gain2 = A("gain2", [128, 8, 2], F32)
    g.Gbc = A("Gbc", [128, 2, 1024], F32)
    g.xn = A("xn", [128, 1024], F32)
    g.ss = A("ss", [128, 1], F32)
    g.rstd = A("rstd", [128, 1], F32)
    g.rstd2 = A("rstd2", [128, 1], F32)

    P.dma("sync", g.ident[:], g.ident_d)
    P.dma("gpsimd", g.maskP[:], g.maskP_d)
    P.dma("gpsimd", g.maskN[:], g.maskN_d)
    P.dma("sync", g.sc[:], g.cvec)
    P.dma("sync", g.tri[:], g.tri_d)
    P.dma("sync", g.triD[0:64], g.tri_d)
    P.dma("sync", g.triD[64:128], g.tri_d)
    P.dma("sync", g.identD[0:64], g.ident_d[0:64, 0:64])
    P.dma("sync", g.identD[64:128], g.ident_d[0:64, 0:64])
    V(P, lambda e: e.memset(g.ones128[:], 1.0), [], [g.ones128])
    V(P, lambda e: e.memset(g.ones1[:], 1.0), [], [g.ones1])
    V(P, lambda e: e.memset(g.ones64[:], 1.0), [], [g.ones64])
    P.dma("sync", g.blk[:], g.blk_d)
    P.dma("sync", g.ngT[:], g.ngT_d)
    P.dma("sync", g.gbT[:], g.gbT_d)
    S(P, lambda e: e.activation(out=g.sc[:], in_=g.sc[:], func=AF.Silu), [g.sc], [g.sc])

    for l in range(1 if debug else L):
        last = l == L - 1
        with_ctx = not last
        xsrc = g.xin if l == 0 else g.xs
        phase0_mod(g, l)
        with Pool(nc) as pool:
            hT = pool.sb("hT", [128, 8, T], BF16)
            g.hT = hT
            phase1_h(g, l, xsrc)
            with Pool(nc) as pool:
                wmix = pool.sb("wmix", [128, 8, 1088], BF16)
                g.wmix = wmix
                mixer_D(g, l, with_ctx)
                mixer_C(g, l)
                mixer_B_pre(g, l)
                mixer_A_pre(g, l)
            mixers_AB_sweeps(g, l)
            mixer_B_finish(g, l)
            mixer_A_finish(g, l)
            load_gbc(g, l, 2, 1)
            phase3_merge(g, l, xsrc, with_ctx)
        P.emit()
        if last:
            def xdst(ti):
                if ti < 2:
                    return None, None
                return out[(ti - 2) * 128:(ti - 1) * 128, :], "out"
        else:
            def xdst(ti):
                return g.xs[ti * 128:(ti + 1) * 128, :], ("x", l + 1)
        load_gbc(g, l, 5, 3)
        phase4_ffn(g, l, xdst, with_ctx)
        P.emit()
    P.barrier()
    P.emit()
    P.close()
    return nc


def _consts():
    ident = np.eye(128, dtype=np.float32)
    kk = np.arange(128)[:, None]
    qq = np.arange(128)[None, :]
    maskP = (kk >= qq).astype(np.float32)
    maskN = (kk <= qq).astype(np.float32)
    t = np.arange(4096)
    rows = (t // 64).astype(np.float32)
    cols = (t % 64).astype(np.float32)
    inv = (10000.0 ** (-np.arange(16, dtype=np.float32) / 16)).astype(np.float32)
    cosT = np.ones((64, T), np.float32)
    sinT = np.zeros((64, T), np.float32)
    for d in range(64):
        pos = rows if d < 32 else cols
        f = inv[d % 16]
        ang = (pos * f).astype(np.float32)
        cosT[d, 256:] = np.cos(ang)
        sgn = -1.0 if (d % 32) < 16 else 1.0
        sinT[d, 256:] = sgn * np.sin(ang)
    ropec = np.concatenate([cosT, cosT], 0)
    ropes = np.concatenate([sinT, sinT], 0)
    jj = np.arange(64)[:, None]
    ii = np.arange(64)[None, :]
    tri = np.stack([(jj <= ii), (jj > ii), (jj >= ii), (jj < ii)], 1).astype(np.float32)
    blk = np.kron(np.eye(2, dtype=np.float32), np.ones((64, 64), np.float32))
    return dict(blk=blk, tri=np.ascontiguousarray(tri), ident=ident, maskP=maskP, maskN=maskN, ropec=np.ascontiguousarray(ropec), ropes=np.ascontiguousarray(ropes))


def _layout_weights(inp):
    w_in = inp["w_in"]
    o = {}
    d0 = 960 + 1040 + 800
    q = w_in[:, :, d0:d0 + 256].reshape(L, D, 4, 64)
    k = w_in[:, :, d0 + 256:d0 + 384].reshape(L, D, 2, 64)
    v = w_in[:, :, d0 + 384:d0 + 512]
    perm = np.array([(d + 16) if (d % 32) < 16 else (d - 16) for d in range(64)])
    qa = np.concatenate([q[:, :, 0], q[:, :, 2]], -1)
    qb = np.concatenate([q[:, :, 1], q[:, :, 3]], -1)
    kk = k.reshape(L, D, 128)
    qr = q[..., perm]
    kr = k[..., perm]
    qra = np.concatenate([qr[:, :, 0], qr[:, :, 2]], -1)
    qrb = np.concatenate([qr[:, :, 1], qr[:, :, 3]], -1)
    krr = kr.reshape(L, D, 128)
    c0 = 960 + 1040
    Z = np.zeros((L, D, 4, 32), np.float32)
    qc = w_in[:, :, c0:c0 + 128].reshape(L, D, 4, 32)
    kc_ = w_in[:, :, c0 + 128:c0 + 256].reshape(L, D, 4, 32)
    qP = np.concatenate([qc, Z], -1).reshape(L, D, 256)
    kP = np.concatenate([kc_, Z], -1).reshape(L, D, 256)
    o["w_C"] = np.ascontiguousarray(np.concatenate([qP, kP, w_in[:, :, c0 + 256:c0 + 512], w_in[:, :, c0 + 544:c0 + 800],
                                                    w_in[:, :, c0 + 512:c0 + 544]], -1))
    gw2 = inp["gla_gw2"].reshape(L, 2, 16, 4, 32)
    o["gw2P"] = np.ascontiguousarray(np.concatenate([gw2, np.zeros_like(gw2)], -1).reshape(L, 2, 16, 256))
    gbb = inp["gla_gb"].reshape(L, 2, 4, 32)
    o["gbP"] = np.ascontiguousarray(np.concatenate([gbb, np.zeros_like(gbb)], -1).reshape(L, 2, 256))
    o["convT"] = np.ascontiguousarray(inp["gdn_conv"].reshape(L, 7, 6, 128).transpose(3, 0, 2, 1))
    o["ngB"] = np.ascontiguousarray(np.tile(inp["gdn_norm_g"], (1, 4)))
    o["gdn_dt_bias"] = inp["gdn_dt_bias"]
    o["gdn_a_log"] = inp["gdn_a_log"]
    mu = np.zeros((L, 1024), np.float32)
    mu[:, :960] = inp["rwkv_mu"]
    o["muT"] = np.ascontiguousarray(mu.reshape(L, 8, 128).transpose(2, 0, 1))
    o["kkwT"] = np.ascontiguousarray(inp["rwkv_kk"].reshape(L, 2, 128).transpose(2, 0, 1))
    o["kaT"] = np.ascontiguousarray(inp["rwkv_ka"].reshape(L, 2, 128).transpose(2, 0, 1))
    o["a0T"] = np.ascontiguousarray(inp["rwkv_a0"].reshape(L, 2, 2, 128).transpose(3, 0, 1, 2))
    for nm in ("rwkv_w2", "rwkv_a2", "rwkv_w0", "rwkv_a0", "rwkv_g2", "rwkv_ka", "rwkv_ln_g", "rwkv_ln_b"):
        o[nm] = inp[nm]
    o["rwkv_rk"] = np.ascontiguousarray(inp["rwkv_rk"].reshape(L, 256))
    o["ngC"] = np.ascontiguousarray(np.tile(inp["gla_norm_g"], (1, 4)))
    o["w_D"] = np.ascontiguousarray(np.concatenate([qa, qb, kk, qra, qrb, krr, v], -1))
    return o


_NC_CACHE = {}


def kernel(**inp):
    inp = {k: np.asarray(v) for k, v in inp.items()}
    debug = bool(int(os.environ.get("KDEBUG", "0")))
    if debug not in _NC_CACHE:
        _NC_CACHE[debug] = build_program(debug)
    nc = _NC_CACHE[debug]
    cst = _consts()
    lw = _layout_weights(inp)
    ngT = np.ascontiguousarray(inp["norm_g"].reshape(L, 4, 8, 128).transpose(3, 0, 1, 2))
    gbT = np.ascontiguousarray(inp["gate_b"].reshape(L, 4, 8, 128).transpose(3, 0, 1, 2))
    shared = dict(ada_w=inp["ada_w"], ada_b=inp["ada_b"], norm_g=inp["norm_g"], ngT=ngT, gbT=gbT, w_in=inp["w_in"],
                  w_branch=inp["w_branch"], w_out=inp["w_out"], ffn_w1=inp["ffn_w1"], ffn_w2=inp["ffn_w2"],
                  attn_sink=inp["attn_sink"], **cst, **lw)
    in_maps = []
    for b in range(8):
        m = dict(shared)
        m["xin"] = np.ascontiguousarray(np.concatenate([inp["ctx"][b], inp["x"][b]], 0))
        cv = np.stack([inp["c"][b], inp["c_ctx"]], -1)
        m["cvec"] = np.ascontiguousarray(cv.reshape(8, 128, 2).transpose(1, 0, 2))
        if debug:
            m["ydbg"] = inp["_ydbg"][b]
        in_maps.append(m)
    ncores = int(os.environ.get("KCORES", "8"))
    res = run_bass_kernel_spmd(nc, in_maps[:ncores], core_ids=list(range(ncores)))
    if debug:
        return res.results[0]
    outs = [res.results[b]["out"] for b in range(ncores)]
    while len(outs) < 8:
        outs.append(np.zeros_like(outs[0]))
    return np.stack(outs, 0).astype(np.float32)
```
